# Optimizing a Trainium2 kernel written in Bass

```python
import math
import jax, jax.numpy as jnp
from jax import lax
import numpy as np

D_MODEL = 1024
BATCH = 8
SEQ = 2048
DEPTH = 1
DEC_BATCH = 8
DEC_SEQ = 64
PAST_LEN = 2048

CHUNK = 64
N_META = 16
MIX_W = D_MODEL
ATT_W = MIX_W // 2
HEAD_DIM = 64
N_HEADS = ATT_W // HEAD_DIM
KV_HEADS = 2
GQA_GROUP = N_HEADS // KV_HEADS
WINDOW = 128
N_BACK = WINDOW // CHUNK
BAND = (N_BACK + 1) * CHUNK
N_BUCKETS = 32
MAX_DISTANCE = 128
RET_W = MIX_W - ATT_W
RET_HEADS = 4
RET_DK = RET_W // RET_HEADS
RET_DV = RET_W // RET_HEADS
ROPE_BASE = 10000.0
PROJ_SPLITS = (ATT_W, KV_HEADS * HEAD_DIM, KV_HEADS * HEAD_DIM, RET_W, RET_W, RET_W, RET_W)
PROJ_W = sum(PROJ_SPLITS)
N_KEYS = 128
N_EXPERTS = N_KEYS * N_KEYS
PEER_HEADS = 8
PEER_TOPK = 16
PEER_DK = 128
PEER_DK_HALF = PEER_DK // 2
PEER_BLOCK = 256
ALPHA = (2.0 * DEPTH) ** 0.25
BETA = (8.0 * DEPTH) ** -0.25
LN_EPS = 1e-5
NEG_INF = -1e30

kernel_name = 'hymba_swa_retention_peer_stream_step'


def layer_norm(x, g, b):
    xf = x.astype(jnp.float32)
    mu = xf.mean(-1, keepdims=True)
    var = jnp.square(xf - mu).mean(-1, keepdims=True)
    y = (xf - mu) * lax.rsqrt(var + LN_EPS) * g.astype(jnp.float32) + b.astype(jnp.float32)
    return y.astype(x.dtype)


def t5_bucket(rel):
    nb = N_BUCKETS // 2
    max_exact = nb // 2
    n = jnp.abs(rel)
    large = max_exact + (jnp.log(jnp.maximum(n, max_exact).astype(jnp.float32) / max_exact)
                         / math.log(MAX_DISTANCE / max_exact) * (nb - max_exact)).astype(jnp.int32)
    large = jnp.minimum(large, nb - 1)
    return jnp.where(rel > 0, nb, 0) + jnp.where(n < max_exact, n, large)


def rel_bias_lookup(rel_bias, rel_np):
    return rel_bias[t5_bucket(jnp.asarray(rel_np, jnp.int32))]


def rotary(x, pos):
    half = x.shape[-1] // 2
    inv = ROPE_BASE ** (-jnp.arange(half, dtype=jnp.float32) / half)
    ang = jnp.asarray(pos, jnp.float32)[:, None] * inv[None, :]
    cos = jnp.cos(ang)[None, :, None, :]
    sin = jnp.sin(ang)[None, :, None, :]
    x1, x2 = x[..., :half], x[..., half:]
    return jnp.concatenate([x1 * cos - x2 * sin, x1 * sin + x2 * cos], axis=-1)


def project(h, w, pos):
    p = jnp.einsum('bsd,de->bse', h, w)
    q, k, v, rq, rk, rv, rg = jnp.split(p, np.cumsum(PROJ_SPLITS)[:-1].tolist(), axis=-1)
    B, L = h.shape[:2]
    q = q.reshape(B, L, KV_HEADS, GQA_GROUP, HEAD_DIM)
    k = k.reshape(B, L, KV_HEADS, HEAD_DIM)
    v = v.reshape(B, L, KV_HEADS, HEAD_DIM)
    rq = rotary(rq.reshape(B, L, RET_HEADS, RET_DK).astype(jnp.float32), pos)
    rk = rotary(rk.reshape(B, L, RET_HEADS, RET_DK).astype(jnp.float32), pos) * (RET_DK ** -0.5)
    rv = rv.reshape(B, L, RET_HEADS, RET_DV).astype(jnp.float32)
    return q, k, v, rq, rk, rv, rg


def sink_attention(q, k, v, bias, mask, sinks):
    s = jnp.einsum('bnqhgd,bnkhd->bnhgqk', q, k, preferred_element_type=jnp.float32) * (HEAD_DIM ** -0.5)
    nb, nq, nk = bias.shape[:3]
    s = s + bias.astype(jnp.float32).reshape(nb, nq, nk, KV_HEADS, GQA_GROUP).transpose(0, 3, 4, 1, 2)[None]
    s = jnp.where(mask[None, :, None, None], s, NEG_INF)
    sink = sinks.astype(jnp.float32).reshape(KV_HEADS, GQA_GROUP)[None, None, :, :, None, None]
    m = jnp.maximum(s.max(-1, keepdims=True), sink)
    p = jnp.exp(s - m)
    denom = p.sum(-1, keepdims=True) + jnp.exp(sink - m)
    return jnp.einsum('bnhgqk,bnkhd->bnqhgd', p / denom, v.astype(jnp.float32))


def prompt_window_attention(q, k, v, mk, mv, rel_bias, sinks):
    B, S = q.shape[:2]
    nC = S // CHUNK
    qb = q.reshape(B, nC, CHUNK, KV_HEADS, GQA_GROUP, HEAD_DIM)
    pad = ((0, 0), (N_BACK * CHUNK, 0), (0, 0), (0, 0))
    kp = jnp.pad(k, pad).reshape(B, nC + N_BACK, CHUNK, KV_HEADS, HEAD_DIM)
    vp = jnp.pad(v, pad).reshape(B, nC + N_BACK, CHUNK, KV_HEADS, HEAD_DIM)
    meta_shape = (B, nC, N_META, KV_HEADS, HEAD_DIM)
    kb = jnp.concatenate([jnp.broadcast_to(mk[:, None], meta_shape)]
                         + [kp[:, j:j + nC] for j in range(N_BACK + 1)], axis=2)
    vb = jnp.concatenate([jnp.broadcast_to(mv[:, None], meta_shape)]
                         + [vp[:, j:j + nC] for j in range(N_BACK + 1)], axis=2)
    i = np.arange(CHUNK)
    jb = np.arange(BAND)
    c = np.arange(nC)
    m = np.arange(N_META)
    rel_meta = m[None, None, :] - (N_META + c[:, None, None] * CHUNK + i[None, :, None])
    rel_band = np.broadcast_to(jb[None, None, :] - N_BACK * CHUNK - i[None, :, None], (nC, CHUNK, BAND))
    rel = np.concatenate([rel_meta, rel_band], axis=-1)
    band_ok = np.broadcast_to((c[:, None, None] - N_BACK) * CHUNK + jb[None, None, :] >= 0, (nC, CHUNK, BAND))
    mask = np.concatenate([np.ones((nC, CHUNK, N_META), dtype=bool), band_ok], axis=-1)
    o = sink_attention(qb, kb, vb, rel_bias_lookup(rel_bias, rel), jnp.asarray(mask), sinks)
    return o.reshape(B, S, ATT_W)


def retention_chunk(state, q, k, v, log_gamma):
    L = q.shape[1]
    idx = jnp.arange(L, dtype=jnp.float32)
    diff = idx[:, None] - idx[None, :]
    decay = jnp.where(diff >= 0, jnp.exp(jnp.maximum(diff, 0.0)[None] * log_gamma[:, None, None]), 0.0)
    s = jnp.einsum('bqhd,bkhd->bhqk', q, k) * decay[None]
    inner = jnp.einsum('bhqk,bkhe->bqhe', s, v)
    q_dec = jnp.exp((idx + 1.0)[:, None] * log_gamma[None, :])
    cross = jnp.einsum('bqhd,bhde->bqhe', q, state) * q_dec[None, :, :, None]
    k_dec = jnp.exp((L - 1.0 - idx)[:, None] * log_gamma[None, :])
    new_state = (jnp.exp(L * log_gamma)[None, :, None, None] * state
                 + jnp.einsum('bkhd,bkhe->bhde', k * k_dec[None, :, :, None], v))
    return inner + cross, new_state


def prompt_retention(rq_m, rk_m, rv_m, rq, rk, rv, log_gamma):
    B, S = rq.shape[:2]
    nC = S // CHUNK
    st0 = jnp.zeros((B, RET_HEADS, RET_DK, RET_DV), jnp.float32)
    o_m, st = retention_chunk(st0, rq_m, rk_m, rv_m, log_gamma)

    def to_chunks(t):
        return t.reshape(B, nC, CHUNK, RET_HEADS, t.shape[-1]).swapaxes(0, 1)

    def step(s, qkv):
        o, s = retention_chunk(s, qkv[0], qkv[1], qkv[2], log_gamma)
        return s, o

    st, o = lax.scan(step, st, (to_chunks(rq), to_chunks(rk), to_chunks(rv)))
    o = o.swapaxes(0, 1).reshape(B, S, RET_HEADS, RET_DV)
    return o_m, o, st


def mix_out(o_att, o_ret, rg, gn_g, w_out):
    mu = o_ret.mean(-1, keepdims=True)
    var = jnp.square(o_ret - mu).mean(-1, keepdims=True)
    y = ((o_ret - mu) * lax.rsqrt(var + LN_EPS)).reshape(o_ret.shape[0], o_ret.shape[1], RET_W)
    y = y * gn_g.astype(jnp.float32) * jax.nn.silu(rg.astype(jnp.float32))
    mixed = jnp.concatenate([o_att, y], axis=-1).astype(rg.dtype)
    return jnp.einsum('bse,ed->bsd', mixed, w_out)


def peer_ffn(x, wq, sub_keys, u_tab, v_tab):
    lead = x.shape[:-1]
    xt = x.reshape(-1, D_MODEL)
    T = xt.shape[0]
    pad = (-T) % PEER_BLOCK
    xt = jnp.pad(xt, ((0, pad), (0, 0))).reshape(-1, PEER_BLOCK, D_MODEL)

    def block(xb):
        q = jnp.einsum('td,de->te', xb, wq).reshape(PEER_BLOCK, PEER_HEADS, 2, PEER_DK_HALF)
        s = jnp.einsum('tpcd,pcnd->tpcn', q, sub_keys, preferred_element_type=jnp.float32)
        s1, i1 = lax.top_k(s[:, :, 0], PEER_TOPK)
        s2, i2 = lax.top_k(s[:, :, 1], PEER_TOPK)
        cand_s = (s1[..., :, None] + s2[..., None, :]).reshape(PEER_BLOCK, PEER_HEADS, PEER_TOPK * PEER_TOPK)
        cand_i = (i1[..., :, None] * N_KEYS + i2[..., None, :]).reshape(PEER_BLOCK, PEER_HEADS, PEER_TOPK * PEER_TOPK)
        top_s, sel = lax.top_k(cand_s, PEER_TOPK)
        eidx = jnp.take_along_axis(cand_i, sel, axis=-1)
        g = jax.nn.softmax(top_s, axis=-1)
        act = jax.nn.gelu(jnp.einsum('tpkd,td->tpk', u_tab[eidx], xb, preferred_element_type=jnp.float32))
        w = (g * act).astype(xb.dtype)
        return jnp.einsum('tpk,tpkd->td', w, v_tab[eidx])

    out = lax.map(block, xt).reshape(-1, D_MODEL)[:T]
    return out.reshape(*lead, D_MODEL)


def post_layer(h, mix, ln1_g, ln1_b, wq, sub_keys, u_tab, v_tab, ln2_g, ln2_b):
    h = layer_norm(ALPHA * h + mix, ln1_g, ln1_b)
    return layer_norm(ALPHA * h + peer_ffn(h, wq, sub_keys, u_tab, v_tab), ln2_g, ln2_b)


def setup_inputs(seed: int = 0) -> dict:
    key = jax.random.key(seed)
    ks = jax.random.split(key, 32)
    f32 = jnp.float32
    nrm = lambda k, shape, scale=1.0: scale * jax.random.normal(k, shape, f32)
    swa_cache = min(WINDOW, PAST_LEN)
    return {
        'x_prompt': nrm(ks[0], (BATCH, SEQ, D_MODEL)),
        'x_sample': nrm(ks[1], (DEC_BATCH, DEC_SEQ, D_MODEL)),
        'cache_meta_k': nrm(ks[2], (DEPTH, DEC_BATCH, N_META, KV_HEADS, HEAD_DIM)),
        'cache_meta_v': nrm(ks[3], (DEPTH, DEC_BATCH, N_META, KV_HEADS, HEAD_DIM)),
        'cache_swa_k': nrm(ks[4], (DEPTH, DEC_BATCH, swa_cache, KV_HEADS, HEAD_DIM)),
        'cache_swa_v': nrm(ks[5], (DEPTH, DEC_BATCH, swa_cache, KV_HEADS, HEAD_DIM)),
        'state_ret': nrm(ks[6], (DEPTH, DEC_BATCH, RET_HEADS, RET_DK, RET_DV), 0.1),
        'meta_tokens': nrm(ks[7], (N_META, D_MODEL)),
        'ln_in_g': 1.0 + nrm(ks[8], (D_MODEL,), 0.02),
        'ln_in_b': nrm(ks[9], (D_MODEL,), 0.02),
        'rel_bias': nrm(ks[10], (N_BUCKETS, N_HEADS), 0.5),
        'w_in': nrm(ks[11], (DEPTH, D_MODEL, PROJ_W), D_MODEL ** -0.5),
        'w_out': nrm(ks[12], (DEPTH, MIX_W, D_MODEL), BETA * MIX_W ** -0.5),
        'attn_sinks': nrm(ks[13], (DEPTH, N_HEADS)),
        'ret_gn_g': 1.0 + nrm(ks[14], (DEPTH, RET_W), 0.02),
        'ln1_g': 1.0 + nrm(ks[15], (DEPTH, D_MODEL), 0.02),
        'ln1_b': nrm(ks[16], (DEPTH, D_MODEL), 0.02),
        'peer_wq': nrm(ks[17], (DEPTH, D_MODEL, PEER_HEADS * PEER_DK), D_MODEL ** -0.5),
        'peer_subkeys': nrm(ks[18], (DEPTH, PEER_HEADS, 2, N_KEYS, PEER_DK_HALF), PEER_DK_HALF ** -0.5),
        'peer_u': nrm(ks[19], (DEPTH, N_EXPERTS, D_MODEL), D_MODEL ** -0.5),
        'peer_v': nrm(ks[20], (DEPTH, N_EXPERTS, D_MODEL), BETA * PEER_HEADS ** -0.5),
        'ln2_g': 1.0 + nrm(ks[21], (DEPTH, D_MODEL), 0.02),
        'ln2_b': nrm(ks[22], (DEPTH, D_MODEL), 0.02),
    }


def reference(x_prompt, x_sample, cache_meta_k, cache_meta_v, cache_swa_k, cache_swa_v, state_ret,
              meta_tokens, ln_in_g, ln_in_b, rel_bias, w_in, w_out, attn_sinks, ret_gn_g, ln1_g, ln1_b,
              peer_wq, peer_subkeys, peer_u, peer_v, ln2_g, ln2_b):
    log_gamma = jnp.log(1.0 - 2.0 ** (-5.0 - jnp.arange(RET_HEADS, dtype=jnp.float32)))
    B, S, _ = x_prompt.shape
    DB, DS, _ = x_sample.shape
    pos_meta = np.arange(N_META)
    pos_p = N_META + np.arange(S)
    pos_s = N_META + PAST_LEN + np.arange(DS)

    h_meta = layer_norm(jnp.broadcast_to(meta_tokens.astype(x_prompt.dtype), (B, N_META, D_MODEL)), ln_in_g, ln_in_b)
    h = layer_norm(x_prompt, ln_in_g, ln_in_b)
    meta_k_l, meta_v_l, swa_k_l, swa_v_l, ret_l = [], [], [], [], []
    for l in range(DEPTH):
        ffn_p = (ln1_g[l], ln1_b[l], peer_wq[l], peer_subkeys[l], peer_u[l], peer_v[l], ln2_g[l], ln2_b[l])
        qm, km, vm, rqm, rkm, rvm, rgm = project(h_meta, w_in[l], pos_meta)
        q, k, v, rq, rk, rv, rg = project(h, w_in[l], pos_p)
        o_att = prompt_window_attention(q, k, v, km, vm, rel_bias, attn_sinks[l])
        o_ret_m, o_ret, st = prompt_retention(rqm, rkm, rvm, rq, rk, rv, log_gamma)
        h_next = post_layer(h, mix_out(o_att, o_ret, rg, ret_gn_g[l], w_out[l]), *ffn_p)
        if l < DEPTH - 1:
            bias_m = rel_bias_lookup(rel_bias, (pos_meta[None, :] - pos_meta[:, None])[None])
            o_att_m = sink_attention(qm[:, None], km[:, None], vm[:, None], bias_m,
                                     jnp.ones((1, N_META, N_META), dtype=bool), attn_sinks[l]).reshape(B, N_META, ATT_W)
            h_meta = post_layer(h_meta, mix_out(o_att_m, o_ret_m, rgm, ret_gn_g[l], w_out[l]), *ffn_p)
        meta_k_l.append(km)
        meta_v_l.append(vm)
        swa_k_l.append(k[:, S - WINDOW:])
        swa_v_l.append(v[:, S - WINDOW:])
        ret_l.append(st)
        h = h_next
    y_prompt = h

    hs = layer_norm(x_sample, ln_in_g, ln_in_b)
    ks_l, vs_l, rs_l = [], [], []
    W = cache_swa_k.shape[2]
    i = np.arange(DS)
    rel_s = np.concatenate([pos_meta[None, :] - pos_s[:, None],
                            np.arange(W)[None, :] - W - i[:, None],
                            i[None, :] - i[:, None]], axis=-1)[None]
    bias_s = rel_bias_lookup(rel_bias, rel_s)
    mask_s = jnp.ones(rel_s.shape, dtype=bool)
    for l in range(DEPTH):
        ffn_p = (ln1_g[l], ln1_b[l], peer_wq[l], peer_subkeys[l], peer_u[l], peer_v[l], ln2_g[l], ln2_b[l])
        q, k, v, rq, rk, rv, rg = project(hs, w_in[l], pos_s)
        kc = jnp.concatenate([cache_meta_k[l].astype(k.dtype), cache_swa_k[l].astype(k.dtype), k], axis=1)[:, None]
        vc = jnp.concatenate([cache_meta_v[l].astype(v.dtype), cache_swa_v[l].astype(v.dtype), v], axis=1)[:, None]
        o_att = sink_attention(q[:, None], kc, vc, bias_s, mask_s, attn_sinks[l]).reshape(DB, DS, ATT_W)
        o_ret, st_new = retention_chunk(state_ret[l].astype(jnp.float32), rq, rk, rv, log_gamma)
        hs = post_layer(hs, mix_out(o_att, o_ret, rg, ret_gn_g[l], w_out[l]), *ffn_p)
        ks_l.append(k)
        vs_l.append(v)
        rs_l.append(st_new)
    y_sample = hs

    return (y_prompt, y_sample,
            jnp.stack(meta_k_l), jnp.stack(meta_v_l), jnp.stack(swa_k_l), jnp.stack(swa_v_l), jnp.stack(ret_l),
            jnp.stack(ks_l), jnp.stack(vs_l), jnp.stack(rs_l))
```

```python
import contextlib
import math
import os
import numpy as np
import concourse.bass as bass
import concourse.mybir as mybir
from concourse.bass_utils import run_bass_kernel_spmd

F32 = mybir.dt.float32
BF16 = mybir.dt.bfloat16
I32 = mybir.dt.int32
U32 = mybir.dt.uint32
ALU = mybir.AluOpType
AF = mybir.ActivationFunctionType
AX = mybir.AxisListType

D = 1024
S = 2048
DS = 64
NMETA = 16
NTILES = S // 128
PW = 2816
NE = 16384
ALPHA = 2.0 ** 0.25
EPS = 1e-5
NCORES = 8
NJ = 128


class T:
    def __init__(self, t, name):
        self.t = t
        self.name = name
        self.w = None
        self.r = {}
        self.dsem = None
        self.dcnt = 0
        self.psem = None
        self.pcnt = 0

    def __getitem__(self, k):
        return self.t[k]


class Prog:
    ENG = ('pe', 'act', 'dve', 'pool', 'sp')

    def __init__(self, nc):
        self.nc = nc
        self.es = contextlib.ExitStack()
        self.q = {e: [] for e in self.ENG}
        self.cnt = {e: 0 for e in self.ENG}
        self.seen = {e: {} for e in self.ENG}
        self.esem = {}
        for e in self.ENG:
            self.esem[e] = self.es.enter_context(nc.semaphore('es_' + e))
        self.nsem = 0
        self.stores = []

    def sb(self, name, shape, dt=F32):
        return T(self.es.enter_context(self.nc.sbuf_tensor('sb_' + name, list(shape), dt)), name)

    def ps(self, name, shape, dt=F32):
        return T(self.es.enter_context(self.nc.psum_tensor('ps_' + name, list(shape), dt)), name)

    def newsem(self, name):
        self.nsem += 1
        return self.es.enter_context(self.nc.semaphore('d_%s_%d' % (name, self.nsem)))

    def _waits(self, eng, reads, writes):
        deps = {}

        def add(d):
            if d is None:
                return
            sem, val, src = d
            if eng == 'pe' and src == 'pe':
                return
            k = id(sem)
            if k not in deps or deps[k][1] < val:
                deps[k] = (sem, val)

        for t in reads:
            add(t.w)
        for t in writes:
            add(t.w)
            for d in t.r.values():
                add(d)
        out = []
        for k, (sem, val) in deps.items():
            if self.seen[eng].get(k, 0) >= val:
                continue
            self.seen[eng][k] = val
            out.append((sem, val))
        return out

    def op(self, eng, fn, reads=(), writes=()):
        waits = self._waits(eng, reads, writes)
        self.cnt[eng] += 1
        me = (self.esem[eng], self.cnt[eng], eng)
        self.q[eng].append((waits, fn, (self.esem[eng], 1)))
        for t in reads:
            t.r[id(me[0])] = me
        for t in writes:
            t.w = me
            t.r = {}

    def dma(self, eng, fn, reads=(), writes=(), store_src=None):
        waits = self._waits(eng, reads, writes)
        owner = writes[0] if writes else store_src
        if eng == 'pool':
            if owner.psem is None:
                owner.psem = self.newsem('p' + owner.name)
            owner.pcnt += 16
            me = (owner.psem, owner.pcnt, 'dma')
        else:
            if owner.dsem is None:
                owner.dsem = self.newsem(owner.name)
            owner.dcnt += 16
            me = (owner.dsem, owner.dcnt, 'dma')
        self.q[eng].append((waits, fn, (me[0], 16)))
        for t in reads:
            t.r[id(me[0])] = me
        for t in writes:
            t.w = me
            t.r = {}
        if store_src is not None:
            self.stores.append(me)

    def finish(self):
        deps = {}
        for sem, val, _ in self.stores:
            k = id(sem)
            if k not in deps or deps[k][1] < val:
                deps[k] = (sem, val)
        for e in self.ENG:
            if e != 'sp' and self.cnt[e] > 0:
                deps[id(self.esem[e])] = (self.esem[e], self.cnt[e])
        self.q['sp'].append((list(deps.values()), None, None))

    def emit(self):
        q = self.q
        import bisect
        eng_of = {id(self.esem[e]): e for e in self.ENG}
        targets = {e: set() for e in self.ENG}
        for name in self.ENG:
            for waits, fn, inc in q[name]:
                for sem, val in waits:
                    if id(sem) in eng_of:
                        targets[eng_of[id(sem)]].add(val)
        tsorted = {e: sorted(targets[e]) for e in self.ENG}

        def real(sem, val):
            e_ = eng_of.get(id(sem))
            if e_ is None:
                return val
            return bisect.bisect_right(tsorted[e_], val)

        def replay(name, e):
            vidx = 0
            for waits, fn, inc in q[name]:
                for sem, val in waits:
                    e.wait_ge(sem, real(sem, val))
                if fn is None:
                    continue
                ins = fn(e)
                if inc is not None:
                    if id(inc[0]) in eng_of:
                        vidx += 1
                        if vidx in targets[name]:
                            ins.then_inc(inc[0], 1)
                    else:
                        ins.then_inc(inc[0], inc[1])

        with self.nc.Block() as block:
            @block.sync
            def _(e):
                replay('sp', e)

            @block.scalar
            def _(e):
                replay('act', e)

            @block.vector
            def _(e):
                replay('dve', e)

            @block.gpsimd
            def _(e):
                replay('pool', e)

            @block.tensor
            def _(e):
                replay('pe', e)
        self.es.close()


def _t5_bucket(rel):
    nb = 16
    max_exact = 8
    n = np.abs(rel)
    v = (np.log(np.maximum(n, max_exact).astype(np.float32) / np.float32(max_exact))
         / np.float32(math.log(128 / 8)) * np.float32(nb - max_exact))
    large = np.minimum(max_exact + v.astype(np.int32), nb - 1)
    return np.where(rel > 0, nb, 0) + np.where(n < max_exact, n, large)


def _host_consts():
    c = {}
    k = np.arange(128)[:, None]
    q = np.arange(128)[None, :]
    c['bk_prev'] = _t5_bucket(k - 128 - q)
    c['bk_cur'] = _t5_bucket(k - q)
    m = np.arange(NMETA)[:, None]
    c['bk_meta0'] = _t5_bucket(m - NMETA - q)
    ok_prev = (q < 64) | (k >= 64)
    ok_cur = (q >= 64) | (k < 64)
    neg = np.float32(-30000.0)
    c['mask_prev'] = np.where(ok_prev, 0.0, neg).astype(np.float32)
    c['mask_cur'] = np.where(ok_cur, 0.0, neg).astype(np.float32)
    half = 64
    inv = (np.float32(10000.0) ** (-np.arange(half, dtype=np.float32) / np.float32(half))).astype(np.float32)
    rot = np.zeros((18, 128, 128), np.float32)
    for ti in range(18):
        if ti < 16:
            pos = NMETA + ti * 128 + np.arange(128)
        elif ti == 16:
            pos = NMETA + S + np.arange(128)
        else:
            pos = np.arange(128)
        ang = pos.astype(np.float32)[:, None] * inv[None, :]
        rot[ti, :, :64] = np.cos(ang)
        rot[ti, :, 64:] = np.sin(ang)
    c['rot'] = rot
    gam = 1.0 - 2.0 ** (-5.0 - np.arange(4, dtype=np.float64))
    t = np.arange(128, dtype=np.float64)[:, None]
    dq = gam[None, :] ** (t + 1.0)
    dk = gam[None, :] ** (-(t + 1.0)) * (128.0 ** -0.5)
    c['dqk'] = np.concatenate([dq, dk], axis=1).astype(np.float32)
    gl = np.zeros((128, 3, 4), np.float32)
    for li, L in enumerate((128, 64, 16)):
        gl[:, li, :] = (gam ** L)[None, :]
    c['gl'] = gl.reshape(128, 12)
    c['rmask'] = (k <= q).astype(np.float32)
    c['iota16'] = np.tile(np.arange(16, dtype=np.float32)[None, :], (128, 1))
    return c


def build_program(dbg=False):
    nc = bass.Bass("TRN2", target_bir_lowering=False)

    def din(name, shape, dt=F32):
        return nc.dram_tensor(name, list(shape), dt, kind="ExternalInput").ap()

    def dout(name, shape, dt=F32):
        return nc.dram_tensor(name, list(shape), dt, kind="ExternalOutput").ap()

    xp_d = din("xp", [S, D])
    xs_d = din("xs", [DS, D])
    xm_d = din("xm", [NMETA, D])
    cmk_d = din("cmk", [NMETA, 128])
    cmv_d = din("cmv", [NMETA, 128])
    csk_d = din("csk", [128, 128])
    csv_d = din("csv", [128, 128])
    st0_d = din("st0", [4, 128, 128])
    lng_d = [din("ln%d_g" % i, [D]) for i in range(3)]
    lnb_d = [din("ln%d_b" % i, [D]) for i in range(3)]
    win_d = din("w_in", [D, PW])
    wout_d = din("w_out", [D, D])
    wq_d = din("wq", [D, D])
    skb_d = din("skb", [8, 128, 256])
    sinks_d = din("sinks", [8])
    gng_d = din("gng", [512])
    u_d = din("peer_u", [NE, D])
    v_d = din("peer_v", [NE, D])
    bprev_d = din("b_prev", [128, 8, 128])
    bcur_d = din("b_cur", [128, 8, 128])
    bm0_d = din("b_meta0", [NMETA, 8, 128])
    bmf_d = din("b_metaf", [NMETA, 8])
    mprev_d = din("mask_prev", [128, 128])
    mcur_d = din("mask_cur", [128, 128])
    rot_d = din("rot", [18, 128, 128])
    dqk_d = din("dqk", [128, 8])
    gl_d = din("gl", [128, 12])
    rmask_d = din("rmask", [128, 128])
    iota16_d = din("iota16", [128, 16])

    yp_d = dout("y_prompt", [S, D])
    ys_d = dout("y_sample", [DS, D])
    mk_d = dout("meta_k", [NMETA, 128])
    mv_d = dout("meta_v", [NMETA, 128])
    sk_d = dout("swa_k", [128, 128])
    sv_d = dout("swa_v", [128, 128])
    rs_d = dout("ret_state", [4, 128, 128])
    sks_d = dout("swa_k_s", [DS, 128])
    svs_d = dout("swa_v_s", [DS, 128])
    rss_d = dout("ret_state_s", [4, 128, 128])
    uvbf_d = nc.dram_tensor("uvbf_scratch", [NE, 2 * D], BF16, kind="Internal").ap()
    dbg_d = {}
    if dbg:
        for nm, w in (("d_h", 1024), ("d_p", 2816), ("d_mixed", 1024), ("d_h1", 1024), ("d_eidx", 128),
                      ("d_g", 128), ("d_dots", 128), ("d_peer", 1024)):
            dbg_d[nm] = dout(nm, [128, w])

    P = Prog(nc)
    op, dma = P.op, P.dma

    w_in = P.sb("w_in", [128, 8, PW], BF16)
    w_out = P.sb("w_out", [128, 8, D], BF16)
    w_q = P.sb("w_q", [128, 8, D], BF16)
    skb = P.sb("skb", [128, 8, 256], BF16)
    lng = [P.sb("lng%d" % i, [128, D], BF16) for i in range(3)]
    lnb = [P.sb("lnb%d" % i, [128, D], BF16) for i in range(3)]
    gng = P.sb("gng", [128, 512])
    esink = P.sb("esink", [128, 8])
    b_prev = P.sb("b_prev", [128, 8, 128], BF16)
    b_cur = P.sb("b_cur", [128, 8, 128], BF16)
    b_mf = P.sb("b_mf", [NMETA, 8])
    rot = P.sb("rot", [128, 128])
    dqk = P.sb("dqk", [128, 8])
    gl = P.sb("gl", [128, 12])
    rmask = P.sb("rmask", [128, 128])
    iota16 = P.sb("iota16", [128, 16])
    ident = P.sb("ident", [128, 128], BF16)

    F4a = P.sb("F4a", [128, D])
    H1 = [P.sb("H1_%d" % i, [128, D]) for i in range(2)]
    F4c = P.sb("F4c", [128, D])
    B2a = P.sb("B2a", [128, D], BF16)
    B2b = P.sb("B2b", [128, D], BF16)
    T8a = P.sb("T8a", [128, 8, 128], BF16)
    T8b = P.sb("T8b", [128, 8, 128], BF16)
    lst = P.sb("lst", [128, 2, 6])
    lmv = P.sb("lmv", [128, 2])
    lrs = P.sb("lrs", [128, 1])
    kvf = P.sb("kvf", [128, 256])
    qbf = P.sb("qbf", [128, 512], BF16)
    kbf = P.sb("kbf", [128, 128], BF16)
    KT = [P.sb("KT%d" % i, [128, 128], BF16) for i in range(2)]
    VX = [P.sb("VX%d" % i, [128, 2, 65], BF16) for i in range(2)]
    KTm = P.sb("KTm", [128, NMETA], BF16)
    VXm = P.sb("VXm", [NMETA, 2, 65], BF16)
    QT = P.sb("QT", [128, 4, 128], BF16)
    sS = P.sb("sS", [128, 512])
    PT1 = [P.sb("PT%d" % a, [128, 512], BF16) for a in range(3)]
    PT = [[PT1[a], PT1[a]] for a in range(3)]
    den = P.sb("den", [128, 8])
    tA = P.sb("tA", [128, 512])
    tB = P.sb("tB", [128, 512])
    rqkb = P.sb("rqkb", [128, D], BF16)
    rvb = P.sb("rvb", [128, 512], BF16)
    gate = P.sb("gate", [128, 512], BF16)
    SmT = P.sb("SmT", [128, 4, 128], BF16)
    state = P.sb("state", [128, 512])
    stbf = P.sb("stbf", [128, 512], BF16)
    tmpS = tA
    gst = P.sb("gst", [128, 4, 6])
    gmv = P.sb("gmv", [128, 4, 2])
    grs = P.sb("grs", [128, 4])
    yA = tB
    cand = F4a
    b_m0 = H1[1]
    sw = [P.sb("sw%d" % i, [128, 256]) for i in range(2)]
    tv = P.sb("tv", [128, 16, 16])
    tiu = P.sb("tiu", [128, 16, 16], U32)
    tif = P.sb("tif", [128, 16, 16])
    t2v = P.sb("t2v", [128, 8, 16])
    t2i = P.sb("t2i", [128, 8, 16], U32)
    k1u = P.sb("k1u", [128, 128], U32)
    k2u = P.sb("k2u", [128, 128], U32)
    k1f = P.sb("k1f", [128, 128])
    k2f = P.sb("k2f", [128, 128])
    i1s = P.sb("i1s", [128, 128])
    i2s = P.sb("i2s", [128, 128])
    eif = P.sb("eif", [128, 128])
    EI = [P.sb("eidx%d" % i, [128, 128], I32) for i in range(2)]
    GS = [P.sb("gsm%d" % i, [128, 128]) for i in range(2)]
    gsum = P.sb("gsum", [128, 8])
    NB = int(os.environ.get("K_NB", "12"))
    GRP = int(os.environ.get("K_GRP", "2"))
    uvb = [P.sb("uvb%d" % i, [128, 2 * D], BF16) for i in range(NB)]
    NDG = 6
    dg = [P.sb("dg%d" % i, [128, 128], BF16) for i in range(NDG)]
    DT = [P.sb("dt%d" % i, [128, GRP]) for i in range(4)]
    GA = [P.sb("gga%d" % i, [128, GRP]) for i in range(4)]
    GB = [P.sb("ggb%d" % i, [128, GRP]) for i in range(4)]
    GC = [P.sb("ggc%d" % i, [128, GRP]) for i in range(4)]
    WT = [P.sb("wt%d" % i, [128, GRP]) for i in range(4)]

    ptr = P.ps("ptr", [128, 8, 128], BF16)
    pA = P.ps("pA", [128, D])
    pV = P.ps("pV", [128, D])
    pS = P.ps("pS", [128, 512])
    pO = [P.ps("pO%d" % i, [128, 512]) for i in range(2)]

    def bc_part(ap, n=128):
        return ap.partition_broadcast(n)

    for i in range(3):
        dma('sp', lambda e, i=i: e.dma_start(out=F4c[:], in_=bc_part(lng_d[i])), writes=[F4c])
        op('dve', lambda e, i=i: e.tensor_scalar(out=lng[i][:], in0=F4c[:], scalar1=-1.0, scalar2=None, op0=ALU.add),
           reads=[F4c], writes=[lng[i]])
        dma('sp', lambda e, i=i: e.dma_start(out=F4a[:], in_=bc_part(lnb_d[i])), writes=[F4a])
        op('dve', lambda e, i=i: e.tensor_copy(out=lnb[i][:], in_=F4a[:]), reads=[F4a], writes=[lnb[i]])
    dma('sp', lambda e: e.dma_start(out=gng[:], in_=bc_part(gng_d)), writes=[gng])
    dma('sp', lambda e: e.dma_start(out=esink[:], in_=bc_part(sinks_d)), writes=[esink])
    op('act', lambda e: e.activation(out=esink[:], in_=esink[:], func=AF.Exp), reads=[esink], writes=[esink])
    dma('sp', lambda e: e.dma_start(out=dqk[:], in_=dqk_d), writes=[dqk])
    dma('sp', lambda e: e.dma_start(out=gl[:], in_=gl_d), writes=[gl])
    dma('sp', lambda e: e.dma_start(out=rmask[:], in_=rmask_d), writes=[rmask])
    dma('sp', lambda e: e.dma_start(out=iota16[:], in_=iota16_d), writes=[iota16])
    dma('sp', lambda e: e.dma_start(out=F4a[:].rearrange("p (h q) -> p h q", h=8), in_=bprev_d), writes=[F4a])
    dma('sp', lambda e: e.dma_start(out=F4c[:].rearrange("p (h q) -> p h q", h=8), in_=bcur_d), writes=[F4c])
    dma('sp', lambda e: e.dma_start(out=b_mf[:], in_=bmf_d), writes=[b_mf])
    dma('sp', lambda e: e.dma_start(out=tA[:, 0:128], in_=mprev_d), writes=[tA])
    dma('sp', lambda e: e.dma_start(out=tB[:, 0:128], in_=mcur_d), writes=[tB])
    op('dve', lambda e: e.tensor_tensor(out=b_prev[:], in0=F4a[:].rearrange("p (h q) -> p h q", h=8),
                                        in1=tA[:, 0:128].unsqueeze(1).to_broadcast([128, 8, 128]), op=ALU.add),
       reads=[F4a, tA], writes=[b_prev])
    op('dve', lambda e: e.tensor_tensor(out=b_cur[:], in0=F4c[:].rearrange("p (h q) -> p h q", h=8),
                                        in1=tB[:, 0:128].unsqueeze(1).to_broadcast([128, 8, 128]), op=ALU.add),
       reads=[F4c, tB], writes=[b_cur])
    iota_i = EI[0]
    identf = sS
    op('pool', lambda e: e.iota(iota_i[:], pattern=[[1, 128]], base=0, channel_multiplier=-1), writes=[iota_i])
    op('dve', lambda e: e.tensor_copy(out=identf[:, 0:128], in_=iota_i[:]), reads=[iota_i], writes=[identf])
    op('dve', lambda e: e.tensor_scalar(out=ident[:], in0=identf[:, 0:128], scalar1=0.0, scalar2=None, op0=ALU.is_equal),
       reads=[identf], writes=[ident])
    for t_ in VX + [VXm]:
        op('dve', lambda e, t_=t_: e.memset(t_[:], 1.0), writes=[t_])
    for t_ in EI:
        op('dve', lambda e, t_=t_: e.memset(t_[:], 0), writes=[t_])
    op('dve', lambda e: e.memset(state[:], 0.0), writes=[state])
    op('dve', lambda e: e.memset(stbf[:], 0.0), writes=[stbf])

    stg = [F4a, H1[0], H1[1], F4c]
    nst = [0]

    def load_cast(dst_fn, src_ap, ncols, dst_tile, permq=False):
        s_ = stg[nst[0] % len(stg)]
        eng = 'act' if (nst[0] % 2 == 0) else 'dve'
        nst[0] += 1
        dma('sp', lambda e: e.dma_start(out=s_[:, 0:ncols], in_=src_ap), writes=[s_])
        if permq:
            def f(e):
                return e.tensor_copy(out=dst_fn(0, 512).rearrange("p (h g d) -> p h g d", h=4, g=2),
                                     in_=s_[:, 0:512].rearrange("p (g h d) -> p h g d", g=2, h=4))
            op('dve', f, reads=[s_], writes=[dst_tile])
            if eng == 'act':
                op('act', lambda e: e.activation(out=dst_fn(512, ncols), in_=s_[:, 512:ncols], func=AF.Copy),
                   reads=[s_], writes=[dst_tile])
            else:
                op('dve', lambda e: e.tensor_copy(out=dst_fn(512, ncols), in_=s_[:, 512:ncols]),
                   reads=[s_], writes=[dst_tile])
        else:
            if eng == 'act':
                op('act', lambda e: e.activation(out=dst_fn(0, ncols), in_=s_[:, 0:ncols], func=AF.Copy),
                   reads=[s_], writes=[dst_tile])
            else:
                op('dve', lambda e: e.tensor_copy(out=dst_fn(0, ncols), in_=s_[:, 0:ncols]),
                   reads=[s_], writes=[dst_tile])

    for kc in range(8):
        for (c0, c1) in ((0, 1024), (1024, 2048), (2048, PW)):
            load_cast(lambda a, b, kc=kc, c0=c0: w_in[:, kc, c0 + a:c0 + b],
                      win_d[kc * 128:(kc + 1) * 128, c0:c1], c1 - c0, w_in, permq=(c0 == 0))
    for kc in range(8):
        load_cast(lambda a, b, kc=kc: w_out[:, kc, a:b], wout_d[kc * 128:(kc + 1) * 128, :], D, w_out)
    for kc in range(8):
        load_cast(lambda a, b, kc=kc: w_q[:, kc, a:b], wq_d[kc * 128:(kc + 1) * 128, :], D, w_q)
    for hp in range(2):
        s_ = stg[hp]
        dma('sp', lambda e, hp=hp, s_=s_: e.dma_start(out=s_[:, 0:1024].rearrange("p (k n) -> p k n", k=4),
                                                      in_=skb_d[hp * 4:(hp + 1) * 4].rearrange("k p n -> p k n")),
            writes=[s_])
        op('dve', lambda e, hp=hp, s_=s_: e.tensor_copy(out=skb[:, hp * 4:(hp + 1) * 4, :],
                                                        in_=s_[:, 0:1024].rearrange("p (k n) -> p k n", k=4)),
           reads=[s_], writes=[skb])

    UVBF = T(None, "uvbf")
    bfs = uvb
    chunks = []
    for (src_d, off) in ((u_d, 0), (v_d, D)):
        src_v = src_d.rearrange("(p r) d -> p r d", p=128)
        dst_v = uvbf_d.rearrange("(p r) d -> p r d", p=128)[:, :, off:off + D]
        for c in range(128):
            chunks.append((src_v, dst_v, UVBF, c))
    DEPTH = 3

    def conv_load(i):
        src_v, dst_v, dst_T, c = chunks[i]
        s_ = stg[i % 4]
        dma('sp', lambda e: e.dma_start(out=s_[:, :], in_=src_v[:, c, :]), writes=[s_])

    def conv_cast_store(i):
        src_v, dst_v, dst_T, c = chunks[i]
        s_ = stg[i % 4]
        b_ = bfs[i % NB]
        eng = ('act', 'dve', 'pool')[i % 3]
        if eng == 'act':
            op('act', lambda e: e.activation(out=b_[:, 0:D], in_=s_[:, :], func=AF.Copy), reads=[s_], writes=[b_])
        else:
            op(eng, lambda e: e.tensor_copy(out=b_[:, 0:D], in_=s_[:, :]), reads=[s_], writes=[b_])
        dma('sp', lambda e: e.dma_start(out=dst_v[:, c, :], in_=b_[:, 0:D]), reads=[b_], writes=[dst_T])

    for i in range(len(chunks) + DEPTH):
        if i < len(chunks):
            conv_load(i)
        if i >= DEPTH:
            conv_cast_store(i - DEPTH)

    dma('sp', lambda e: e.dma_start(out=b_m0[:NMETA, :].rearrange("p (h q) -> p h q", h=8), in_=bm0_d), writes=[b_m0])

    def layernorm(src, dst, gi, nt):
        for c in range(2):
            op('dve', lambda e, c=c: e.bn_stats(out=lst[:nt, c, :], in_=src[:nt, c * 512:(c + 1) * 512]),
               reads=[src], writes=[lst])
        op('dve', lambda e: e.bn_aggr(out=lmv[:nt, :], in_=lst[:nt].rearrange("p a b -> p (a b)")),
           reads=[lst], writes=[lmv])
        op('act', lambda e: e.activation(out=lrs[:nt, :], in_=lmv[:nt, 1:2], func=AF.Sqrt, bias=EPS, scale=1.0),
           reads=[lmv], writes=[lrs])
        op('dve', lambda e: e.reciprocal(out=lrs[:nt, :], in_=lrs[:nt, :]), reads=[lrs], writes=[lrs])
        op('dve', lambda e: e.tensor_scalar(out=dst[:nt, :], in0=src[:nt, :], scalar1=lmv[:nt, 0:1],
                                            scalar2=lrs[:nt, 0:1], op0=ALU.subtract, op1=ALU.mult),
           reads=[src, lmv, lrs], writes=[dst])
        op('dve', lambda e: e.scalar_tensor_tensor(out=dst[:nt, :], in0=lng[gi][:nt, :], scalar=1.0, in1=dst[:nt, :],
                                                   op0=ALU.add, op1=ALU.mult), reads=[dst, lng[gi]], writes=[dst])
        op('dve', lambda e: e.tensor_tensor(out=dst[:nt, :], in0=dst[:nt, :], in1=lnb[gi][:nt, :], op=ALU.add),
           reads=[dst, lnb[gi]], writes=[dst])

    def transpose_to(src_bf, dstT, nt, nblk=8, evac='dve'):
        for j in range(nblk):
            op('pe', lambda e, j=j: e.transpose(out=ptr[:, j, :nt], in_=src_bf[:nt, j * 128:(j + 1) * 128],
                                                identity=ident[:nt, :nt]),
               reads=[src_bf, ident], writes=[ptr])
        if evac == 'dve':
            op('dve', lambda e: e.tensor_copy(out=dstT[:, 0:nblk, :nt], in_=ptr[:, 0:nblk, :nt]),
               reads=[ptr], writes=[dstT])
        else:
            op('act', lambda e: e.activation(out=dstT[:, 0:nblk, :nt], in_=ptr[:, 0:nblk, :nt], func=AF.Copy),
               reads=[ptr], writes=[dstT])

    def mm_full(out_ps, lhsT8, nt, rhs_w, c0, c1):
        c = c0
        while c < c1:
            ce = min(c + 512, c1)
            for kc in range(8):
                op('pe', lambda e, kc=kc, c=c, ce=ce: e.matmul(out_ps[:nt, c - c0:ce - c0], lhsT=lhsT8[:, kc, :nt],
                                                              rhs=rhs_w[:, kc, c:ce], start=(kc == 0), stop=(kc == 7)),
                   reads=[lhsT8, rhs_w], writes=[out_ps])
            c = ce

    def dbg_dump(name, tile_, ncols, nt):
        if dbg and name in dbg_d:
            dma('sp', lambda e: e.dma_start(out=dbg_d[name][:nt, 0:ncols], in_=tile_[:nt, 0:ncols]),
                reads=[tile_], store_src=tile_)

    def tile_params(kind, ti):
        if kind == 'meta':
            return NMETA, 17, 2, xm_d
        if kind == 'prompt':
            return 128, ti, 0, xp_d[ti * 128:(ti + 1) * 128, :]
        return DS, 16, 1, xs_d

    def sample_setup():
        dma('sp', lambda e: e.dma_start(out=rs_d.rearrange("h d e -> d h e"), in_=state[:].rearrange("p (h e) -> p h e", h=4)),
            reads=[state], store_src=state)
        dma('sp', lambda e: e.dma_start(out=kvf[:, 0:128], in_=csk_d), writes=[kvf])
        dma('sp', lambda e: e.dma_start(out=kvf[:, 128:256], in_=csv_d), writes=[kvf])
        op('act', lambda e: e.activation(out=kbf[:, :], in_=kvf[:, 0:128], func=AF.Copy), reads=[kvf], writes=[kbf])
        op('dve', lambda e: e.tensor_copy(out=VX[0][:, :, 0:64], in_=kvf[:, 128:256].rearrange("p (g d) -> p g d", g=2)),
           reads=[kvf], writes=[VX[0]])
        op('pe', lambda e: e.transpose(out=ptr[:, 0, :], in_=kbf[:, :], identity=ident[:, :]), reads=[kbf, ident], writes=[ptr])
        op('dve', lambda e: e.tensor_copy(out=KT[0][:, :], in_=ptr[:, 0, :]), reads=[ptr], writes=[KT[0]])
        dma('sp', lambda e: e.dma_start(out=kvf[:NMETA, 0:128], in_=cmk_d), writes=[kvf])
        dma('sp', lambda e: e.dma_start(out=kvf[:NMETA, 128:256], in_=cmv_d), writes=[kvf])
        op('act', lambda e: e.activation(out=kbf[:NMETA, :], in_=kvf[:NMETA, 0:128], func=AF.Copy), reads=[kvf], writes=[kbf])
        op('dve', lambda e: e.tensor_copy(out=VXm[:NMETA, :, 0:64], in_=kvf[:NMETA, 128:256].rearrange("p (g d) -> p g d", g=2)),
           reads=[kvf], writes=[VXm])
        op('pe', lambda e: e.transpose(out=ptr[:, 0, :NMETA], in_=kbf[:NMETA, :], identity=ident[:NMETA, :NMETA]),
           reads=[kbf, ident], writes=[ptr])
        op('dve', lambda e: e.tensor_copy(out=KTm[:, :NMETA], in_=ptr[:, 0, :NMETA]), reads=[ptr], writes=[KTm])
        dma('sp', lambda e: e.dma_start(out=state[:].rearrange("p (h e) -> p h e", h=4), in_=st0_d.rearrange("h d e -> d h e")),
            writes=[state])
        op('act', lambda e: e.activation(out=stbf[:], in_=state[:], func=AF.Copy), reads=[state], writes=[stbf])

    def front(kind, ti, k):
        nt, rti, gli, x_src = tile_params(kind, ti)
        gsm = GS[k % 2]
        if kind == 'sample':
            sample_setup()
            yield
        h_t = F4a
        h1 = H1[k % 2]
        eidx = EI[k % 2]
        dma('sp', lambda e: e.dma_start(out=h_t[:nt, :], in_=x_src), writes=[h_t])
        dma('sp', lambda e: e.dma_start(out=rot[:, :], in_=rot_d[rti]), writes=[rot])
        layernorm(h_t, h_t, 0, nt)
        yield
        op('act', lambda e: e.activation(out=B2a[:nt, :], in_=h_t[:nt, :], func=AF.Copy), reads=[h_t], writes=[B2a])
        transpose_to(B2a, T8a, nt, 8, 'dve')
        hT = T8a
        yield
        mm_full(pA, hT, nt, w_in, 0, 768)
        yield
        op('act', lambda e: e.activation(out=qbf[:nt, :], in_=pA[:nt, 0:512], func=AF.Copy), reads=[pA], writes=[qbf])
        op('dve', lambda e: e.tensor_copy(out=kvf[:nt, :], in_=pA[:nt, 512:768]), reads=[pA], writes=[kvf])
        if kind == 'meta':
            dma('sp', lambda e: e.dma_start(out=mk_d, in_=kvf[:nt, 0:128]), reads=[kvf], store_src=kvf)
            dma('sp', lambda e: e.dma_start(out=mv_d, in_=kvf[:nt, 128:256]), reads=[kvf], store_src=kvf)
        elif kind == 'prompt' and ti == NTILES - 1:
            dma('sp', lambda e: e.dma_start(out=sk_d, in_=kvf[:nt, 0:128]), reads=[kvf], store_src=kvf)
            dma('sp', lambda e: e.dma_start(out=sv_d, in_=kvf[:nt, 128:256]), reads=[kvf], store_src=kvf)
        elif kind == 'sample':
            dma('sp', lambda e: e.dma_start(out=sks_d, in_=kvf[:nt, 0:128]), reads=[kvf], store_src=kvf)
            dma('sp', lambda e: e.dma_start(out=svs_d, in_=kvf[:nt, 128:256]), reads=[kvf], store_src=kvf)
        if kind == 'meta':
            kt_cur, vx_cur = KTm, VXm
        elif kind == 'prompt':
            kt_cur, vx_cur = KT[ti % 2], VX[ti % 2]
        else:
            kt_cur, vx_cur = KT[1], VX[1]
        op('act', lambda e: e.activation(out=kbf[:nt, :], in_=kvf[:nt, 0:128], func=AF.Copy), reads=[kvf], writes=[kbf])
        op('dve', lambda e: e.tensor_copy(out=vx_cur[:nt, :, 0:64],
                                          in_=kvf[:nt, 128:256].rearrange("p (g d) -> p g d", g=2)),
           reads=[kvf], writes=[vx_cur])
        op('pe', lambda e: e.transpose(out=ptr[:, 0, :nt], in_=kbf[:nt, :], identity=ident[:nt, :nt]),
           reads=[kbf, ident], writes=[ptr])
        op('dve', lambda e: e.tensor_copy(out=kt_cur[:, :nt], in_=ptr[:, 0, :nt]), reads=[ptr], writes=[kt_cur])
        yield
        mm_full(pA, hT, nt, w_in, 768, 1792)
        yield
        rqk = F4c
        cos_b = rot[:nt, 0:64].unsqueeze(1).to_broadcast([nt, 8, 64])
        sin_b = rot[:nt, 64:128].unsqueeze(1).to_broadcast([nt, 8, 64])
        pv = pA[:nt, :].rearrange("p (a t d) -> p a t d", a=8, t=2)
        x1 = pv[:, :, 0, :]
        x2 = pv[:, :, 1, :]
        tAv = tA[:nt, :].rearrange("p (a d) -> p a d", a=8)
        tBv = tB[:nt, :].rearrange("p (a d) -> p a d", a=8)
        rv4 = rqk[:nt, :].rearrange("p (a t d) -> p a t d", a=8, t=2)
        op('dve', lambda e: e.tensor_tensor(out=tAv, in0=x1, in1=cos_b, op=ALU.mult), reads=[pA, rot], writes=[tA])
        op('dve', lambda e: e.tensor_tensor(out=tBv, in0=x2, in1=sin_b, op=ALU.mult), reads=[pA, rot], writes=[tB])
        op('dve', lambda e: e.tensor_tensor(out=rv4[:, :, 0, :], in0=tAv, in1=tBv, op=ALU.subtract),
           reads=[tA, tB], writes=[rqk])
        op('dve', lambda e: e.tensor_tensor(out=tAv, in0=x1, in1=sin_b, op=ALU.mult), reads=[pA, rot], writes=[tA])
        op('dve', lambda e: e.tensor_tensor(out=tBv, in0=x2, in1=cos_b, op=ALU.mult), reads=[pA, rot], writes=[tB])
        op('dve', lambda e: e.tensor_tensor(out=rv4[:, :, 1, :], in0=tAv, in1=tBv, op=ALU.add),
           reads=[tA, tB], writes=[rqk])
        op('dve', lambda e: e.tensor_tensor(out=rqkb[:nt, :].rearrange("p (a d) -> p a d", a=8),
                                            in0=rqk[:nt, :].rearrange("p (a d) -> p a d", a=8),
                                            in1=dqk[:nt, :].unsqueeze(2).to_broadcast([nt, 8, 128]), op=ALU.mult),
           reads=[rqk, dqk], writes=[rqkb])
        yield
        mm_full(pA, hT, nt, w_in, 1792, PW)
        op('act', lambda e: e.activation(out=rvb[:nt, :], in_=pA[:nt, 0:512], func=AF.Copy), reads=[pA], writes=[rvb])
        if kind != 'meta':
            op('act', lambda e: e.activation(out=gate[:nt, :], in_=pA[:nt, 512:1024], func=AF.Silu),
               reads=[pA], writes=[gate])
        yield
        mixed = B2b
        if kind != 'meta':
            transpose_to(qbf, QT, nt, 4, 'act')
            if kind == 'prompt':
                groups = []
                if ti > 0:
                    groups.append((KT[(ti - 1) % 2], VX[(ti - 1) % 2], 128, b_prev, 0))
                groups.append((kt_cur, vx_cur, nt, b_cur, 1))
                groups.append((KTm, VXm, NMETA, b_m0 if ti == 0 else b_mf, 2))
            else:
                groups = [(KT[0], VX[0], 128, b_prev, 0), (kt_cur, vx_cur, nt, b_cur, 1), (KTm, VXm, NMETA, b_mf, 2)]
            for g in range(2):
                pr = slice(g * 64, (g + 1) * 64)
                for (kt_, vx_, nk, btab, a) in groups:
                    op('pe', lambda e, kt_=kt_, nk=nk, pr=pr: e.matmul(
                        pS[:nk, 0:4 * nt].rearrange("p (h q) -> p h q", h=4), lhsT=kt_[pr, :nk], rhs=QT[pr, :, :nt],
                        start=True, stop=True), reads=[kt_, QT], writes=[pS])
                    if btab is b_mf:
                        bias_ap = btab[:nk, g * 4:(g + 1) * 4].unsqueeze(2).to_broadcast([nk, 4, nt])
                    elif btab is b_m0:
                        bias_ap = b_m0[:nk, :].rearrange("p (h q) -> p h q", h=8)[:, g * 4:(g + 1) * 4, :nt]
                    else:
                        bias_ap = btab[:nk, g * 4:(g + 1) * 4, :nt]
                    op('dve', lambda e, nk=nk, bias_ap=bias_ap: e.scalar_tensor_tensor(
                        out=sS[:nk, 0:4 * nt].rearrange("p (h q) -> p h q", h=4),
                        in0=pS[:nk, 0:4 * nt].rearrange("p (h q) -> p h q", h=4), scalar=0.125,
                        in1=bias_ap, op0=ALU.mult, op1=ALU.add),
                        reads=[pS, btab], writes=[sS])
                    op('act', lambda e, nk=nk, a=a, g=g: e.activation(out=PT[a][g][:nk, 0:4 * nt], in_=sS[:nk, 0:4 * nt],
                                                                       func=AF.Exp), reads=[sS], writes=[PT[a][g]])
                yield
                for hh in range(4):
                    for gi_, (kt_, vx_, nk, btab, a) in enumerate(groups):
                        op('pe', lambda e, hh=hh, vx_=vx_, nk=nk, a=a, g=g, gi_=gi_: e.matmul(
                            pO[g][:nt, hh * 65:(hh + 1) * 65], lhsT=PT[a][g][:nk, hh * nt:(hh + 1) * nt],
                            rhs=vx_[:nk, g, :], start=(gi_ == 0), stop=(gi_ == len(groups) - 1)),
                            reads=[PT[a][g], vx_], writes=[pO[g]])
                pov = pO[g][:nt, 0:260].rearrange("p (h d) -> p h d", h=4)
                op('dve', lambda e, pov=pov, g=g: e.tensor_tensor(out=den[:nt, g * 4:(g + 1) * 4].unsqueeze(2),
                                                                    in0=pov[:, :, 64:65],
                                                                    in1=esink[:nt, g * 4:(g + 1) * 4].unsqueeze(2), op=ALU.add),
                   reads=[pO[g], esink], writes=[den])
                op('dve', lambda e, g=g: e.reciprocal(out=den[:nt, g * 4:(g + 1) * 4], in_=den[:nt, g * 4:(g + 1) * 4]),
                   reads=[den], writes=[den])
                op('dve', lambda e, pov=pov, g=g: e.tensor_tensor(
                    out=mixed[:nt, g * 256:(g + 1) * 256].rearrange("p (h d) -> p h d", h=4), in0=pov[:, :, 0:64],
                    in1=den[:nt, g * 4:(g + 1) * 4].unsqueeze(2).to_broadcast([nt, 4, 64]), op=ALU.mult),
                   reads=[pO[g], den], writes=[mixed])
                yield

        transpose_to(rqkb, T8b, nt, 8, 'act')
        RT = T8b
        if kind != 'meta':
            for h in range(4):
                op('pe', lambda e, h=h: e.matmul(pS[:nt, h * 128:h * 128 + nt], lhsT=RT[:, 4 + h, :nt], rhs=RT[:, h, :nt],
                                                 start=True, stop=True), reads=[RT], writes=[pS])
            op('dve', lambda e: e.tensor_tensor(out=SmT[:nt, :, :nt],
                                                in0=pS[:nt, :].rearrange("p (h q) -> p h q", h=4)[:, :, :nt],
                                                in1=rmask[:nt, :nt].unsqueeze(1).to_broadcast([nt, 4, nt]), op=ALU.mult),
               reads=[pS, rmask], writes=[SmT])
            for h in range(4):
                op('pe', lambda e, h=h: e.matmul(pO[0][:nt, h * 128:(h + 1) * 128], lhsT=SmT[:nt, h, :nt],
                                                 rhs=rvb[:nt, h * 128:(h + 1) * 128], start=True, stop=False),
                   reads=[SmT, rvb], writes=[pO[0]])
                op('pe', lambda e, h=h: e.matmul(pO[0][:nt, h * 128:(h + 1) * 128], lhsT=RT[:, h, :nt],
                                                 rhs=stbf[:, h * 128:(h + 1) * 128], start=False, stop=True),
                   reads=[RT, stbf], writes=[pO[0]])
        yield
        for h in range(4):
            op('pe', lambda e, h=h: e.matmul(pO[1][:, h * 128:(h + 1) * 128], lhsT=rqkb[:nt, 512 + h * 128:512 + (h + 1) * 128],
                                             rhs=rvb[:nt, h * 128:(h + 1) * 128], start=True, stop=True),
               reads=[rqkb, rvb], writes=[pO[1]])
        op('dve', lambda e: e.tensor_tensor(out=tmpS[:], in0=pO[1][:], in1=state[:], op=ALU.add),
           reads=[pO[1], state], writes=[tmpS])
        op('dve', lambda e: e.tensor_tensor(out=state[:].rearrange("p (h e) -> p h e", h=4),
                                            in0=tmpS[:].rearrange("p (h e) -> p h e", h=4),
                                            in1=gl[:, gli * 4:(gli + 1) * 4].unsqueeze(2).to_broadcast([128, 4, 128]),
                                            op=ALU.mult), reads=[tmpS, gl], writes=[state])
        op('act', lambda e: e.activation(out=stbf[:], in_=state[:], func=AF.Copy), reads=[state], writes=[stbf])
        yield
        if kind == 'meta':
            return

        for h in range(4):
            op('dve', lambda e, h=h: e.bn_stats(out=gst[:nt, h, :], in_=pO[0][:nt, h * 128:(h + 1) * 128]),
               reads=[pO[0]], writes=[gst])
        for h in range(4):
            op('dve', lambda e, h=h: e.bn_aggr(out=gmv[:nt, h, :], in_=gst[:nt, h, :]), reads=[gst], writes=[gmv])
        op('act', lambda e: e.activation(out=grs[:nt, :].unsqueeze(2), in_=gmv[:nt, :, 1:2], func=AF.Sqrt, bias=EPS, scale=1.0),
           reads=[gmv], writes=[grs])
        op('dve', lambda e: e.reciprocal(out=grs[:nt, :], in_=grs[:nt, :]), reads=[grs], writes=[grs])
        yv = yA[:nt, :].rearrange("p (h d) -> p h d", h=4)
        op('dve', lambda e: e.tensor_tensor(out=yv, in0=pO[0][:nt, :].rearrange("p (h d) -> p h d", h=4),
                                            in1=gmv[:nt, :, 0:1].to_broadcast([nt, 4, 128]), op=ALU.subtract),
           reads=[pO[0], gmv], writes=[yA])
        op('dve', lambda e: e.tensor_tensor(out=yv, in0=yv, in1=grs[:nt, :].unsqueeze(2).to_broadcast([nt, 4, 128]),
                                            op=ALU.mult), reads=[yA, grs], writes=[yA])
        op('dve', lambda e: e.tensor_tensor(out=yA[:nt, :], in0=yA[:nt, :], in1=gng[:nt, :], op=ALU.mult),
           reads=[yA, gng], writes=[yA])
        op('dve', lambda e: e.tensor_tensor(out=mixed[:nt, 512:1024], in0=yA[:nt, :], in1=gate[:nt, :], op=ALU.mult),
           reads=[yA, gate], writes=[mixed])
        yield

        transpose_to(mixed, T8a, nt, 8, 'dve')
        mm_full(pA, T8a, nt, w_out, 0, D)
        yield
        op('dve', lambda e: e.scalar_tensor_tensor(out=h1[:nt, :], in0=h_t[:nt, :], scalar=ALPHA, in1=pA[:nt, :],
                                                   op0=ALU.mult, op1=ALU.add), reads=[h_t, pA], writes=[h1])
        layernorm(h1, h1, 1, nt)
        yield

        op('act', lambda e: e.activation(out=B2a[:nt, :], in_=h1[:nt, :], func=AF.Copy), reads=[h1], writes=[B2a])
        transpose_to(B2a, T8a, nt, 8, 'dve')
        mm_full(pA, T8a, nt, w_q, 0, D)
        yield
        op('act', lambda e: e.activation(out=B2b[:nt, :], in_=pA[:nt, :], func=AF.Copy), reads=[pA], writes=[B2b])
        transpose_to(B2b, T8b, nt, 8, 'act')
        s_sb = F4c
        for hf in range(2):
            for pl in range(4):
                p_ = hf * 4 + pl
                op('pe', lambda e, p_=p_, pl=pl: e.matmul(pA[:nt, pl * 256:(pl + 1) * 256], lhsT=T8b[:, p_, :nt],
                                                          rhs=skb[:, p_, :], start=True, stop=True),
                   reads=[T8b, skb], writes=[pA])
            if hf == 0:
                op('act', lambda e: e.activation(out=s_sb[:nt, :], in_=pA[:nt, :], func=AF.Copy), reads=[pA], writes=[s_sb])
            else:
                op('dve', lambda e: e.tensor_copy(out=s_sb[:nt, :], in_=pA[:nt, :]), reads=[pA], writes=[s_sb])
            for gl_ in range(8):
                gi_ = hf * 8 + gl_
                sg = slice(gl_ * 128, (gl_ + 1) * 128)
                w_ = sw[gi_ % 2]
                op('dve', lambda e, gi_=gi_, sg=sg: e.max(out=tv[:nt, gi_, 0:8], in_=s_sb[:nt, sg]), reads=[s_sb], writes=[tv])
                op('dve', lambda e, gi_=gi_, sg=sg: e.max_index(out=tiu[:nt, gi_, 0:8], in_max=tv[:nt, gi_, 0:8],
                                                                in_values=s_sb[:nt, sg]), reads=[s_sb, tv], writes=[tiu])
                op('dve', lambda e, gi_=gi_, sg=sg, w_=w_: e.match_replace(out=w_[:nt, 0:128], in_to_replace=tv[:nt, gi_, 0:8],
                                                                           in_values=s_sb[:nt, sg], imm_value=-1e30),
                   reads=[s_sb, tv], writes=[w_])
                op('dve', lambda e, gi_=gi_, w_=w_: e.max(out=tv[:nt, gi_, 8:16], in_=w_[:nt, 0:128]), reads=[w_], writes=[tv])
                op('dve', lambda e, gi_=gi_, w_=w_: e.max_index(out=tiu[:nt, gi_, 8:16], in_max=tv[:nt, gi_, 8:16],
                                                                in_values=w_[:nt, 0:128]), reads=[w_, tv], writes=[tiu])
                if gl_ % 2 == 1:
                    yield
        op('dve', lambda e: e.tensor_copy(out=tif[:nt], in_=tiu[:nt]), reads=[tiu], writes=[tif])
        tv4 = tv[:nt].rearrange("p (a c) k -> p a c k", c=2)
        tif4 = tif[:nt].rearrange("p (a c) k -> p a c k", c=2)
        cv = cand[:nt, :].rearrange("p (a k m) -> p a k m", a=4, k=16)
        for hf in range(2):
            hs = slice(hf * 4, (hf + 1) * 4)
            op('dve', lambda e, hs=hs: e.tensor_tensor(out=cv, in0=tv4[:, hs, 0, :].unsqueeze(3).to_broadcast([nt, 4, 16, 16]),
                                                       in1=tv4[:, hs, 1, :].unsqueeze(2).to_broadcast([nt, 4, 16, 16]), op=ALU.add),
               reads=[tv], writes=[cand])
            for pl in range(4):
                p_ = hf * 4 + pl
                sg = slice(pl * 256, (pl + 1) * 256)
                w_ = sw[p_ % 2]
                op('dve', lambda e, p_=p_, sg=sg: e.max(out=t2v[:nt, p_, 0:8], in_=cand[:nt, sg]), reads=[cand], writes=[t2v])
                op('dve', lambda e, p_=p_, sg=sg: e.max_index(out=t2i[:nt, p_, 0:8], in_max=t2v[:nt, p_, 0:8],
                                                              in_values=cand[:nt, sg]), reads=[cand, t2v], writes=[t2i])
                op('dve', lambda e, p_=p_, sg=sg, w_=w_: e.match_replace(out=w_[:nt, :], in_to_replace=t2v[:nt, p_, 0:8],
                                                                         in_values=cand[:nt, sg], imm_value=-1e30),
                   reads=[cand, t2v], writes=[w_])
                op('dve', lambda e, p_=p_, w_=w_: e.max(out=t2v[:nt, p_, 8:16], in_=w_[:nt, :]), reads=[w_], writes=[t2v])
                op('dve', lambda e, p_=p_, w_=w_: e.max_index(out=t2i[:nt, p_, 8:16], in_max=t2v[:nt, p_, 8:16],
                                                              in_values=w_[:nt, :]), reads=[w_, t2v], writes=[t2i])
                if pl % 2 == 1:
                    yield
        t2if = t2i[:nt].rearrange("p a k -> p (a k)")
        op('dve', lambda e: e.tensor_single_scalar(out=k1u[:nt, :], in_=t2if, scalar=4, op=ALU.logical_shift_right),
           reads=[t2i], writes=[k1u])
        op('dve', lambda e: e.tensor_single_scalar(out=k2u[:nt, :], in_=t2if, scalar=15, op=ALU.bitwise_and),
           reads=[t2i], writes=[k2u])
        op('dve', lambda e: e.tensor_copy(out=k1f[:nt, :], in_=k1u[:nt, :]), reads=[k1u], writes=[k1f])
        op('dve', lambda e: e.tensor_copy(out=k2f[:nt, :], in_=k2u[:nt, :]), reads=[k2u], writes=[k2f])
        ohv = s_sb[:nt, :].rearrange("p (a k m) -> p a k m", a=4, k=16)
        iob = iota16[:nt, :].unsqueeze(1).unsqueeze(1).to_broadcast([nt, 4, 16, 16])
        for (kf, c_, dst_) in ((k1f, 0, i1s), (k2f, 1, i2s)):
            for hf in range(2):
                hs = slice(hf * 4, (hf + 1) * 4)
                js = slice(hf * 64, (hf + 1) * 64)
                op('dve', lambda e, kf=kf, js=js: e.tensor_tensor(
                    out=ohv, in0=iob,
                    in1=kf[:nt, js].rearrange("p (a k) -> p a k", a=4).unsqueeze(3).to_broadcast([nt, 4, 16, 16]),
                    op=ALU.is_equal), reads=[iota16, kf], writes=[s_sb])
                op('dve', lambda e, c_=c_, hs=hs: e.tensor_tensor(
                    out=ohv, in0=ohv, in1=tif4[:, hs, c_, :].unsqueeze(2).to_broadcast([nt, 4, 16, 16]), op=ALU.mult),
                   reads=[s_sb, tif], writes=[s_sb])
                op('dve', lambda e, dst_=dst_, js=js: e.tensor_reduce(out=dst_[:nt, js].rearrange("p (a k) -> p a k", a=4),
                                                                     in_=ohv, axis=AX.X, op=ALU.add),
                   reads=[s_sb], writes=[dst_])
                yield
        op('dve', lambda e: e.scalar_tensor_tensor(out=eif[:nt, :], in0=i1s[:nt, :], scalar=128.0, in1=i2s[:nt, :],
                                                   op0=ALU.mult, op1=ALU.add), reads=[i1s, i2s], writes=[eif])
        op('dve', lambda e: e.tensor_copy(out=eidx[:nt, :], in_=eif[:nt, :]), reads=[eif], writes=[eidx])
        gv = gsm[:nt, :].rearrange("p (a k) -> p a k", a=8)
        op('dve', lambda e: e.tensor_tensor(out=gv, in0=t2v[:nt], in1=t2v[:nt, :, 0:1].to_broadcast([nt, 8, 16]),
                                            op=ALU.subtract), reads=[t2v], writes=[gsm])
        op('act', lambda e: e.activation(out=gsm[:nt, :], in_=gsm[:nt, :], func=AF.Exp), reads=[gsm], writes=[gsm])
        op('dve', lambda e: e.tensor_reduce(out=gsum[:nt, :], in_=gv, axis=AX.X, op=ALU.add), reads=[gsm], writes=[gsum])
        op('dve', lambda e: e.reciprocal(out=gsum[:nt, :], in_=gsum[:nt, :]), reads=[gsum], writes=[gsum])
        op('dve', lambda e: e.tensor_tensor(out=gv, in0=gv, in1=gsum[:nt, :].unsqueeze(2).to_broadcast([nt, 8, 16]),
                                            op=ALU.mult), reads=[gsm, gsum], writes=[gsm])
        yield

    def uvphase(kind, ti, k):
        nt = 128 if kind == 'prompt' else DS
        h1 = H1[k % 2]
        eidx = EI[k % 2]
        gsm = GS[k % 2]
        NG = NJ // GRP
        NS = len(DT)

        def part1(g):
            s_ = g % NS
            dt_, ga_, gb_, gc_ = DT[s_], GA[s_], GB[s_], GC[s_]
            gs = slice(g * GRP, (g + 1) * GRP)
            for jj in range(GRP):
                j = g * GRP + jj
                b_ = uvb[j % NB]
                dma('pool', lambda e, j=j, b_=b_: e.indirect_dma_start(
                    out=b_[:, :], out_offset=None, in_=uvbf_d,
                    in_offset=bass.IndirectOffsetOnAxis(ap=eidx[:, j:j + 1], axis=0)), reads=[eidx, UVBF], writes=[b_])
                op('dve', lambda e, jj=jj, b_=b_, dt_=dt_: e.scalar_tensor_tensor(
                    out=b_[:nt, 0:D], in0=b_[:nt, 0:D], scalar=1.0, in1=h1[:nt, :], op0=ALU.mult, op1=ALU.mult,
                    accum_out=dt_[:nt, jj:jj + 1]), reads=[b_, h1], writes=[b_, dt_])
            op('dve', lambda e: e.scalar_tensor_tensor(out=ga_[:nt, :], in0=dt_[:nt, :], scalar=0.044715,
                                                       in1=dt_[:nt, :], op0=ALU.mult, op1=ALU.mult),
               reads=[dt_], writes=[ga_])
            op('dve', lambda e: e.scalar_tensor_tensor(out=ga_[:nt, :], in0=ga_[:nt, :], scalar=1.0,
                                                       in1=dt_[:nt, :], op0=ALU.add, op1=ALU.mult),
               reads=[ga_, dt_], writes=[ga_])
            op('dve', lambda e: e.scalar_tensor_tensor(out=gc_[:nt, :], in0=dt_[:nt, :], scalar=0.5, in1=gsm[:nt, gs],
                                                       op0=ALU.mult, op1=ALU.mult), reads=[dt_, gsm], writes=[gc_])

        def part1b(g):
            s_ = g % NS
            ga_, gb_ = GA[s_], GB[s_]
            op('act', lambda e: e.activation(out=gb_[:nt, :], in_=ga_[:nt, :], func=AF.Tanh,
                                             scale=0.7978845608028654), reads=[ga_], writes=[gb_])

        def part2(g):
            s_ = g % NS
            gb_, gc_, wt_ = GB[s_], GC[s_], WT[s_]
            op('dve', lambda e: e.scalar_tensor_tensor(out=wt_[:nt, :], in0=gb_[:nt, :], scalar=1.0, in1=gc_[:nt, :],
                                                       op0=ALU.add, op1=ALU.mult), reads=[gb_, gc_], writes=[wt_])
            for jj in range(GRP):
                j = g * GRP + jj
                b_ = uvb[j % NB]
                d_ = dg[j % NDG]
                op('act', lambda e, jj=jj, d_=d_: e.activation(out=d_[:nt, :nt], in_=ident[:nt, :nt], func=AF.Copy,
                                                               scale=wt_[:nt, jj:jj + 1]), reads=[ident, wt_], writes=[d_])
                for hb in range(2):
                    op('pe', lambda e, j=j, b_=b_, d_=d_, hb=hb: e.matmul(
                        pV[:nt, hb * 512:(hb + 1) * 512], lhsT=d_[:nt, :nt], rhs=b_[:nt, D + hb * 512:D + (hb + 1) * 512],
                        start=(j == 0), stop=(j == NJ - 1)), reads=[b_, d_], writes=[pV])

        D1 = int(os.environ.get("K_D1", "0"))
        D2 = int(os.environ.get("K_D2", "1"))
        for g in range(NG + D2):
            if g < NG:
                part1(g)
            if 0 <= g - D1 < NG:
                part1b(g - D1)
            if 0 <= g - D2 < NG:
                part2(g - D2)
            yield
        op('dve', lambda e: e.scalar_tensor_tensor(out=h1[:nt, :], in0=h1[:nt, :], scalar=ALPHA, in1=pV[:nt, :],
                                                   op0=ALU.mult, op1=ALU.add), reads=[h1, pV], writes=[h1])
        layernorm(h1, h1, 2, nt)
        if kind == 'prompt':
            dma('sp', lambda e: e.dma_start(out=yp_d[ti * 128:(ti + 1) * 128, :], in_=h1[:nt, :]), reads=[h1], store_src=h1)
        else:
            dma('sp', lambda e: e.dma_start(out=ys_d, in_=h1[:nt, :]), reads=[h1], store_src=h1)
            dma('sp', lambda e: e.dma_start(out=rss_d.rearrange("h d e -> d h e"), in_=state[:].rearrange("p (h e) -> p h e", h=4)),
                reads=[state], store_src=state)
        yield

    def run(g):
        n = 0
        for _ in g:
            n += 1
        return n

    def interleave(gens, ests):
        n = len(gens)
        prog = [0] * n
        alive = [True] * n
        while any(alive):
            best = None
            for i in range(n):
                if alive[i] and (best is None or prog[i] / ests[i] < prog[best] / ests[best]):
                    best = i
            try:
                next(gens[best])
                prog[best] += 1
            except StopIteration:
                alive[best] = False

    n_prompt = int(os.environ.get("K_NPROMPT", NTILES))
    tiles = [('prompt', i) for i in range(n_prompt)] + [('sample', 0)]
    K = len(tiles) - 1
    run(front('meta', 0, 0))
    fest = run(front(tiles[0][0], tiles[0][1], 0))
    nopipe = int(os.environ.get("K_NOPIPE", "0"))
    for k in range(K + 1):
        gens = [uvphase(tiles[k][0], tiles[k][1], k)]
        ests = [NJ // GRP + 3]
        if k + 1 <= K:
            gens.append(front(tiles[k + 1][0], tiles[k + 1][1], k + 1))
            ests.append(fest)
        if nopipe:
            for g_ in gens:
                run(g_)
        else:
            interleave(gens, ests)
    P.finish()
    P.emit()
    return nc


_NC_CACHE = {}


def _in_maps(inp):
    c = _host_consts()
    f = lambda a: np.ascontiguousarray(np.asarray(a, dtype=np.float32))
    rel_bias = f(inp['rel_bias'])
    shared = {
        "xm": f(inp['meta_tokens']),
        "ln0_g": f(inp['ln_in_g']), "ln0_b": f(inp['ln_in_b']),
        "ln1_g": f(inp['ln1_g'][0]), "ln1_b": f(inp['ln1_b'][0]),
        "ln2_g": f(inp['ln2_g'][0]), "ln2_b": f(inp['ln2_b'][0]),
        "w_in": f(inp['w_in'][0]), "w_out": f(inp['w_out'][0]), "wq": f(inp['peer_wq'][0]),
        "sinks": f(inp['attn_sinks'][0]), "gng": f(inp['ret_gn_g'][0]),
        "peer_u": f(inp['peer_u'][0]), "peer_v": f(inp['peer_v'][0]),
        "b_prev": f(rel_bias[c['bk_prev']].transpose(0, 2, 1)),
        "b_cur": f(rel_bias[c['bk_cur']].transpose(0, 2, 1)),
        "b_meta0": f(rel_bias[c['bk_meta0']].transpose(0, 2, 1)),
        "b_metaf": f(np.broadcast_to(rel_bias[15][None, :], (NMETA, 8))),
        "mask_prev": c['mask_prev'], "mask_cur": c['mask_cur'],
        "rot": c['rot'], "dqk": c['dqk'], "gl": c['gl'], "rmask": c['rmask'], "iota16": c['iota16'],
    }
    sk = f(inp['peer_subkeys'][0])
    skb = np.zeros((8, 128, 256), np.float32)
    for cc in range(2):
        skb[:, cc * 64:(cc + 1) * 64, cc * 128:(cc + 1) * 128] = sk[:, cc].transpose(0, 2, 1)
    shared["skb"] = skb
    maps = []
    for b in range(NCORES):
        m = dict(shared)
        m["xp"] = f(inp['x_prompt'][b])
        m["xs"] = f(inp['x_sample'][b])
        m["cmk"] = f(inp['cache_meta_k'][0, b]).reshape(NMETA, 128)
        m["cmv"] = f(inp['cache_meta_v'][0, b]).reshape(NMETA, 128)
        m["csk"] = f(inp['cache_swa_k'][0, b]).reshape(128, 128)
        m["csv"] = f(inp['cache_swa_v'][0, b]).reshape(128, 128)
        m["st0"] = f(inp['state_ret'][0, b])
        maps.append(m)
    return maps


def kernel(**inp):
    dbg = bool(int(os.environ.get("K_DEBUG", "0")))
    key = (dbg,) + tuple(os.environ.get(k_, "") for k_ in ("K_NPROMPT", "K_STAGE", "K_D1", "K_D2", "K_NB", "K_GRP", "K_NOPIPE"))
    if key not in _NC_CACHE:
        _NC_CACHE[key] = build_program(dbg)
    nc = _NC_CACHE[key]
    maps = _in_maps(inp)
    res = run_bass_kernel_spmd(nc, maps, core_ids=list(range(NCORES)))
    r = res.results
    st = lambda k: np.stack([np.asarray(r[b][k]) for b in range(NCORES)])
    outs = (
        st("y_prompt"),
        st("y_sample"),
        st("meta_k").reshape(1, NCORES, NMETA, 2, 64),
        st("meta_v").reshape(1, NCORES, NMETA, 2, 64),
        st("swa_k").reshape(1, NCORES, 128, 2, 64),
        st("swa_v").reshape(1, NCORES, 128, 2, 64),
        st("ret_state").reshape(1, NCORES, 4, 128, 128),
        st("swa_k_s").reshape(1, NCORES, DS, 2, 64),
        st("swa_v_s").reshape(1, NCORES, DS, 2, 64),
        st("ret_state_s").reshape(1, NCORES, 4, 128, 128),
    )
    if dbg:
        kernel.debug = [{k: np.asarray(v) for k, v in r[b].items() if k.startswith("d_")} for b in range(NCORES)]
    return tuple(np.ascontiguousarray(o.astype(np.float32)) for o in outs)
```

```python
import contextlib
import math
import os
import numpy as np
import concourse.bass as bass
import concourse.mybir as mybir
from concourse.bass_utils import run_bass_kernel_spmd

F32 = mybir.dt.float32
BF16 = mybir.dt.bfloat16
I32 = mybir.dt.int32
U32 = mybir.dt.uint32
ALU = mybir.AluOpType
AF = mybir.ActivationFunctionType
AX = mybir.AxisListType

D = 1024
S = 2048
DS = 64
NMETA = 16
NTILES = S // 128
PW = 2816
NE = 16384
ALPHA = 2.0 ** 0.25
EPS = 1e-5
NCORES = 8
NJ = 128


class T:
    def __init__(self, t, name):
        self.t = t
        self.name = name
        self.w = None
        self.r = {}
        self.dsem = None
        self.dcnt = 0
        self.psem = None
        self.pcnt = 0

    def __getitem__(self, k):
        return self.t[k]


class Prog:
    ENG = ('pe', 'act', 'dve', 'pool', 'sp')

    def __init__(self, nc):
        self.nc = nc
        self.es = contextlib.ExitStack()
        self.q = {e: [] for e in self.ENG}
        self.cnt = {e: 0 for e in self.ENG}
        self.seen = {e: {} for e in self.ENG}
        self.esem = {}
        for e in self.ENG:
            self.esem[e] = self.es.enter_context(nc.semaphore('es_' + e))
        self.nsem = 0
        self.stores = []

    def sb(self, name, shape, dt=F32):
        return T(self.es.enter_context(self.nc.sbuf_tensor('sb_' + name, list(shape), dt)), name)

    def ps(self, name, shape, dt=F32):
        return T(self.es.enter_context(self.nc.psum_tensor('ps_' + name, list(shape), dt)), name)

    def newsem(self, name):
        self.nsem += 1
        return self.es.enter_context(self.nc.semaphore('d_%s_%d' % (name, self.nsem)))

    def _waits(self, eng, reads, writes):
        deps = {}

        def add(d):
            if d is None:
                return
            sem, val, src = d
            if eng == 'pe' and src == 'pe':
                return
            k = id(sem)
            if k not in deps or deps[k][1] < val:
                deps[k] = (sem, val)

        for t in reads:
            add(t.w)
        for t in writes:
            add(t.w)
            for d in t.r.values():
                add(d)
        out = []
        for k, (sem, val) in deps.items():
            if self.seen[eng].get(k, 0) >= val:
                continue
            self.seen[eng][k] = val
            out.append((sem, val))
        return out

    def op(self, eng, fn, reads=(), writes=()):
        waits = self._waits(eng, reads, writes)
        self.cnt[eng] += 1
        me = (self.esem[eng], self.cnt[eng], eng)
        self.q[eng].append((waits, fn, (self.esem[eng], 1)))
        for t in reads:
            t.r[id(me[0])] = me
        for t in writes:
            t.w = me
            t.r = {}

    def dma(self, eng, fn, reads=(), writes=(), store_src=None):
        waits = self._waits(eng, reads, writes)
        owner = writes[0] if writes else store_src
        if eng == 'pool':
            if owner.psem is None:
                owner.psem = self.newsem('p' + owner.name)
            owner.pcnt += 16
            me = (owner.psem, owner.pcnt, 'dma')
        else:
            if owner.dsem is None:
                owner.dsem = self.newsem(owner.name)
            owner.dcnt += 16
            me = (owner.dsem, owner.dcnt, 'dma')
        self.q[eng].append((waits, fn, (me[0], 16)))
        for t in reads:
            t.r[id(me[0])] = me
        for t in writes:
            t.w = me
            t.r = {}
        if store_src is not None:
            self.stores.append(me)

    def finish(self):
        deps = {}
        for sem, val, _ in self.stores:
            k = id(sem)
            if k not in deps or deps[k][1] < val:
                deps[k] = (sem, val)
        for e in self.ENG:
            if e != 'sp' and self.cnt[e] > 0:
                deps[id(self.esem[e])] = (self.esem[e], self.cnt[e])
        self.q['sp'].append((list(deps.values()), None, None))

    def emit(self):
        q = self.q
        import bisect
        eng_of = {id(self.esem[e]): e for e in self.ENG}
        targets = {e: set() for e in self.ENG}
        for name in self.ENG:
            for waits, fn, inc in q[name]:
                for sem, val in waits:
                    if id(sem) in eng_of:
                        targets[eng_of[id(sem)]].add(val)
        tsorted = {e: sorted(targets[e]) for e in self.ENG}

        def real(sem, val):
            e_ = eng_of.get(id(sem))
            if e_ is None:
                return val
            return bisect.bisect_right(tsorted[e_], val)

        def replay(name, e):
            vidx = 0
            for waits, fn, inc in q[name]:
                for sem, val in waits:
                    e.wait_ge(sem, real(sem, val))
                if fn is None:
                    continue
                ins = fn(e)
                if inc is not None:
                    if id(inc[0]) in eng_of:
                        vidx += 1
                        if vidx in targets[name]:
                            ins.then_inc(inc[0], 1)
                    else:
                        ins.then_inc(inc[0], inc[1])

        with self.nc.Block() as block:
            @block.sync
            def _(e):
                replay('sp', e)

            @block.scalar
            def _(e):
                replay('act', e)

            @block.vector
            def _(e):
                replay('dve', e)

            @block.gpsimd
            def _(e):
                replay('pool', e)

            @block.tensor
            def _(e):
                replay('pe', e)
        self.es.close()


def _t5_bucket(rel):
    nb = 16
    max_exact = 8
    n = np.abs(rel)
    v = (np.log(np.maximum(n, max_exact).astype(np.float32) / np.float32(max_exact))
         / np.float32(math.log(128 / 8)) * np.float32(nb - max_exact))
    large = np.minimum(max_exact + v.astype(np.int32), nb - 1)
    return np.where(rel > 0, nb, 0) + np.where(n < max_exact, n, large)


def _host_consts():
    c = {}
    k = np.arange(128)[:, None]
    q = np.arange(128)[None, :]
    c['bk_prev'] = _t5_bucket(k - 128 - q)
    c['bk_cur'] = _t5_bucket(k - q)
    m = np.arange(NMETA)[:, None]
    c['bk_meta0'] = _t5_bucket(m - NMETA - q)
    ok_prev = (q < 64) | (k >= 64)
    ok_cur = (q >= 64) | (k < 64)
    neg = np.float32(-30000.0)
    c['mask_prev'] = np.where(ok_prev, 0.0, neg).astype(np.float32)
    c['mask_cur'] = np.where(ok_cur, 0.0, neg).astype(np.float32)
    half = 64
    inv = (np.float32(10000.0) ** (-np.arange(half, dtype=np.float32) / np.float32(half))).astype(np.float32)
    rot = np.zeros((18, 128, 128), np.float32)
    for ti in range(18):
        if ti < 16:
            pos = NMETA + ti * 128 + np.arange(128)
        elif ti == 16:
            pos = NMETA + S + np.arange(128)
        else:
            pos = np.arange(128)
        ang = pos.astype(np.float32)[:, None] * inv[None, :]
        rot[ti, :, :64] = np.cos(ang)
        rot[ti, :, 64:] = np.sin(ang)
    c['rot'] = rot
    gam = 1.0 - 2.0 ** (-5.0 - np.arange(4, dtype=np.float64))
    t = np.arange(128, dtype=np.float64)[:, None]
    dq = gam[None, :] ** (t + 1.0)
    dk = gam[None, :] ** (-(t + 1.0)) * (128.0 ** -0.5)
    c['dqk'] = np.concatenate([dq, dk], axis=1).astype(np.float32)
    gl = np.zeros((128, 3, 4), np.float32)
    for li, L in enumerate((128, 64, 16)):
        gl[:, li, :] = (gam ** L)[None, :]
    c['gl'] = gl.reshape(128, 12)
    c['rmask'] = (k <= q).astype(np.float32)
    c['iota16'] = np.tile(np.arange(16, dtype=np.float32)[None, :], (128, 1))
    return c


def build_program(dbg=False):
    nc = bass.Bass("TRN2", target_bir_lowering=False)

    def din(name, shape, dt=F32):
        return nc.dram_tensor(name, list(shape), dt, kind="ExternalInput").ap()

    def dout(name, shape, dt=F32):
        return nc.dram_tensor(name, list(shape), dt, kind="ExternalOutput").ap()

    xp_d = din("xp", [S, D])
    xs_d = din("xs", [DS, D])
    xm_d = din("xm", [NMETA, D])
    cmk_d = din("cmk", [NMETA, 128])
    cmv_d = din("cmv", [NMETA, 128])
    csk_d = din("csk", [128, 128])
    csv_d = din("csv", [128, 128])
    st0_d = din("st0", [4, 128, 128])
    lng_d = [din("ln%d_g" % i, [D]) for i in range(3)]
    lnb_d = [din("ln%d_b" % i, [D]) for i in range(3)]
    win_d = din("w_in", [D, PW])
    wout_d = din("w_out", [D, D])
    wq_d = din("wq", [D, D])
    skb_d = din("skb", [8, 128, 256])
    sinks_d = din("sinks", [8])
    gng_d = din("gng", [512])
    u_d = din("peer_u", [NE, D])
    v_d = din("peer_v", [NE, D])
    bprev_d = din("b_prev", [128, 8, 128])
    bcur_d = din("b_cur", [128, 8, 128])
    bm0_d = din("b_meta0", [NMETA, 8, 128])
    bmf_d = din("b_metaf", [NMETA, 8])
    mprev_d = din("mask_prev", [128, 128])
    mcur_d = din("mask_cur", [128, 128])
    rot_d = din("rot", [18, 128, 128])
    dqk_d = din("dqk", [128, 8])
    gl_d = din("gl", [128, 12])
    rmask_d = din("rmask", [128, 128])
    iota16_d = din("iota16", [128, 16])

    yp_d = dout("y_prompt", [S, D])
    ys_d = dout("y_sample", [DS, D])
    mk_d = dout("meta_k", [NMETA, 128])
    mv_d = dout("meta_v", [NMETA, 128])
    sk_d = dout("swa_k", [128, 128])
    sv_d = dout("swa_v", [128, 128])
    rs_d = dout("ret_state", [4, 128, 128])
    sks_d = dout("swa_k_s", [DS, 128])
    svs_d = dout("swa_v_s", [DS, 128])
    rss_d = dout("ret_state_s", [4, 128, 128])
    uvbf_d = nc.dram_tensor("uvbf_scratch", [NE, 2 * D], BF16, kind="Internal").ap()
    dbg_d = {}
    if dbg:
        for nm, w in (("d_h", 1024), ("d_p", 2816), ("d_mixed", 1024), ("d_h1", 1024), ("d_eidx", 128),
                      ("d_g", 128), ("d_dots", 128), ("d_peer", 1024)):
            dbg_d[nm] = dout(nm, [128, w])

    P = Prog(nc)
    op, dma = P.op, P.dma

    w_in = P.sb("w_in", [128, 8, PW], BF16)
    w_out = P.sb("w_out", [128, 8, D], BF16)
    w_q = P.sb("w_q", [128, 8, D], BF16)
    skb = P.sb("skb", [128, 8, 256], BF16)
    lng = [P.sb("lng%d" % i, [128, D], BF16) for i in range(3)]
    lnb = [P.sb("lnb%d" % i, [128, D], BF16) for i in range(3)]
    gng = P.sb("gng", [128, 512])
    esink = P.sb("esink", [128, 8])
    b_prev = P.sb("b_prev", [128, 8, 128], BF16)
    b_cur = P.sb("b_cur", [128, 8, 128], BF16)
    b_mf = P.sb("b_mf", [NMETA, 8])
    rot = P.sb("rot", [128, 128])
    dqk = P.sb("dqk", [128, 8])
    gl = P.sb("gl", [128, 12])
    rmask = P.sb("rmask", [128, 128])
    iota16 = P.sb("iota16", [128, 16])
    ident = P.sb("ident", [128, 128], BF16)

    F4a = P.sb("F4a", [128, D])
    H1 = [P.sb("H1_%d" % i, [128, D]) for i in range(2)]
    F4c = P.sb("F4c", [128, D])
    B2a = P.sb("B2a", [128, D], BF16)
    B2b = P.sb("B2b", [128, D], BF16)
    T8a = P.sb("T8a", [128, 8, 128], BF16)
    T8b = P.sb("T8b", [128, 8, 128], BF16)
    lst = P.sb("lst", [128, 2, 6])
    lmv = P.sb("lmv", [128, 2])
    lrs = P.sb("lrs", [128, 1])
    kvf = P.sb("kvf", [128, 256])
    qbf = P.sb("qbf", [128, 512], BF16)
    kbf = P.sb("kbf", [128, 128], BF16)
    KT = [P.sb("KT%d" % i, [128, 128], BF16) for i in range(2)]
    VX = [P.sb("VX%d" % i, [128, 2, 65], BF16) for i in range(2)]
    KTm = P.sb("KTm", [128, NMETA], BF16)
    VXm = P.sb("VXm", [NMETA, 2, 65], BF16)
    QT = P.sb("QT", [128, 4, 128], BF16)
    sS = P.sb("sS", [128, 512])
    PT1 = [P.sb("PT%d" % a, [128, 512], BF16) for a in range(3)]
    PT = [[PT1[a], PT1[a]] for a in range(3)]
    den = P.sb("den", [128, 8])
    tA = P.sb("tA", [128, 512])
    tB = P.sb("tB", [128, 512])
    rqkb = P.sb("rqkb", [128, D], BF16)
    rvb = P.sb("rvb", [128, 512], BF16)
    gate = P.sb("gate", [128, 512], BF16)
    SmT = P.sb("SmT", [128, 4, 128], BF16)
    state = P.sb("state", [128, 512])
    stbf = P.sb("stbf", [128, 512], BF16)
    tmpS = tA
    gst = P.sb("gst", [128, 4, 6])
    gmv = P.sb("gmv", [128, 4, 2])
    grs = P.sb("grs", [128, 4])
    yA = tB
    cand = F4a
    b_m0 = H1[1]
    sw = [P.sb("sw%d" % i, [128, 256]) for i in range(2)]
    tv = P.sb("tv", [128, 16, 16])
    tiu = P.sb("tiu", [128, 16, 16], U32)
    tif = P.sb("tif", [128, 16, 16])
    t2v = P.sb("t2v", [128, 8, 16])
    t2i = P.sb("t2i", [128, 8, 16], U32)
    k1u = P.sb("k1u", [128, 128], U32)
    k2u = P.sb("k2u", [128, 128], U32)
    k1f = P.sb("k1f", [128, 128])
    k2f = P.sb("k2f", [128, 128])
    i1s = P.sb("i1s", [128, 128])
    i2s = P.sb("i2s", [128, 128])
    eif = P.sb("eif", [128, 128])
    EI = [P.sb("eidx%d" % i, [128, 128], I32) for i in range(2)]
    GS = [P.sb("gsm%d" % i, [128, 128]) for i in range(2)]
    gsum = P.sb("gsum", [128, 8])
    NB = int(os.environ.get("K_NB", "12"))
    GRP = int(os.environ.get("K_GRP", "2"))
    uvb = [P.sb("uvb%d" % i, [128, 2 * D], BF16) for i in range(NB)]
    NDG = 6
    dg = [P.sb("dg%d" % i, [128, 128], BF16) for i in range(NDG)]
    DT = [P.sb("dt%d" % i, [128, GRP]) for i in range(4)]
    GA = [P.sb("gga%d" % i, [128, GRP]) for i in range(4)]
    GB = [P.sb("ggb%d" % i, [128, GRP]) for i in range(4)]
    GC = [P.sb("ggc%d" % i, [128, GRP]) for i in range(4)]
    WT = [P.sb("wt%d" % i, [128, GRP]) for i in range(4)]

    ptr = P.ps("ptr", [128, 8, 128], BF16)
    pA = P.ps("pA", [128, D])
    pV = P.ps("pV", [128, D])
    pS = P.ps("pS", [128, 512])
    pO = [P.ps("pO%d" % i, [128, 512]) for i in range(2)]

    def bc_part(ap, n=128):
        return ap.partition_broadcast(n)

    for i in range(3):
        dma('sp', lambda e, i=i: e.dma_start(out=F4c[:], in_=bc_part(lng_d[i])), writes=[F4c])
        op('dve', lambda e, i=i: e.tensor_scalar(out=lng[i][:], in0=F4c[:], scalar1=-1.0, scalar2=None, op0=ALU.add),
           reads=[F4c], writes=[lng[i]])
        dma('sp', lambda e, i=i: e.dma_start(out=F4a[:], in_=bc_part(lnb_d[i])), writes=[F4a])
        op('dve', lambda e, i=i: e.tensor_copy(out=lnb[i][:], in_=F4a[:]), reads=[F4a], writes=[lnb[i]])
    dma('sp', lambda e: e.dma_start(out=gng[:], in_=bc_part(gng_d)), writes=[gng])
    dma('sp', lambda e: e.dma_start(out=esink[:], in_=bc_part(sinks_d)), writes=[esink])
    op('act', lambda e: e.activation(out=esink[:], in_=esink[:], func=AF.Exp), reads=[esink], writes=[esink])
    dma('sp', lambda e: e.dma_start(out=dqk[:], in_=dqk_d), writes=[dqk])
    dma('sp', lambda e: e.dma_start(out=gl[:], in_=gl_d), writes=[gl])
    dma('sp', lambda e: e.dma_start(out=rmask[:], in_=rmask_d), writes=[rmask])
    dma('sp', lambda e: e.dma_start(out=iota16[:], in_=iota16_d), writes=[iota16])
    dma('sp', lambda e: e.dma_start(out=F4a[:].rearrange("p (h q) -> p h q", h=8), in_=bprev_d), writes=[F4a])
    dma('sp', lambda e: e.dma_start(out=F4c[:].rearrange("p (h q) -> p h q", h=8), in_=bcur_d), writes=[F4c])
    dma('sp', lambda e: e.dma_start(out=b_mf[:], in_=bmf_d), writes=[b_mf])
    dma('sp', lambda e: e.dma_start(out=tA[:, 0:128], in_=mprev_d), writes=[tA])
    dma('sp', lambda e: e.dma_start(out=tB[:, 0:128], in_=mcur_d), writes=[tB])
    op('dve', lambda e: e.tensor_tensor(out=b_prev[:], in0=F4a[:].rearrange("p (h q) -> p h q", h=8),
                                        in1=tA[:, 0:128].unsqueeze(1).to_broadcast([128, 8, 128]), op=ALU.add),
       reads=[F4a, tA], writes=[b_prev])
    op('dve', lambda e: e.tensor_tensor(out=b_cur[:], in0=F4c[:].rearrange("p (h q) -> p h q", h=8),
                                        in1=tB[:, 0:128].unsqueeze(1).to_broadcast([128, 8, 128]), op=ALU.add),
       reads=[F4c, tB], writes=[b_cur])
    iota_i = EI[0]
    identf = sS
    op('pool', lambda e: e.iota(iota_i[:], pattern=[[1, 128]], base=0, channel_multiplier=-1), writes=[iota_i])
    op('dve', lambda e: e.tensor_copy(out=identf[:, 0:128], in_=iota_i[:]), reads=[iota_i], writes=[identf])
    op('dve', lambda e: e.tensor_scalar(out=ident[:], in0=identf[:, 0:128], scalar1=0.0, scalar2=None, op0=ALU.is_equal),
       reads=[identf], writes=[ident])
    for t_ in VX + [VXm]:
        op('dve', lambda e, t_=t_: e.memset(t_[:], 1.0), writes=[t_])
    for t_ in EI:
        op('dve', lambda e, t_=t_: e.memset(t_[:], 0), writes=[t_])
    op('dve', lambda e: e.memset(state[:], 0.0), writes=[state])
    op('dve', lambda e: e.memset(stbf[:], 0.0), writes=[stbf])

    stg = [F4a, H1[0], H1[1], F4c]
    nst = [0]

    def load_cast(dst_fn, src_ap, ncols, dst_tile, permq=False):
        s_ = stg[nst[0] % len(stg)]
        eng = 'act' if (nst[0] % 2 == 0) else 'dve'
        nst[0] += 1
        dma('sp', lambda e: e.dma_start(out=s_[:, 0:ncols], in_=src_ap), writes=[s_])
        if permq:
            def f(e):
                return e.tensor_copy(out=dst_fn(0, 512).rearrange("p (h g d) -> p h g d", h=4, g=2),
                                     in_=s_[:, 0:512].rearrange("p (g h d) -> p h g d", g=2, h=4))
            op('dve', f, reads=[s_], writes=[dst_tile])
            if eng == 'act':
                op('act', lambda e: e.activation(out=dst_fn(512, ncols), in_=s_[:, 512:ncols], func=AF.Copy),
                   reads=[s_], writes=[dst_tile])
            else:
                op('dve', lambda e: e.tensor_copy(out=dst_fn(512, ncols), in_=s_[:, 512:ncols]),
                   reads=[s_], writes=[dst_tile])
        else:
            if eng == 'act':
                op('act', lambda e: e.activation(out=dst_fn(0, ncols), in_=s_[:, 0:ncols], func=AF.Copy),
                   reads=[s_], writes=[dst_tile])
            else:
                op('dve', lambda e: e.tensor_copy(out=dst_fn(0, ncols), in_=s_[:, 0:ncols]),
                   reads=[s_], writes=[dst_tile])

    for kc in range(8):
        for (c0, c1) in ((0, 1024), (1024, 2048), (2048, PW)):
            load_cast(lambda a, b, kc=kc, c0=c0: w_in[:, kc, c0 + a:c0 + b],
                      win_d[kc * 128:(kc + 1) * 128, c0:c1], c1 - c0, w_in, permq=(c0 == 0))
    for kc in range(8):
        load_cast(lambda a, b, kc=kc: w_out[:, kc, a:b], wout_d[kc * 128:(kc + 1) * 128, :], D, w_out)
    for kc in range(8):
        load_cast(lambda a, b, kc=kc: w_q[:, kc, a:b], wq_d[kc * 128:(kc + 1) * 128, :], D, w_q)
    for hp in range(2):
        s_ = stg[hp]
        dma('sp', lambda e, hp=hp, s_=s_: e.dma_start(out=s_[:, 0:1024].rearrange("p (k n) -> p k n", k=4),
                                                      in_=skb_d[hp * 4:(hp + 1) * 4].rearrange("k p n -> p k n")),
            writes=[s_])
        op('dve', lambda e, hp=hp, s_=s_: e.tensor_copy(out=skb[:, hp * 4:(hp + 1) * 4, :],
                                                        in_=s_[:, 0:1024].rearrange("p (k n) -> p k n", k=4)),
           reads=[s_], writes=[skb])

    UVBF = T(None, "uvbf")
    chunks = []
    for (src_d, off) in ((u_d, 0), (v_d, D)):
        src_v = src_d.rearrange("(p r) d -> p r d", p=128)
        dst_v = uvbf_d.rearrange("(p r) d -> p r d", p=128)[:, :, off:off + D]
        for c in range(128):
            chunks.append((src_v, dst_v, c))
    NSTG = NB // 2
    DEPTH = NSTG - 1

    def conv_load(i):
        src_v, dst_v, c = chunks[i]
        s_ = uvb[i % NSTG]
        dma('sp', lambda e: e.dma_start(out=s_.t.bitcast(F32)[:, :], in_=src_v[:, c, :]), writes=[s_])

    def conv_cast_store(i):
        src_v, dst_v, c = chunks[i]
        s_ = uvb[i % NSTG]
        b_ = uvb[NSTG + i % NSTG]
        eng = ('act', 'dve', 'pool')[i % 3]
        if eng == 'act':
            op('act', lambda e: e.activation(out=b_[:, 0:D], in_=s_.t.bitcast(F32)[:, :], func=AF.Copy), reads=[s_], writes=[b_])
        else:
            op(eng, lambda e: e.tensor_copy(out=b_[:, 0:D], in_=s_.t.bitcast(F32)[:, :]), reads=[s_], writes=[b_])
        dma('sp', lambda e: e.dma_start(out=dst_v[:, c, :], in_=b_[:, 0:D]), reads=[b_], writes=[UVBF])

    def conv_gen():
        for i in range(len(chunks) + DEPTH):
            if i < len(chunks):
                conv_load(i)
            if i >= DEPTH:
                conv_cast_store(i - DEPTH)
            yield

    dma('sp', lambda e: e.dma_start(out=b_m0[:NMETA, :].rearrange("p (h q) -> p h q", h=8), in_=bm0_d), writes=[b_m0])

    def layernorm(src, dst, gi, nt):
        for c in range(2):
            op('dve', lambda e, c=c: e.bn_stats(out=lst[:nt, c, :], in_=src[:nt, c * 512:(c + 1) * 512]),
               reads=[src], writes=[lst])
        op('dve', lambda e: e.bn_aggr(out=lmv[:nt, :], in_=lst[:nt].rearrange("p a b -> p (a b)")),
           reads=[lst], writes=[lmv])
        op('act', lambda e: e.activation(out=lrs[:nt, :], in_=lmv[:nt, 1:2], func=AF.Sqrt, bias=EPS, scale=1.0),
           reads=[lmv], writes=[lrs])
        op('dve', lambda e: e.reciprocal(out=lrs[:nt, :], in_=lrs[:nt, :]), reads=[lrs], writes=[lrs])
        op('dve', lambda e: e.tensor_scalar(out=dst[:nt, :], in0=src[:nt, :], scalar1=lmv[:nt, 0:1],
                                            scalar2=lrs[:nt, 0:1], op0=ALU.subtract, op1=ALU.mult),
           reads=[src, lmv, lrs], writes=[dst])
        op('dve', lambda e: e.scalar_tensor_tensor(out=dst[:nt, :], in0=lng[gi][:nt, :], scalar=1.0, in1=dst[:nt, :],
                                                   op0=ALU.add, op1=ALU.mult), reads=[dst, lng[gi]], writes=[dst])
        op('dve', lambda e: e.tensor_tensor(out=dst[:nt, :], in0=dst[:nt, :], in1=lnb[gi][:nt, :], op=ALU.add),
           reads=[dst, lnb[gi]], writes=[dst])

    def transpose_to(src_bf, dstT, nt, nblk=8, evac='dve'):
        for j in range(nblk):
            op('pe', lambda e, j=j: e.transpose(out=ptr[:, j, :nt], in_=src_bf[:nt, j * 128:(j + 1) * 128],
                                                identity=ident[:nt, :nt]),
               reads=[src_bf, ident], writes=[ptr])
        if evac == 'dve':
            op('dve', lambda e: e.tensor_copy(out=dstT[:, 0:nblk, :nt], in_=ptr[:, 0:nblk, :nt]),
               reads=[ptr], writes=[dstT])
        else:
            op('act', lambda e: e.activation(out=dstT[:, 0:nblk, :nt], in_=ptr[:, 0:nblk, :nt], func=AF.Copy),
               reads=[ptr], writes=[dstT])

    def mm_full(out_ps, lhsT8, nt, rhs_w, c0, c1):
        c = c0
        while c < c1:
            ce = min(c + 512, c1)
            for kc in range(8):
                op('pe', lambda e, kc=kc, c=c, ce=ce: e.matmul(out_ps[:nt, c - c0:ce - c0], lhsT=lhsT8[:, kc, :nt],
                                                              rhs=rhs_w[:, kc, c:ce], start=(kc == 0), stop=(kc == 7)),
                   reads=[lhsT8, rhs_w], writes=[out_ps])
            c = ce

    def dbg_dump(name, tile_, ncols, nt):
        if dbg and name in dbg_d:
            dma('sp', lambda e: e.dma_start(out=dbg_d[name][:nt, 0:ncols], in_=tile_[:nt, 0:ncols]),
                reads=[tile_], store_src=tile_)

    def tile_params(kind, ti):
        if kind == 'meta':
            return NMETA, 17, 2, xm_d
        if kind == 'prompt':
            return 128, ti, 0, xp_d[ti * 128:(ti + 1) * 128, :]
        return DS, 16, 1, xs_d

    def sample_setup():
        dma('sp', lambda e: e.dma_start(out=rs_d.rearrange("h d e -> d h e"), in_=state[:].rearrange("p (h e) -> p h e", h=4)),
            reads=[state], store_src=state)
        dma('sp', lambda e: e.dma_start(out=kvf[:, 0:128], in_=csk_d), writes=[kvf])
        dma('sp', lambda e: e.dma_start(out=kvf[:, 128:256], in_=csv_d), writes=[kvf])
        op('act', lambda e: e.activation(out=kbf[:, :], in_=kvf[:, 0:128], func=AF.Copy), reads=[kvf], writes=[kbf])
        op('dve', lambda e: e.tensor_copy(out=VX[0][:, :, 0:64], in_=kvf[:, 128:256].rearrange("p (g d) -> p g d", g=2)),
           reads=[kvf], writes=[VX[0]])
        op('pe', lambda e: e.transpose(out=ptr[:, 0, :], in_=kbf[:, :], identity=ident[:, :]), reads=[kbf, ident], writes=[ptr])
        op('dve', lambda e: e.tensor_copy(out=KT[0][:, :], in_=ptr[:, 0, :]), reads=[ptr], writes=[KT[0]])
        dma('sp', lambda e: e.dma_start(out=kvf[:NMETA, 0:128], in_=cmk_d), writes=[kvf])
        dma('sp', lambda e: e.dma_start(out=kvf[:NMETA, 128:256], in_=cmv_d), writes=[kvf])
        op('act', lambda e: e.activation(out=kbf[:NMETA, :], in_=kvf[:NMETA, 0:128], func=AF.Copy), reads=[kvf], writes=[kbf])
        op('dve', lambda e: e.tensor_copy(out=VXm[:NMETA, :, 0:64], in_=kvf[:NMETA, 128:256].rearrange("p (g d) -> p g d", g=2)),
           reads=[kvf], writes=[VXm])
        op('pe', lambda e: e.transpose(out=ptr[:, 0, :NMETA], in_=kbf[:NMETA, :], identity=ident[:NMETA, :NMETA]),
           reads=[kbf, ident], writes=[ptr])
        op('dve', lambda e: e.tensor_copy(out=KTm[:, :NMETA], in_=ptr[:, 0, :NMETA]), reads=[ptr], writes=[KTm])
        dma('sp', lambda e: e.dma_start(out=state[:].rearrange("p (h e) -> p h e", h=4), in_=st0_d.rearrange("h d e -> d h e")),
            writes=[state])
        op('act', lambda e: e.activation(out=stbf[:], in_=state[:], func=AF.Copy), reads=[state], writes=[stbf])

    def front(kind, ti, k):
        nt, rti, gli, x_src = tile_params(kind, ti)
        gsm = GS[k % 2]
        if kind == 'sample':
            sample_setup()
            yield
        h_t = F4a
        h1 = H1[k % 2]
        eidx = EI[k % 2]
        dma('sp', lambda e: e.dma_start(out=h_t[:nt, :], in_=x_src), writes=[h_t])
        dma('sp', lambda e: e.dma_start(out=rot[:, :], in_=rot_d[rti]), writes=[rot])
        layernorm(h_t, h_t, 0, nt)
        yield
        op('act', lambda e: e.activation(out=B2a[:nt, :], in_=h_t[:nt, :], func=AF.Copy), reads=[h_t], writes=[B2a])
        transpose_to(B2a, T8a, nt, 8, 'dve')
        hT = T8a
        yield
        mm_full(pA, hT, nt, w_in, 0, 768)
        yield
        op('act', lambda e: e.activation(out=qbf[:nt, :], in_=pA[:nt, 0:512], func=AF.Copy), reads=[pA], writes=[qbf])
        op('dve', lambda e: e.tensor_copy(out=kvf[:nt, :], in_=pA[:nt, 512:768]), reads=[pA], writes=[kvf])
        if kind == 'meta':
            dma('sp', lambda e: e.dma_start(out=mk_d, in_=kvf[:nt, 0:128]), reads=[kvf], store_src=kvf)
            dma('sp', lambda e: e.dma_start(out=mv_d, in_=kvf[:nt, 128:256]), reads=[kvf], store_src=kvf)
        elif kind == 'prompt' and ti == NTILES - 1:
            dma('sp', lambda e: e.dma_start(out=sk_d, in_=kvf[:nt, 0:128]), reads=[kvf], store_src=kvf)
            dma('sp', lambda e: e.dma_start(out=sv_d, in_=kvf[:nt, 128:256]), reads=[kvf], store_src=kvf)
        elif kind == 'sample':
            dma('sp', lambda e: e.dma_start(out=sks_d, in_=kvf[:nt, 0:128]), reads=[kvf], store_src=kvf)
            dma('sp', lambda e: e.dma_start(out=svs_d, in_=kvf[:nt, 128:256]), reads=[kvf], store_src=kvf)
        if kind == 'meta':
            kt_cur, vx_cur = KTm, VXm
        elif kind == 'prompt':
            kt_cur, vx_cur = KT[ti % 2], VX[ti % 2]
        else:
            kt_cur, vx_cur = KT[1], VX[1]
        op('act', lambda e: e.activation(out=kbf[:nt, :], in_=kvf[:nt, 0:128], func=AF.Copy), reads=[kvf], writes=[kbf])
        op('dve', lambda e: e.tensor_copy(out=vx_cur[:nt, :, 0:64],
                                          in_=kvf[:nt, 128:256].rearrange("p (g d) -> p g d", g=2)),
           reads=[kvf], writes=[vx_cur])
        op('pe', lambda e: e.transpose(out=ptr[:, 0, :nt], in_=kbf[:nt, :], identity=ident[:nt, :nt]),
           reads=[kbf, ident], writes=[ptr])
        op('dve', lambda e: e.tensor_copy(out=kt_cur[:, :nt], in_=ptr[:, 0, :nt]), reads=[ptr], writes=[kt_cur])
        yield
        mm_full(pA, hT, nt, w_in, 768, 1792)
        yield
        rqk = F4c
        cos_b = rot[:nt, 0:64].unsqueeze(1).to_broadcast([nt, 8, 64])
        sin_b = rot[:nt, 64:128].unsqueeze(1).to_broadcast([nt, 8, 64])
        pv = pA[:nt, :].rearrange("p (a t d) -> p a t d", a=8, t=2)
        x1 = pv[:, :, 0, :]
        x2 = pv[:, :, 1, :]
        tAv = tA[:nt, :].rearrange("p (a d) -> p a d", a=8)
        tBv = tB[:nt, :].rearrange("p (a d) -> p a d", a=8)
        rv4 = rqk[:nt, :].rearrange("p (a t d) -> p a t d", a=8, t=2)
        op('dve', lambda e: e.tensor_tensor(out=tAv, in0=x1, in1=cos_b, op=ALU.mult), reads=[pA, rot], writes=[tA])
        op('dve', lambda e: e.tensor_tensor(out=tBv, in0=x2, in1=sin_b, op=ALU.mult), reads=[pA, rot], writes=[tB])
        op('dve', lambda e: e.tensor_tensor(out=rv4[:, :, 0, :], in0=tAv, in1=tBv, op=ALU.subtract),
           reads=[tA, tB], writes=[rqk])
        op('dve', lambda e: e.tensor_tensor(out=tAv, in0=x1, in1=sin_b, op=ALU.mult), reads=[pA, rot], writes=[tA])
        op('dve', lambda e: e.tensor_tensor(out=tBv, in0=x2, in1=cos_b, op=ALU.mult), reads=[pA, rot], writes=[tB])
        op('dve', lambda e: e.tensor_tensor(out=rv4[:, :, 1, :], in0=tAv, in1=tBv, op=ALU.add),
           reads=[tA, tB], writes=[rqk])
        op('dve', lambda e: e.tensor_tensor(out=rqkb[:nt, :].rearrange("p (a d) -> p a d", a=8),
                                            in0=rqk[:nt, :].rearrange("p (a d) -> p a d", a=8),
                                            in1=dqk[:nt, :].unsqueeze(2).to_broadcast([nt, 8, 128]), op=ALU.mult),
           reads=[rqk, dqk], writes=[rqkb])
        yield
        mm_full(pA, hT, nt, w_in, 1792, PW)
        op('act', lambda e: e.activation(out=rvb[:nt, :], in_=pA[:nt, 0:512], func=AF.Copy), reads=[pA], writes=[rvb])
        if kind != 'meta':
            op('act', lambda e: e.activation(out=gate[:nt, :], in_=pA[:nt, 512:1024], func=AF.Silu),
               reads=[pA], writes=[gate])
        yield
        mixed = B2b
        if kind != 'meta':
            transpose_to(qbf, QT, nt, 4, 'act')
            if kind == 'prompt':
                groups = []
                if ti > 0:
                    groups.append((KT[(ti - 1) % 2], VX[(ti - 1) % 2], 128, b_prev, 0))
                groups.append((kt_cur, vx_cur, nt, b_cur, 1))
                groups.append((KTm, VXm, NMETA, b_m0 if ti == 0 else b_mf, 2))
            else:
                groups = [(KT[0], VX[0], 128, b_prev, 0), (kt_cur, vx_cur, nt, b_cur, 1), (KTm, VXm, NMETA, b_mf, 2)]
            for g in range(2):
                pr = slice(g * 64, (g + 1) * 64)
                for (kt_, vx_, nk, btab, a) in groups:
                    op('pe', lambda e, kt_=kt_, nk=nk, pr=pr: e.matmul(
                        pS[:nk, 0:4 * nt].rearrange("p (h q) -> p h q", h=4), lhsT=kt_[pr, :nk], rhs=QT[pr, :, :nt],
                        start=True, stop=True), reads=[kt_, QT], writes=[pS])
                    if btab is b_mf:
                        bias_ap = btab[:nk, g * 4:(g + 1) * 4].unsqueeze(2).to_broadcast([nk, 4, nt])
                    elif btab is b_m0:
                        bias_ap = b_m0[:nk, :].rearrange("p (h q) -> p h q", h=8)[:, g * 4:(g + 1) * 4, :nt]
                    else:
                        bias_ap = btab[:nk, g * 4:(g + 1) * 4, :nt]
                    op('dve', lambda e, nk=nk, bias_ap=bias_ap: e.scalar_tensor_tensor(
                        out=sS[:nk, 0:4 * nt].rearrange("p (h q) -> p h q", h=4),
                        in0=pS[:nk, 0:4 * nt].rearrange("p (h q) -> p h q", h=4), scalar=0.125,
                        in1=bias_ap, op0=ALU.mult, op1=ALU.add),
                        reads=[pS, btab], writes=[sS])
                    op('act', lambda e, nk=nk, a=a, g=g: e.activation(out=PT[a][g][:nk, 0:4 * nt], in_=sS[:nk, 0:4 * nt],
                                                                       func=AF.Exp), reads=[sS], writes=[PT[a][g]])
                yield
                for hh in range(4):
                    for gi_, (kt_, vx_, nk, btab, a) in enumerate(groups):
                        op('pe', lambda e, hh=hh, vx_=vx_, nk=nk, a=a, g=g, gi_=gi_: e.matmul(
                            pO[g][:nt, hh * 65:(hh + 1) * 65], lhsT=PT[a][g][:nk, hh * nt:(hh + 1) * nt],
                            rhs=vx_[:nk, g, :], start=(gi_ == 0), stop=(gi_ == len(groups) - 1)),
                            reads=[PT[a][g], vx_], writes=[pO[g]])
                pov = pO[g][:nt, 0:260].rearrange("p (h d) -> p h d", h=4)
                op('dve', lambda e, pov=pov, g=g: e.tensor_tensor(out=den[:nt, g * 4:(g + 1) * 4].unsqueeze(2),
                                                                    in0=pov[:, :, 64:65],
                                                                    in1=esink[:nt, g * 4:(g + 1) * 4].unsqueeze(2), op=ALU.add),
                   reads=[pO[g], esink], writes=[den])
                op('dve', lambda e, g=g: e.reciprocal(out=den[:nt, g * 4:(g + 1) * 4], in_=den[:nt, g * 4:(g + 1) * 4]),
                   reads=[den], writes=[den])
                op('dve', lambda e, pov=pov, g=g: e.tensor_tensor(
                    out=mixed[:nt, g * 256:(g + 1) * 256].rearrange("p (h d) -> p h d", h=4), in0=pov[:, :, 0:64],
                    in1=den[:nt, g * 4:(g + 1) * 4].unsqueeze(2).to_broadcast([nt, 4, 64]), op=ALU.mult),
                   reads=[pO[g], den], writes=[mixed])
                yield

        transpose_to(rqkb, T8b, nt, 8, 'act')
        RT = T8b
        if kind != 'meta':
            for h in range(4):
                op('pe', lambda e, h=h: e.matmul(pS[:nt, h * 128:h * 128 + nt], lhsT=RT[:, 4 + h, :nt], rhs=RT[:, h, :nt],
                                                 start=True, stop=True), reads=[RT], writes=[pS])
            op('dve', lambda e: e.tensor_tensor(out=SmT[:nt, :, :nt],
                                                in0=pS[:nt, :].rearrange("p (h q) -> p h q", h=4)[:, :, :nt],
                                                in1=rmask[:nt, :nt].unsqueeze(1).to_broadcast([nt, 4, nt]), op=ALU.mult),
               reads=[pS, rmask], writes=[SmT])
            for h in range(4):
                op('pe', lambda e, h=h: e.matmul(pO[0][:nt, h * 128:(h + 1) * 128], lhsT=SmT[:nt, h, :nt],
                                                 rhs=rvb[:nt, h * 128:(h + 1) * 128], start=True, stop=False),
                   reads=[SmT, rvb], writes=[pO[0]])
                op('pe', lambda e, h=h: e.matmul(pO[0][:nt, h * 128:(h + 1) * 128], lhsT=RT[:, h, :nt],
                                                 rhs=stbf[:, h * 128:(h + 1) * 128], start=False, stop=True),
                   reads=[RT, stbf], writes=[pO[0]])
        yield
        for h in range(4):
            op('pe', lambda e, h=h: e.matmul(pO[1][:, h * 128:(h + 1) * 128], lhsT=rqkb[:nt, 512 + h * 128:512 + (h + 1) * 128],
                                             rhs=rvb[:nt, h * 128:(h + 1) * 128], start=True, stop=True),
               reads=[rqkb, rvb], writes=[pO[1]])
        op('dve', lambda e: e.tensor_tensor(out=tmpS[:], in0=pO[1][:], in1=state[:], op=ALU.add),
           reads=[pO[1], state], writes=[tmpS])
        op('dve', lambda e: e.tensor_tensor(out=state[:].rearrange("p (h e) -> p h e", h=4),
                                            in0=tmpS[:].rearrange("p (h e) -> p h e", h=4),
                                            in1=gl[:, gli * 4:(gli + 1) * 4].unsqueeze(2).to_broadcast([128, 4, 128]),
                                            op=ALU.mult), reads=[tmpS, gl], writes=[state])
        op('act', lambda e: e.activation(out=stbf[:], in_=state[:], func=AF.Copy), reads=[state], writes=[stbf])
        yield
        if kind == 'meta':
            return

        for h in range(4):
            op('dve', lambda e, h=h: e.bn_stats(out=gst[:nt, h, :], in_=pO[0][:nt, h * 128:(h + 1) * 128]),
               reads=[pO[0]], writes=[gst])
        for h in range(4):
            op('dve', lambda e, h=h: e.bn_aggr(out=gmv[:nt, h, :], in_=gst[:nt, h, :]), reads=[gst], writes=[gmv])
        op('act', lambda e: e.activation(out=grs[:nt, :].unsqueeze(2), in_=gmv[:nt, :, 1:2], func=AF.Sqrt, bias=EPS, scale=1.0),
           reads=[gmv], writes=[grs])
        op('dve', lambda e: e.reciprocal(out=grs[:nt, :], in_=grs[:nt, :]), reads=[grs], writes=[grs])
        yv = yA[:nt, :].rearrange("p (h d) -> p h d", h=4)
        op('dve', lambda e: e.tensor_tensor(out=yv, in0=pO[0][:nt, :].rearrange("p (h d) -> p h d", h=4),
                                            in1=gmv[:nt, :, 0:1].to_broadcast([nt, 4, 128]), op=ALU.subtract),
           reads=[pO[0], gmv], writes=[yA])
        op('dve', lambda e: e.tensor_tensor(out=yv, in0=yv, in1=grs[:nt, :].unsqueeze(2).to_broadcast([nt, 4, 128]),
                                            op=ALU.mult), reads=[yA, grs], writes=[yA])
        op('dve', lambda e: e.tensor_tensor(out=yA[:nt, :], in0=yA[:nt, :], in1=gng[:nt, :], op=ALU.mult),
           reads=[yA, gng], writes=[yA])
        op('dve', lambda e: e.tensor_tensor(out=mixed[:nt, 512:1024], in0=yA[:nt, :], in1=gate[:nt, :], op=ALU.mult),
           reads=[yA, gate], writes=[mixed])
        yield

        transpose_to(mixed, T8a, nt, 8, 'dve')
        mm_full(pA, T8a, nt, w_out, 0, D)
        yield
        op('dve', lambda e: e.scalar_tensor_tensor(out=h1[:nt, :], in0=h_t[:nt, :], scalar=ALPHA, in1=pA[:nt, :],
                                                   op0=ALU.mult, op1=ALU.add), reads=[h_t, pA], writes=[h1])
        layernorm(h1, h1, 1, nt)
        yield

        op('act', lambda e: e.activation(out=B2a[:nt, :], in_=h1[:nt, :], func=AF.Copy), reads=[h1], writes=[B2a])
        transpose_to(B2a, T8a, nt, 8, 'dve')
        mm_full(pA, T8a, nt, w_q, 0, D)
        yield
        op('act', lambda e: e.activation(out=B2b[:nt, :], in_=pA[:nt, :], func=AF.Copy), reads=[pA], writes=[B2b])
        transpose_to(B2b, T8b, nt, 8, 'act')
        s_sb = F4c
        for hf in range(2):
            for pl in range(4):
                p_ = hf * 4 + pl
                op('pe', lambda e, p_=p_, pl=pl: e.matmul(pA[:nt, pl * 256:(pl + 1) * 256], lhsT=T8b[:, p_, :nt],
                                                          rhs=skb[:, p_, :], start=True, stop=True),
                   reads=[T8b, skb], writes=[pA])
            if hf == 0:
                op('act', lambda e: e.activation(out=s_sb[:nt, :], in_=pA[:nt, :], func=AF.Copy), reads=[pA], writes=[s_sb])
            else:
                op('dve', lambda e: e.tensor_copy(out=s_sb[:nt, :], in_=pA[:nt, :]), reads=[pA], writes=[s_sb])
            for gl_ in range(8):
                gi_ = hf * 8 + gl_
                sg = slice(gl_ * 128, (gl_ + 1) * 128)
                w_ = sw[gi_ % 2]
                op('dve', lambda e, gi_=gi_, sg=sg: e.max(out=tv[:nt, gi_, 0:8], in_=s_sb[:nt, sg]), reads=[s_sb], writes=[tv])
                op('dve', lambda e, gi_=gi_, sg=sg: e.max_index(out=tiu[:nt, gi_, 0:8], in_max=tv[:nt, gi_, 0:8],
                                                                in_values=s_sb[:nt, sg]), reads=[s_sb, tv], writes=[tiu])
                op('dve', lambda e, gi_=gi_, sg=sg, w_=w_: e.match_replace(out=w_[:nt, 0:128], in_to_replace=tv[:nt, gi_, 0:8],
                                                                           in_values=s_sb[:nt, sg], imm_value=-1e30),
                   reads=[s_sb, tv], writes=[w_])
                op('dve', lambda e, gi_=gi_, w_=w_: e.max(out=tv[:nt, gi_, 8:16], in_=w_[:nt, 0:128]), reads=[w_], writes=[tv])
                op('dve', lambda e, gi_=gi_, w_=w_: e.max_index(out=tiu[:nt, gi_, 8:16], in_max=tv[:nt, gi_, 8:16],
                                                                in_values=w_[:nt, 0:128]), reads=[w_, tv], writes=[tiu])
                if gl_ % 2 == 1:
                    yield
        op('dve', lambda e: e.tensor_copy(out=tif[:nt], in_=tiu[:nt]), reads=[tiu], writes=[tif])
        tv4 = tv[:nt].rearrange("p (a c) k -> p a c k", c=2)
        tif4 = tif[:nt].rearrange("p (a c) k -> p a c k", c=2)
        cv = cand[:nt, :].rearrange("p (a k m) -> p a k m", a=4, k=16)
        for hf in range(2):
            hs = slice(hf * 4, (hf + 1) * 4)
            op('dve', lambda e, hs=hs: e.tensor_tensor(out=cv, in0=tv4[:, hs, 0, :].unsqueeze(3).to_broadcast([nt, 4, 16, 16]),
                                                       in1=tv4[:, hs, 1, :].unsqueeze(2).to_broadcast([nt, 4, 16, 16]), op=ALU.add),
               reads=[tv], writes=[cand])
            for pl in range(4):
                p_ = hf * 4 + pl
                sg = slice(pl * 256, (pl + 1) * 256)
                w_ = sw[p_ % 2]
                op('dve', lambda e, p_=p_, sg=sg: e.max(out=t2v[:nt, p_, 0:8], in_=cand[:nt, sg]), reads=[cand], writes=[t2v])
                op('dve', lambda e, p_=p_, sg=sg: e.max_index(out=t2i[:nt, p_, 0:8], in_max=t2v[:nt, p_, 0:8],
                                                              in_values=cand[:nt, sg]), reads=[cand, t2v], writes=[t2i])
                op('dve', lambda e, p_=p_, sg=sg, w_=w_: e.match_replace(out=w_[:nt, :], in_to_replace=t2v[:nt, p_, 0:8],
                                                                         in_values=cand[:nt, sg], imm_value=-1e30),
                   reads=[cand, t2v], writes=[w_])
                op('dve', lambda e, p_=p_, w_=w_: e.max(out=t2v[:nt, p_, 8:16], in_=w_[:nt, :]), reads=[w_], writes=[t2v])
                op('dve', lambda e, p_=p_, w_=w_: e.max_index(out=t2i[:nt, p_, 8:16], in_max=t2v[:nt, p_, 8:16],
                                                              in_values=w_[:nt, :]), reads=[w_, t2v], writes=[t2i])
                if pl % 2 == 1:
                    yield
        t2if = t2i[:nt].rearrange("p a k -> p (a k)")
        op('dve', lambda e: e.tensor_single_scalar(out=k1u[:nt, :], in_=t2if, scalar=4, op=ALU.logical_shift_right),
           reads=[t2i], writes=[k1u])
        op('dve', lambda e: e.tensor_single_scalar(out=k2u[:nt, :], in_=t2if, scalar=15, op=ALU.bitwise_and),
           reads=[t2i], writes=[k2u])
        op('dve', lambda e: e.tensor_copy(out=k1f[:nt, :], in_=k1u[:nt, :]), reads=[k1u], writes=[k1f])
        op('dve', lambda e: e.tensor_copy(out=k2f[:nt, :], in_=k2u[:nt, :]), reads=[k2u], writes=[k2f])
        ohv = s_sb[:nt, :].rearrange("p (a k m) -> p a k m", a=4, k=16)
        iob = iota16[:nt, :].unsqueeze(1).unsqueeze(1).to_broadcast([nt, 4, 16, 16])
        for (kf, c_, dst_) in ((k1f, 0, i1s), (k2f, 1, i2s)):
            for hf in range(2):
                hs = slice(hf * 4, (hf + 1) * 4)
                js = slice(hf * 64, (hf + 1) * 64)
                op('dve', lambda e, kf=kf, js=js: e.tensor_tensor(
                    out=ohv, in0=iob,
                    in1=kf[:nt, js].rearrange("p (a k) -> p a k", a=4).unsqueeze(3).to_broadcast([nt, 4, 16, 16]),
                    op=ALU.is_equal), reads=[iota16, kf], writes=[s_sb])
                op('dve', lambda e, c_=c_, hs=hs: e.tensor_tensor(
                    out=ohv, in0=ohv, in1=tif4[:, hs, c_, :].unsqueeze(2).to_broadcast([nt, 4, 16, 16]), op=ALU.mult),
                   reads=[s_sb, tif], writes=[s_sb])
                op('dve', lambda e, dst_=dst_, js=js: e.tensor_reduce(out=dst_[:nt, js].rearrange("p (a k) -> p a k", a=4),
                                                                     in_=ohv, axis=AX.X, op=ALU.add),
                   reads=[s_sb], writes=[dst_])
                yield
        op('dve', lambda e: e.scalar_tensor_tensor(out=eif[:nt, :], in0=i1s[:nt, :], scalar=128.0, in1=i2s[:nt, :],
                                                   op0=ALU.mult, op1=ALU.add), reads=[i1s, i2s], writes=[eif])
        op('dve', lambda e: e.tensor_copy(out=eidx[:nt, :], in_=eif[:nt, :]), reads=[eif], writes=[eidx])
        gv = gsm[:nt, :].rearrange("p (a k) -> p a k", a=8)
        op('dve', lambda e: e.tensor_tensor(out=gv, in0=t2v[:nt], in1=t2v[:nt, :, 0:1].to_broadcast([nt, 8, 16]),
                                            op=ALU.subtract), reads=[t2v], writes=[gsm])
        op('act', lambda e: e.activation(out=gsm[:nt, :], in_=gsm[:nt, :], func=AF.Exp), reads=[gsm], writes=[gsm])
        op('dve', lambda e: e.tensor_reduce(out=gsum[:nt, :], in_=gv, axis=AX.X, op=ALU.add), reads=[gsm], writes=[gsum])
        op('dve', lambda e: e.reciprocal(out=gsum[:nt, :], in_=gsum[:nt, :]), reads=[gsum], writes=[gsum])
        op('dve', lambda e: e.tensor_tensor(out=gv, in0=gv, in1=gsum[:nt, :].unsqueeze(2).to_broadcast([nt, 8, 16]),
                                            op=ALU.mult), reads=[gsm, gsum], writes=[gsm])
        yield

    def uvphase(kind, ti, k):
        nt = 128 if kind == 'prompt' else DS
        h1 = H1[k % 2]
        eidx = EI[k % 2]
        gsm = GS[k % 2]
        NG = NJ // GRP
        NS = len(DT)

        def part1(g):
            s_ = g % NS
            dt_, ga_, gb_, gc_ = DT[s_], GA[s_], GB[s_], GC[s_]
            gs = slice(g * GRP, (g + 1) * GRP)
            for jj in range(GRP):
                j = g * GRP + jj
                b_ = uvb[j % NB]
                dma('pool', lambda e, j=j, b_=b_: e.indirect_dma_start(
                    out=b_[:, :], out_offset=None, in_=uvbf_d,
                    in_offset=bass.IndirectOffsetOnAxis(ap=eidx[:, j:j + 1], axis=0)), reads=[eidx, UVBF], writes=[b_])
                op('dve', lambda e, jj=jj, b_=b_, dt_=dt_: e.scalar_tensor_tensor(
                    out=b_[:nt, 0:D], in0=b_[:nt, 0:D], scalar=1.0, in1=h1[:nt, :], op0=ALU.mult, op1=ALU.mult,
                    accum_out=dt_[:nt, jj:jj + 1]), reads=[b_, h1], writes=[b_, dt_])
            op('dve', lambda e: e.scalar_tensor_tensor(out=ga_[:nt, :], in0=dt_[:nt, :], scalar=0.044715,
                                                       in1=dt_[:nt, :], op0=ALU.mult, op1=ALU.mult),
               reads=[dt_], writes=[ga_])
            op('dve', lambda e: e.scalar_tensor_tensor(out=ga_[:nt, :], in0=ga_[:nt, :], scalar=1.0,
                                                       in1=dt_[:nt, :], op0=ALU.add, op1=ALU.mult),
               reads=[ga_, dt_], writes=[ga_])
            op('dve', lambda e: e.scalar_tensor_tensor(out=gc_[:nt, :], in0=dt_[:nt, :], scalar=0.5, in1=gsm[:nt, gs],
                                                       op0=ALU.mult, op1=ALU.mult), reads=[dt_, gsm], writes=[gc_])

        def part1b(g):
            s_ = g % NS
            ga_, gb_ = GA[s_], GB[s_]
            op('act', lambda e: e.activation(out=gb_[:nt, :], in_=ga_[:nt, :], func=AF.Tanh,
                                             scale=0.7978845608028654), reads=[ga_], writes=[gb_])

        def part2(g):
            s_ = g % NS
            gb_, gc_, wt_ = GB[s_], GC[s_], WT[s_]
            op('dve', lambda e: e.scalar_tensor_tensor(out=wt_[:nt, :], in0=gb_[:nt, :], scalar=1.0, in1=gc_[:nt, :],
                                                       op0=ALU.add, op1=ALU.mult), reads=[gb_, gc_], writes=[wt_])
            for jj in range(GRP):
                j = g * GRP + jj
                b_ = uvb[j % NB]
                d_ = dg[j % NDG]
                op('act', lambda e, jj=jj, d_=d_: e.activation(out=d_[:nt, :nt], in_=ident[:nt, :nt], func=AF.Copy,
                                                               scale=wt_[:nt, jj:jj + 1]), reads=[ident, wt_], writes=[d_])
                for hb in range(2):
                    op('pe', lambda e, j=j, b_=b_, d_=d_, hb=hb: e.matmul(
                        pV[:nt, hb * 512:(hb + 1) * 512], lhsT=d_[:nt, :nt], rhs=b_[:nt, D + hb * 512:D + (hb + 1) * 512],
                        start=(j == 0), stop=(j == NJ - 1)), reads=[b_, d_], writes=[pV])

        D1 = int(os.environ.get("K_D1", "0"))
        D2 = int(os.environ.get("K_D2", "1"))
        for g in range(NG + D2):
            if g < NG:
                part1(g)
            if 0 <= g - D1 < NG:
                part1b(g - D1)
            if 0 <= g - D2 < NG:
                part2(g - D2)
            yield
        op('dve', lambda e: e.scalar_tensor_tensor(out=h1[:nt, :], in0=h1[:nt, :], scalar=ALPHA, in1=pV[:nt, :],
                                                   op0=ALU.mult, op1=ALU.add), reads=[h1, pV], writes=[h1])
        layernorm(h1, h1, 2, nt)
        if kind == 'prompt':
            dma('sp', lambda e: e.dma_start(out=yp_d[ti * 128:(ti + 1) * 128, :], in_=h1[:nt, :]), reads=[h1], store_src=h1)
        else:
            dma('sp', lambda e: e.dma_start(out=ys_d, in_=h1[:nt, :]), reads=[h1], store_src=h1)
            dma('sp', lambda e: e.dma_start(out=rss_d.rearrange("h d e -> d h e"), in_=state[:].rearrange("p (h e) -> p h e", h=4)),
                reads=[state], store_src=state)
        yield

    def run(g):
        n = 0
        for _ in g:
            n += 1
        return n

    def interleave(gens, ests):
        n = len(gens)
        prog = [0] * n
        alive = [True] * n
        while any(alive):
            best = None
            for i in range(n):
                if alive[i] and (best is None or prog[i] / ests[i] < prog[best] / ests[best]):
                    best = i
            try:
                next(gens[best])
                prog[best] += 1
            except StopIteration:
                alive[best] = False

    n_prompt = int(os.environ.get("K_NPROMPT", NTILES))
    tiles = [('prompt', i) for i in range(n_prompt)] + [('sample', 0)]
    K = len(tiles) - 1
    def first_fronts():
        yield from front('meta', 0, 0)
        n0 = [0]
        for _ in front(tiles[0][0], tiles[0][1], 0):
            n0[0] += 1
            yield
        fest_box.append(n0[0])

    fest_box = []
    interleave([conv_gen(), first_fronts()], [len(chunks) + DEPTH, 100])
    fest = fest_box[0]
    nopipe = int(os.environ.get("K_NOPIPE", "0"))
    for k in range(K + 1):
        gens = [uvphase(tiles[k][0], tiles[k][1], k)]
        ests = [NJ // GRP + 3]
        if k + 1 <= K:
            gens.append(front(tiles[k + 1][0], tiles[k + 1][1], k + 1))
            ests.append(fest)
        if nopipe:
            for g_ in gens:
                run(g_)
        else:
            interleave(gens, ests)
    P.finish()
    P.emit()
    return nc


_NC_CACHE = {}


def _in_maps(inp):
    c = _host_consts()
    f = lambda a: np.ascontiguousarray(np.asarray(a, dtype=np.float32))
    rel_bias = f(inp['rel_bias'])
    shared = {
        "xm": f(inp['meta_tokens']),
        "ln0_g": f(inp['ln_in_g']), "ln0_b": f(inp['ln_in_b']),
        "ln1_g": f(inp['ln1_g'][0]), "ln1_b": f(inp['ln1_b'][0]),
        "ln2_g": f(inp['ln2_g'][0]), "ln2_b": f(inp['ln2_b'][0]),
        "w_in": f(inp['w_in'][0]), "w_out": f(inp['w_out'][0]), "wq": f(inp['peer_wq'][0]),
        "sinks": f(inp['attn_sinks'][0]), "gng": f(inp['ret_gn_g'][0]),
        "peer_u": f(inp['peer_u'][0]), "peer_v": f(inp['peer_v'][0]),
        "b_prev": f(rel_bias[c['bk_prev']].transpose(0, 2, 1)),
        "b_cur": f(rel_bias[c['bk_cur']].transpose(0, 2, 1)),
        "b_meta0": f(rel_bias[c['bk_meta0']].transpose(0, 2, 1)),
        "b_metaf": f(np.broadcast_to(rel_bias[15][None, :], (NMETA, 8))),
        "mask_prev": c['mask_prev'], "mask_cur": c['mask_cur'],
        "rot": c['rot'], "dqk": c['dqk'], "gl": c['gl'], "rmask": c['rmask'], "iota16": c['iota16'],
    }
    sk = f(inp['peer_subkeys'][0])
    skb = np.zeros((8, 128, 256), np.float32)
    for cc in range(2):
        skb[:, cc * 64:(cc + 1) * 64, cc * 128:(cc + 1) * 128] = sk[:, cc].transpose(0, 2, 1)
    shared["skb"] = skb
    maps = []
    for b in range(NCORES):
        m = dict(shared)
        m["xp"] = f(inp['x_prompt'][b])
        m["xs"] = f(inp['x_sample'][b])
        m["cmk"] = f(inp['cache_meta_k'][0, b]).reshape(NMETA, 128)
        m["cmv"] = f(inp['cache_meta_v'][0, b]).reshape(NMETA, 128)
        m["csk"] = f(inp['cache_swa_k'][0, b]).reshape(128, 128)
        m["csv"] = f(inp['cache_swa_v'][0, b]).reshape(128, 128)
        m["st0"] = f(inp['state_ret'][0, b])
        maps.append(m)
    return maps


def kernel(**inp):
    dbg = bool(int(os.environ.get("K_DEBUG", "0")))
    key = (dbg,) + tuple(os.environ.get(k_, "") for k_ in ("K_NPROMPT", "K_STAGE", "K_D1", "K_D2", "K_NB", "K_GRP", "K_NOPIPE"))
    if key not in _NC_CACHE:
        _NC_CACHE[key] = build_program(dbg)
    nc = _NC_CACHE[key]
    maps = _in_maps(inp)
    res = run_bass_kernel_spmd(nc, maps, core_ids=list(range(NCORES)))
    r = res.results
    st = lambda k: np.stack([np.asarray(r[b][k]) for b in range(NCORES)])
    outs = (
        st("y_prompt"),
        st("y_sample"),
        st("meta_k").reshape(1, NCORES, NMETA, 2, 64),
        st("meta_v").reshape(1, NCORES, NMETA, 2, 64),
        st("swa_k").reshape(1, NCORES, 128, 2, 64),
        st("swa_v").reshape(1, NCORES, 128, 2, 64),
        st("ret_state").reshape(1, NCORES, 4, 128, 128),
        st("swa_k_s").reshape(1, NCORES, DS, 2, 64),
        st("swa_v_s").reshape(1, NCORES, DS, 2, 64),
        st("ret_state_s").reshape(1, NCORES, 4, 128, 128),
    )
    if dbg:
        kernel.debug = [{k: np.asarray(v) for k, v in r[b].items() if k.startswith("d_")} for b in range(NCORES)]
    return tuple(np.ascontiguousarray(o.astype(np.float32)) for o in outs)
```

```python
import contextlib
import math
import os
import numpy as np
import concourse.bass as bass
import concourse.mybir as mybir
from concourse.bass_utils import run_bass_kernel_spmd

F32 = mybir.dt.float32
BF16 = mybir.dt.bfloat16
I32 = mybir.dt.int32
U32 = mybir.dt.uint32
ALU = mybir.AluOpType
AF = mybir.ActivationFunctionType
AX = mybir.AxisListType

D = 1024
S = 2048
DS = 64
NMETA = 16
NTILES = S // 128
PW = 2816
NE = 16384
ALPHA = 2.0 ** 0.25
EPS = 1e-5
NCORES = 8
NJ = 128


class T:
    def __init__(self, t, name):
        self.t = t
        self.name = name
        self.w = None
        self.r = {}
        self.dsem = None
        self.dcnt = 0
        self.psem = None
        self.pcnt = 0

    def __getitem__(self, k):
        return self.t[k]


class Prog:
    ENG = ('pe', 'act', 'dve', 'pool', 'sp')

    def __init__(self, nc):
        self.nc = nc
        self.es = contextlib.ExitStack()
        self.q = {e: [] for e in self.ENG}
        self.cnt = {e: 0 for e in self.ENG}
        self.seen = {e: {} for e in self.ENG}
        self.esem = {}
        for e in self.ENG:
            self.esem[e] = self.es.enter_context(nc.semaphore('es_' + e))
        self.nsem = 0
        self.stores = []

    def sb(self, name, shape, dt=F32):
        return T(self.es.enter_context(self.nc.sbuf_tensor('sb_' + name, list(shape), dt)), name)

    def ps(self, name, shape, dt=F32):
        return T(self.es.enter_context(self.nc.psum_tensor('ps_' + name, list(shape), dt)), name)

    def newsem(self, name):
        self.nsem += 1
        return self.es.enter_context(self.nc.semaphore('d_%s_%d' % (name, self.nsem)))

    def _waits(self, eng, reads, writes):
        deps = {}

        def add(d):
            if d is None:
                return
            sem, val, src = d
            if eng == 'pe' and src == 'pe':
                return
            k = id(sem)
            if k not in deps or deps[k][1] < val:
                deps[k] = (sem, val)

        for t in reads:
            add(t.w)
        for t in writes:
            add(t.w)
            for d in t.r.values():
                add(d)
        out = []
        for k, (sem, val) in deps.items():
            if self.seen[eng].get(k, 0) >= val:
                continue
            self.seen[eng][k] = val
            out.append((sem, val))
        return out

    def op(self, eng, fn, reads=(), writes=()):
        waits = self._waits(eng, reads, writes)
        self.cnt[eng] += 1
        me = (self.esem[eng], self.cnt[eng], eng)
        self.q[eng].append((waits, fn, (self.esem[eng], 1)))
        for t in reads:
            t.r[id(me[0])] = me
        for t in writes:
            t.w = me
            t.r = {}

    def dma(self, eng, fn, reads=(), writes=(), store_src=None):
        waits = self._waits(eng, reads, writes)
        owner = writes[0] if writes else store_src
        if eng == 'pool':
            if owner.psem is None:
                owner.psem = self.newsem('p' + owner.name)
            owner.pcnt += 16
            me = (owner.psem, owner.pcnt, 'dma')
        else:
            if owner.dsem is None:
                owner.dsem = self.newsem(owner.name)
            owner.dcnt += 16
            me = (owner.dsem, owner.dcnt, 'dma')
        self.q[eng].append((waits, fn, (me[0], 16)))
        for t in reads:
            t.r[id(me[0])] = me
        for t in writes:
            t.w = me
            t.r = {}
        if store_src is not None:
            self.stores.append(me)

    def finish(self):
        deps = {}
        for sem, val, _ in self.stores:
            k = id(sem)
            if k not in deps or deps[k][1] < val:
                deps[k] = (sem, val)
        for e in self.ENG:
            if e != 'sp' and self.cnt[e] > 0:
                deps[id(self.esem[e])] = (self.esem[e], self.cnt[e])
        self.q['sp'].append((list(deps.values()), None, None))

    def emit(self):
        q = self.q
        import bisect
        eng_of = {id(self.esem[e]): e for e in self.ENG}
        targets = {e: set() for e in self.ENG}
        for name in self.ENG:
            for waits, fn, inc in q[name]:
                for sem, val in waits:
                    if id(sem) in eng_of:
                        targets[eng_of[id(sem)]].add(val)
        tsorted = {e: sorted(targets[e]) for e in self.ENG}

        def real(sem, val):
            e_ = eng_of.get(id(sem))
            if e_ is None:
                return val
            return bisect.bisect_right(tsorted[e_], val)

        def replay(name, e):
            vidx = 0
            for waits, fn, inc in q[name]:
                for sem, val in waits:
                    e.wait_ge(sem, real(sem, val))
                if fn is None:
                    continue
                ins = fn(e)
                if inc is not None:
                    if id(inc[0]) in eng_of:
                        vidx += 1
                        if vidx in targets[name]:
                            ins.then_inc(inc[0], 1)
                    else:
                        ins.then_inc(inc[0], inc[1])

        with self.nc.Block() as block:
            @block.sync
            def _(e):
                replay('sp', e)

            @block.scalar
            def _(e):
                replay('act', e)

            @block.vector
            def _(e):
                replay('dve', e)

            @block.gpsimd
            def _(e):
                replay('pool', e)

            @block.tensor
            def _(e):
                replay('pe', e)
        self.es.close()


def _t5_bucket(rel):
    nb = 16
    max_exact = 8
    n = np.abs(rel)
    v = (np.log(np.maximum(n, max_exact).astype(np.float32) / np.float32(max_exact))
         / np.float32(math.log(128 / 8)) * np.float32(nb - max_exact))
    large = np.minimum(max_exact + v.astype(np.int32), nb - 1)
    return np.where(rel > 0, nb, 0) + np.where(n < max_exact, n, large)


def _host_consts():
    c = {}
    k = np.arange(128)[:, None]
    q = np.arange(128)[None, :]
    c['bk_prev'] = _t5_bucket(k - 128 - q)
    c['bk_cur'] = _t5_bucket(k - q)
    m = np.arange(NMETA)[:, None]
    c['bk_meta0'] = _t5_bucket(m - NMETA - q)
    ok_prev = (q < 64) | (k >= 64)
    ok_cur = (q >= 64) | (k < 64)
    neg = np.float32(-30000.0)
    c['mask_prev'] = np.where(ok_prev, 0.0, neg).astype(np.float32)
    c['mask_cur'] = np.where(ok_cur, 0.0, neg).astype(np.float32)
    half = 64
    inv = (np.float32(10000.0) ** (-np.arange(half, dtype=np.float32) / np.float32(half))).astype(np.float32)
    rot = np.zeros((18, 128, 128), np.float32)
    for ti in range(18):
        if ti < 16:
            pos = NMETA + ti * 128 + np.arange(128)
        elif ti == 16:
            pos = NMETA + S + np.arange(128)
        else:
            pos = np.arange(128)
        ang = pos.astype(np.float32)[:, None] * inv[None, :]
        rot[ti, :, :64] = np.cos(ang)
        rot[ti, :, 64:] = np.sin(ang)
    c['rot'] = rot
    gam = 1.0 - 2.0 ** (-5.0 - np.arange(4, dtype=np.float64))
    t = np.arange(128, dtype=np.float64)[:, None]
    dq = gam[None, :] ** (t + 1.0)
    dk = gam[None, :] ** (-(t + 1.0)) * (128.0 ** -0.5)
    c['dqk'] = np.concatenate([dq, dk], axis=1).astype(np.float32)
    gl = np.zeros((128, 3, 4), np.float32)
    for li, L in enumerate((128, 64, 16)):
        gl[:, li, :] = (gam ** L)[None, :]
    c['gl'] = gl.reshape(128, 12)
    c['rmask'] = (k <= q).astype(np.float32)
    c['iota16'] = np.tile(np.arange(16, dtype=np.float32)[None, :], (128, 1))
    return c


def build_program(dbg=False):
    nc = bass.Bass("TRN2", target_bir_lowering=False)

    def din(name, shape, dt=F32):
        return nc.dram_tensor(name, list(shape), dt, kind="ExternalInput").ap()

    def dout(name, shape, dt=F32):
        return nc.dram_tensor(name, list(shape), dt, kind="ExternalOutput").ap()

    xp_d = din("xp", [S, D])
    xs_d = din("xs", [DS, D])
    xm_d = din("xm", [NMETA, D])
    cmk_d = din("cmk", [NMETA, 128])
    cmv_d = din("cmv", [NMETA, 128])
    csk_d = din("csk", [128, 128])
    csv_d = din("csv", [128, 128])
    st0_d = din("st0", [4, 128, 128])
    lng_d = [din("ln%d_g" % i, [D]) for i in range(3)]
    lnb_d = [din("ln%d_b" % i, [D]) for i in range(3)]
    win_d = din("w_in", [D, PW])
    wout_d = din("w_out", [D, D])
    wq_d = din("wq", [D, D])
    skb_d = din("skb", [8, 128, 256])
    sinks_d = din("sinks", [8])
    gng_d = din("gng", [512])
    u_d = din("peer_u", [NE, D])
    v_d = din("peer_v", [NE, D])
    bprev_d = din("b_prev", [128, 8, 128])
    bcur_d = din("b_cur", [128, 8, 128])
    bm0_d = din("b_meta0", [NMETA, 8, 128])
    bmf_d = din("b_metaf", [NMETA, 8])
    mprev_d = din("mask_prev", [128, 128])
    mcur_d = din("mask_cur", [128, 128])
    rot_d = din("rot", [18, 128, 128])
    dqk_d = din("dqk", [128, 8])
    gl_d = din("gl", [128, 12])
    rmask_d = din("rmask", [128, 128])
    iota16_d = din("iota16", [128, 16])

    yp_d = dout("y_prompt", [S, D])
    ys_d = dout("y_sample", [DS, D])
    mk_d = dout("meta_k", [NMETA, 128])
    mv_d = dout("meta_v", [NMETA, 128])
    sk_d = dout("swa_k", [128, 128])
    sv_d = dout("swa_v", [128, 128])
    rs_d = dout("ret_state", [4, 128, 128])
    sks_d = dout("swa_k_s", [DS, 128])
    svs_d = dout("swa_v_s", [DS, 128])
    rss_d = dout("ret_state_s", [4, 128, 128])
    uvbf_d = nc.dram_tensor("uvbf_scratch", [NE, 2 * D], BF16, kind="Internal").ap()
    dbg_d = {}
    if dbg:
        for nm, w in (("d_h", 1024), ("d_p", 2816), ("d_mixed", 1024), ("d_h1", 1024), ("d_eidx", 128),
                      ("d_g", 128), ("d_dots", 128), ("d_peer", 1024)):
            dbg_d[nm] = dout(nm, [128, w])

    P = Prog(nc)
    op, dma = P.op, P.dma

    w_in = P.sb("w_in", [128, 8, PW], BF16)
    w_out = P.sb("w_out", [128, 8, D], BF16)
    w_q = P.sb("w_q", [128, 8, D], BF16)
    skb = P.sb("skb", [128, 8, 256], BF16)
    lng = [P.sb("lng%d" % i, [128, D], BF16) for i in range(3)]
    lnb = [P.sb("lnb%d" % i, [128, D], BF16) for i in range(3)]
    gng = P.sb("gng", [128, 512])
    esink = P.sb("esink", [128, 8])
    b_prev = P.sb("b_prev", [128, 8, 128], BF16)
    b_cur = P.sb("b_cur", [128, 8, 128], BF16)
    b_mf = P.sb("b_mf", [NMETA, 8])
    rot = P.sb("rot", [128, 128])
    dqk = P.sb("dqk", [128, 8])
    gl = P.sb("gl", [128, 12])
    rmask = P.sb("rmask", [128, 128])
    iota16 = P.sb("iota16", [128, 16])
    ident = P.sb("ident", [128, 128], BF16)

    F4a = P.sb("F4a", [128, D])
    H1 = [P.sb("H1_%d" % i, [128, D]) for i in range(2)]
    F4c = P.sb("F4c", [128, D])
    B2a = P.sb("B2a", [128, D], BF16)
    B2b = P.sb("B2b", [128, D], BF16)
    T8a = P.sb("T8a", [128, 8, 128], BF16)
    T8b = P.sb("T8b", [128, 8, 128], BF16)
    lst = P.sb("lst", [128, 2, 6])
    lmv = P.sb("lmv", [128, 2])
    lrs = P.sb("lrs", [128, 1])
    kvf = P.sb("kvf", [128, 256])
    qbf = P.sb("qbf", [128, 512], BF16)
    kbf = P.sb("kbf", [128, 128], BF16)
    KT = [P.sb("KT%d" % i, [128, 128], BF16) for i in range(2)]
    VX = [P.sb("VX%d" % i, [128, 2, 65], BF16) for i in range(2)]
    KTm = P.sb("KTm", [128, NMETA], BF16)
    VXm = P.sb("VXm", [NMETA, 2, 65], BF16)
    QT = P.sb("QT", [128, 4, 128], BF16)
    sS = P.sb("sS", [128, 512])
    PT1 = [P.sb("PT%d" % a, [128, 512], BF16) for a in range(3)]
    PT = [[PT1[a], PT1[a]] for a in range(3)]
    den = P.sb("den", [128, 8])
    tA = P.sb("tA", [128, 512])
    tB = P.sb("tB", [128, 512])
    rqkb = P.sb("rqkb", [128, D], BF16)
    rvb = P.sb("rvb", [128, 512], BF16)
    gate = P.sb("gate", [128, 512], BF16)
    SmT = P.sb("SmT", [128, 4, 128], BF16)
    state = P.sb("state", [128, 512])
    stbf = P.sb("stbf", [128, 512], BF16)
    tmpS = tA
    gst = P.sb("gst", [128, 4, 6])
    gmv = P.sb("gmv", [128, 4, 2])
    grs = P.sb("grs", [128, 4])
    yA = tB
    cand = F4a
    b_m0 = H1[1]
    sw = [P.sb("sw%d" % i, [128, 256]) for i in range(2)]
    tv = P.sb("tv", [128, 16, 16])
    tiu = P.sb("tiu", [128, 16, 16], U32)
    tif = P.sb("tif", [128, 16, 16])
    t2v = P.sb("t2v", [128, 8, 16])
    t2i = P.sb("t2i", [128, 8, 16], U32)
    k1u = P.sb("k1u", [128, 128], U32)
    k2u = P.sb("k2u", [128, 128], U32)
    k1f = P.sb("k1f", [128, 128])
    k2f = P.sb("k2f", [128, 128])
    i1s = P.sb("i1s", [128, 128])
    i2s = P.sb("i2s", [128, 128])
    eif = P.sb("eif", [128, 128])
    EI = [P.sb("eidx%d" % i, [128, 128], I32) for i in range(2)]
    GS = [P.sb("gsm%d" % i, [128, 128]) for i in range(2)]
    gsum = P.sb("gsum", [128, 8])
    NB = int(os.environ.get("K_NB", "8"))
    GRP = int(os.environ.get("K_GRP", "2"))
    uvb = [P.sb("uvb%d" % i, [128, 2 * D], BF16) for i in range(NB)]
    NDG = int(os.environ.get("K_NDG", "8"))
    dg = [P.sb("dg%d" % i, [128, 128], BF16) for i in range(NDG)]
    DT = [P.sb("dt%d" % i, [128, GRP]) for i in range(4)]
    GA = [P.sb("gga%d" % i, [128, GRP]) for i in range(4)]
    GB = [P.sb("ggb%d" % i, [128, GRP]) for i in range(4)]
    GC = [P.sb("ggc%d" % i, [128, GRP]) for i in range(4)]
    WT = [P.sb("wt%d" % i, [128, GRP]) for i in range(4)]

    ptr = P.ps("ptr", [128, 8, 128], BF16)
    pA = P.ps("pA", [128, D])
    pV = P.ps("pV", [128, D])
    pS = P.ps("pS", [128, 512])
    pO = [P.ps("pO%d" % i, [128, 512]) for i in range(2)]

    def bc_part(ap, n=128):
        return ap.partition_broadcast(n)

    for i in range(3):
        dma('sp', lambda e, i=i: e.dma_start(out=F4c[:], in_=bc_part(lng_d[i])), writes=[F4c])
        op('dve', lambda e, i=i: e.tensor_scalar(out=lng[i][:], in0=F4c[:], scalar1=-1.0, scalar2=None, op0=ALU.add),
           reads=[F4c], writes=[lng[i]])
        dma('sp', lambda e, i=i: e.dma_start(out=F4a[:], in_=bc_part(lnb_d[i])), writes=[F4a])
        op('dve', lambda e, i=i: e.tensor_copy(out=lnb[i][:], in_=F4a[:]), reads=[F4a], writes=[lnb[i]])
    dma('sp', lambda e: e.dma_start(out=gng[:], in_=bc_part(gng_d)), writes=[gng])
    dma('sp', lambda e: e.dma_start(out=esink[:], in_=bc_part(sinks_d)), writes=[esink])
    op('act', lambda e: e.activation(out=esink[:], in_=esink[:], func=AF.Exp), reads=[esink], writes=[esink])
    dma('sp', lambda e: e.dma_start(out=dqk[:], in_=dqk_d), writes=[dqk])
    dma('sp', lambda e: e.dma_start(out=gl[:], in_=gl_d), writes=[gl])
    dma('sp', lambda e: e.dma_start(out=rmask[:], in_=rmask_d), writes=[rmask])
    dma('sp', lambda e: e.dma_start(out=iota16[:], in_=iota16_d), writes=[iota16])
    dma('sp', lambda e: e.dma_start(out=F4a[:].rearrange("p (h q) -> p h q", h=8), in_=bprev_d), writes=[F4a])
    dma('sp', lambda e: e.dma_start(out=F4c[:].rearrange("p (h q) -> p h q", h=8), in_=bcur_d), writes=[F4c])
    dma('sp', lambda e: e.dma_start(out=b_mf[:], in_=bmf_d), writes=[b_mf])
    dma('sp', lambda e: e.dma_start(out=tA[:, 0:128], in_=mprev_d), writes=[tA])
    dma('sp', lambda e: e.dma_start(out=tB[:, 0:128], in_=mcur_d), writes=[tB])
    op('dve', lambda e: e.tensor_tensor(out=b_prev[:], in0=F4a[:].rearrange("p (h q) -> p h q", h=8),
                                        in1=tA[:, 0:128].unsqueeze(1).to_broadcast([128, 8, 128]), op=ALU.add),
       reads=[F4a, tA], writes=[b_prev])
    op('dve', lambda e: e.tensor_tensor(out=b_cur[:], in0=F4c[:].rearrange("p (h q) -> p h q", h=8),
                                        in1=tB[:, 0:128].unsqueeze(1).to_broadcast([128, 8, 128]), op=ALU.add),
       reads=[F4c, tB], writes=[b_cur])
    iota_i = EI[0]
    identf = sS
    op('pool', lambda e: e.iota(iota_i[:], pattern=[[1, 128]], base=0, channel_multiplier=-1), writes=[iota_i])
    op('dve', lambda e: e.tensor_copy(out=identf[:, 0:128], in_=iota_i[:]), reads=[iota_i], writes=[identf])
    op('dve', lambda e: e.tensor_scalar(out=ident[:], in0=identf[:, 0:128], scalar1=0.0, scalar2=None, op0=ALU.is_equal),
       reads=[identf], writes=[ident])
    for t_ in VX + [VXm]:
        op('dve', lambda e, t_=t_: e.memset(t_[:], 1.0), writes=[t_])
    for t_ in EI:
        op('dve', lambda e, t_=t_: e.memset(t_[:], 0), writes=[t_])
    op('dve', lambda e: e.memset(state[:], 0.0), writes=[state])
    op('dve', lambda e: e.memset(stbf[:], 0.0), writes=[stbf])

    stg = [F4a, H1[0], H1[1], F4c]
    nst = [0]

    def load_cast(dst_fn, src_ap, ncols, dst_tile, permq=False):
        s_ = stg[nst[0] % len(stg)]
        eng = 'act' if (nst[0] % 2 == 0) else 'dve'
        nst[0] += 1
        dma('sp', lambda e: e.dma_start(out=s_[:, 0:ncols], in_=src_ap), writes=[s_])
        if permq:
            def f(e):
                return e.tensor_copy(out=dst_fn(0, 512).rearrange("p (h g d) -> p h g d", h=4, g=2),
                                     in_=s_[:, 0:512].rearrange("p (g h d) -> p h g d", g=2, h=4))
            op('dve', f, reads=[s_], writes=[dst_tile])
            if eng == 'act':
                op('act', lambda e: e.activation(out=dst_fn(512, ncols), in_=s_[:, 512:ncols], func=AF.Copy),
                   reads=[s_], writes=[dst_tile])
            else:
                op('dve', lambda e: e.tensor_copy(out=dst_fn(512, ncols), in_=s_[:, 512:ncols]),
                   reads=[s_], writes=[dst_tile])
        else:
            if eng == 'act':
                op('act', lambda e: e.activation(out=dst_fn(0, ncols), in_=s_[:, 0:ncols], func=AF.Copy),
                   reads=[s_], writes=[dst_tile])
            else:
                op('dve', lambda e: e.tensor_copy(out=dst_fn(0, ncols), in_=s_[:, 0:ncols]),
                   reads=[s_], writes=[dst_tile])

    for kc in range(8):
        for (c0, c1) in ((0, 1024), (1024, 2048), (2048, PW)):
            load_cast(lambda a, b, kc=kc, c0=c0: w_in[:, kc, c0 + a:c0 + b],
                      win_d[kc * 128:(kc + 1) * 128, c0:c1], c1 - c0, w_in, permq=(c0 == 0))
    for kc in range(8):
        load_cast(lambda a, b, kc=kc: w_out[:, kc, a:b], wout_d[kc * 128:(kc + 1) * 128, :], D, w_out)
    for kc in range(8):
        load_cast(lambda a, b, kc=kc: w_q[:, kc, a:b], wq_d[kc * 128:(kc + 1) * 128, :], D, w_q)
    for hp in range(2):
        s_ = stg[hp]
        dma('sp', lambda e, hp=hp, s_=s_: e.dma_start(out=s_[:, 0:1024].rearrange("p (k n) -> p k n", k=4),
                                                      in_=skb_d[hp * 4:(hp + 1) * 4].rearrange("k p n -> p k n")),
            writes=[s_])
        op('dve', lambda e, hp=hp, s_=s_: e.tensor_copy(out=skb[:, hp * 4:(hp + 1) * 4, :],
                                                        in_=s_[:, 0:1024].rearrange("p (k n) -> p k n", k=4)),
           reads=[s_], writes=[skb])

    UVBF = T(None, "uvbf")
    chunks = []
    for (src_d, off) in ((u_d, 0), (v_d, D)):
        src_v = src_d.rearrange("(p r) d -> p r d", p=128)
        dst_v = uvbf_d.rearrange("(p r) d -> p r d", p=128)[:, :, off:off + D]
        for c in range(128):
            chunks.append((src_v, dst_v, c))
    NSTG = NB // 2
    DEPTH = NSTG - 1

    def conv_load(i):
        src_v, dst_v, c = chunks[i]
        s_ = uvb[i % NSTG]
        dma('sp', lambda e: e.dma_start(out=s_.t.bitcast(F32)[:, :], in_=src_v[:, c, :]), writes=[s_])

    def conv_cast_store(i):
        src_v, dst_v, c = chunks[i]
        s_ = uvb[i % NSTG]
        b_ = uvb[NSTG + i % NSTG]
        eng = ('act', 'dve', 'pool')[i % 3]
        if eng == 'act':
            op('act', lambda e: e.activation(out=b_[:, 0:D], in_=s_.t.bitcast(F32)[:, :], func=AF.Copy), reads=[s_], writes=[b_])
        else:
            op(eng, lambda e: e.tensor_copy(out=b_[:, 0:D], in_=s_.t.bitcast(F32)[:, :]), reads=[s_], writes=[b_])
        dma('sp', lambda e: e.dma_start(out=dst_v[:, c, :], in_=b_[:, 0:D]), reads=[b_], writes=[UVBF])

    def conv_gen():
        for i in range(len(chunks) + DEPTH):
            if i < len(chunks):
                conv_load(i)
            if i >= DEPTH:
                conv_cast_store(i - DEPTH)
            yield

    dma('sp', lambda e: e.dma_start(out=b_m0[:NMETA, :].rearrange("p (h q) -> p h q", h=8), in_=bm0_d), writes=[b_m0])

    def layernorm(src, dst, gi, nt):
        for c in range(2):
            op('dve', lambda e, c=c: e.bn_stats(out=lst[:nt, c, :], in_=src[:nt, c * 512:(c + 1) * 512]),
               reads=[src], writes=[lst])
        op('dve', lambda e: e.bn_aggr(out=lmv[:nt, :], in_=lst[:nt].rearrange("p a b -> p (a b)")),
           reads=[lst], writes=[lmv])
        op('act', lambda e: e.activation(out=lrs[:nt, :], in_=lmv[:nt, 1:2], func=AF.Sqrt, bias=EPS, scale=1.0),
           reads=[lmv], writes=[lrs])
        op('dve', lambda e: e.reciprocal(out=lrs[:nt, :], in_=lrs[:nt, :]), reads=[lrs], writes=[lrs])
        op('dve', lambda e: e.tensor_scalar(out=dst[:nt, :], in0=src[:nt, :], scalar1=lmv[:nt, 0:1],
                                            scalar2=lrs[:nt, 0:1], op0=ALU.subtract, op1=ALU.mult),
           reads=[src, lmv, lrs], writes=[dst])
        op('dve', lambda e: e.scalar_tensor_tensor(out=dst[:nt, :], in0=lng[gi][:nt, :], scalar=1.0, in1=dst[:nt, :],
                                                   op0=ALU.add, op1=ALU.mult), reads=[dst, lng[gi]], writes=[dst])
        op('dve', lambda e: e.tensor_tensor(out=dst[:nt, :], in0=dst[:nt, :], in1=lnb[gi][:nt, :], op=ALU.add),
           reads=[dst, lnb[gi]], writes=[dst])

    def transpose_to(src_bf, dstT, nt, nblk=8, evac='dve'):
        for j in range(nblk):
            op('pe', lambda e, j=j: e.transpose(out=ptr[:, j, :nt], in_=src_bf[:nt, j * 128:(j + 1) * 128],
                                                identity=ident[:nt, :nt]),
               reads=[src_bf, ident], writes=[ptr])
        if evac == 'dve':
            op('dve', lambda e: e.tensor_copy(out=dstT[:, 0:nblk, :nt], in_=ptr[:, 0:nblk, :nt]),
               reads=[ptr], writes=[dstT])
        else:
            op('act', lambda e: e.activation(out=dstT[:, 0:nblk, :nt], in_=ptr[:, 0:nblk, :nt], func=AF.Copy),
               reads=[ptr], writes=[dstT])

    def mm_full(out_ps, lhsT8, nt, rhs_w, c0, c1):
        c = c0
        while c < c1:
            ce = min(c + 512, c1)
            for kc in range(8):
                op('pe', lambda e, kc=kc, c=c, ce=ce: e.matmul(out_ps[:nt, c - c0:ce - c0], lhsT=lhsT8[:, kc, :nt],
                                                              rhs=rhs_w[:, kc, c:ce], start=(kc == 0), stop=(kc == 7)),
                   reads=[lhsT8, rhs_w], writes=[out_ps])
            c = ce

    def dbg_dump(name, tile_, ncols, nt):
        if dbg and name in dbg_d:
            dma('sp', lambda e: e.dma_start(out=dbg_d[name][:nt, 0:ncols], in_=tile_[:nt, 0:ncols]),
                reads=[tile_], store_src=tile_)

    def tile_params(kind, ti):
        if kind == 'meta':
            return NMETA, 17, 2, xm_d
        if kind == 'prompt':
            return 128, ti, 0, xp_d[ti * 128:(ti + 1) * 128, :]
        return DS, 16, 1, xs_d

    def sample_setup():
        dma('sp', lambda e: e.dma_start(out=rs_d.rearrange("h d e -> d h e"), in_=state[:].rearrange("p (h e) -> p h e", h=4)),
            reads=[state], store_src=state)
        dma('sp', lambda e: e.dma_start(out=kvf[:, 0:128], in_=csk_d), writes=[kvf])
        dma('sp', lambda e: e.dma_start(out=kvf[:, 128:256], in_=csv_d), writes=[kvf])
        op('act', lambda e: e.activation(out=kbf[:, :], in_=kvf[:, 0:128], func=AF.Copy), reads=[kvf], writes=[kbf])
        op('dve', lambda e: e.tensor_copy(out=VX[0][:, :, 0:64], in_=kvf[:, 128:256].rearrange("p (g d) -> p g d", g=2)),
           reads=[kvf], writes=[VX[0]])
        op('pe', lambda e: e.transpose(out=ptr[:, 0, :], in_=kbf[:, :], identity=ident[:, :]), reads=[kbf, ident], writes=[ptr])
        op('dve', lambda e: e.tensor_copy(out=KT[0][:, :], in_=ptr[:, 0, :]), reads=[ptr], writes=[KT[0]])
        dma('sp', lambda e: e.dma_start(out=kvf[:NMETA, 0:128], in_=cmk_d), writes=[kvf])
        dma('sp', lambda e: e.dma_start(out=kvf[:NMETA, 128:256], in_=cmv_d), writes=[kvf])
        op('act', lambda e: e.activation(out=kbf[:NMETA, :], in_=kvf[:NMETA, 0:128], func=AF.Copy), reads=[kvf], writes=[kbf])
        op('dve', lambda e: e.tensor_copy(out=VXm[:NMETA, :, 0:64], in_=kvf[:NMETA, 128:256].rearrange("p (g d) -> p g d", g=2)),
           reads=[kvf], writes=[VXm])
        op('pe', lambda e: e.transpose(out=ptr[:, 0, :NMETA], in_=kbf[:NMETA, :], identity=ident[:NMETA, :NMETA]),
           reads=[kbf, ident], writes=[ptr])
        op('dve', lambda e: e.tensor_copy(out=KTm[:, :NMETA], in_=ptr[:, 0, :NMETA]), reads=[ptr], writes=[KTm])
        dma('sp', lambda e: e.dma_start(out=state[:].rearrange("p (h e) -> p h e", h=4), in_=st0_d.rearrange("h d e -> d h e")),
            writes=[state])
        op('act', lambda e: e.activation(out=stbf[:], in_=state[:], func=AF.Copy), reads=[state], writes=[stbf])

    def front(kind, ti, k):
        nt, rti, gli, x_src = tile_params(kind, ti)
        gsm = GS[k % 2]
        if kind == 'sample':
            sample_setup()
            yield
        h_t = F4a
        h1 = H1[k % 2]
        eidx = EI[k % 2]
        dma('sp', lambda e: e.dma_start(out=h_t[:nt, :], in_=x_src), writes=[h_t])
        dma('sp', lambda e: e.dma_start(out=rot[:, :], in_=rot_d[rti]), writes=[rot])
        layernorm(h_t, h_t, 0, nt)
        yield
        op('act', lambda e: e.activation(out=B2a[:nt, :], in_=h_t[:nt, :], func=AF.Copy), reads=[h_t], writes=[B2a])
        transpose_to(B2a, T8a, nt, 8, 'dve')
        hT = T8a
        yield
        mm_full(pA, hT, nt, w_in, 0, 768)
        yield
        op('act', lambda e: e.activation(out=qbf[:nt, :], in_=pA[:nt, 0:512], func=AF.Copy), reads=[pA], writes=[qbf])
        op('dve', lambda e: e.tensor_copy(out=kvf[:nt, :], in_=pA[:nt, 512:768]), reads=[pA], writes=[kvf])
        if kind == 'meta':
            dma('sp', lambda e: e.dma_start(out=mk_d, in_=kvf[:nt, 0:128]), reads=[kvf], store_src=kvf)
            dma('sp', lambda e: e.dma_start(out=mv_d, in_=kvf[:nt, 128:256]), reads=[kvf], store_src=kvf)
        elif kind == 'prompt' and ti == NTILES - 1:
            dma('sp', lambda e: e.dma_start(out=sk_d, in_=kvf[:nt, 0:128]), reads=[kvf], store_src=kvf)
            dma('sp', lambda e: e.dma_start(out=sv_d, in_=kvf[:nt, 128:256]), reads=[kvf], store_src=kvf)
        elif kind == 'sample':
            dma('sp', lambda e: e.dma_start(out=sks_d, in_=kvf[:nt, 0:128]), reads=[kvf], store_src=kvf)
            dma('sp', lambda e: e.dma_start(out=svs_d, in_=kvf[:nt, 128:256]), reads=[kvf], store_src=kvf)
        if kind == 'meta':
            kt_cur, vx_cur = KTm, VXm
        elif kind == 'prompt':
            kt_cur, vx_cur = KT[ti % 2], VX[ti % 2]
        else:
            kt_cur, vx_cur = KT[1], VX[1]
        op('act', lambda e: e.activation(out=kbf[:nt, :], in_=kvf[:nt, 0:128], func=AF.Copy), reads=[kvf], writes=[kbf])
        op('dve', lambda e: e.tensor_copy(out=vx_cur[:nt, :, 0:64],
                                          in_=kvf[:nt, 128:256].rearrange("p (g d) -> p g d", g=2)),
           reads=[kvf], writes=[vx_cur])
        op('pe', lambda e: e.transpose(out=ptr[:, 0, :nt], in_=kbf[:nt, :], identity=ident[:nt, :nt]),
           reads=[kbf, ident], writes=[ptr])
        op('dve', lambda e: e.tensor_copy(out=kt_cur[:, :nt], in_=ptr[:, 0, :nt]), reads=[ptr], writes=[kt_cur])
        yield
        mm_full(pA, hT, nt, w_in, 768, 1792)
        yield
        rqk = F4c
        cos_b = rot[:nt, 0:64].unsqueeze(1).to_broadcast([nt, 8, 64])
        sin_b = rot[:nt, 64:128].unsqueeze(1).to_broadcast([nt, 8, 64])
        pv = pA[:nt, :].rearrange("p (a t d) -> p a t d", a=8, t=2)
        x1 = pv[:, :, 0, :]
        x2 = pv[:, :, 1, :]
        tAv = tA[:nt, :].rearrange("p (a d) -> p a d", a=8)
        tBv = tB[:nt, :].rearrange("p (a d) -> p a d", a=8)
        rv4 = rqk[:nt, :].rearrange("p (a t d) -> p a t d", a=8, t=2)
        op('dve', lambda e: e.tensor_tensor(out=tAv, in0=x1, in1=cos_b, op=ALU.mult), reads=[pA, rot], writes=[tA])
        op('dve', lambda e: e.tensor_tensor(out=tBv, in0=x2, in1=sin_b, op=ALU.mult), reads=[pA, rot], writes=[tB])
        op('dve', lambda e: e.tensor_tensor(out=rv4[:, :, 0, :], in0=tAv, in1=tBv, op=ALU.subtract),
           reads=[tA, tB], writes=[rqk])
        op('dve', lambda e: e.tensor_tensor(out=tAv, in0=x1, in1=sin_b, op=ALU.mult), reads=[pA, rot], writes=[tA])
        op('dve', lambda e: e.tensor_tensor(out=tBv, in0=x2, in1=cos_b, op=ALU.mult), reads=[pA, rot], writes=[tB])
        op('dve', lambda e: e.tensor_tensor(out=rv4[:, :, 1, :], in0=tAv, in1=tBv, op=ALU.add),
           reads=[tA, tB], writes=[rqk])
        op('dve', lambda e: e.tensor_tensor(out=rqkb[:nt, :].rearrange("p (a d) -> p a d", a=8),
                                            in0=rqk[:nt, :].rearrange("p (a d) -> p a d", a=8),
                                            in1=dqk[:nt, :].unsqueeze(2).to_broadcast([nt, 8, 128]), op=ALU.mult),
           reads=[rqk, dqk], writes=[rqkb])
        yield
        mm_full(pA, hT, nt, w_in, 1792, PW)
        op('act', lambda e: e.activation(out=rvb[:nt, :], in_=pA[:nt, 0:512], func=AF.Copy), reads=[pA], writes=[rvb])
        if kind != 'meta':
            op('act', lambda e: e.activation(out=gate[:nt, :], in_=pA[:nt, 512:1024], func=AF.Silu),
               reads=[pA], writes=[gate])
        yield
        mixed = B2b
        if kind != 'meta':
            transpose_to(qbf, QT, nt, 4, 'act')
            if kind == 'prompt':
                groups = []
                if ti > 0:
                    groups.append((KT[(ti - 1) % 2], VX[(ti - 1) % 2], 128, b_prev, 0))
                groups.append((kt_cur, vx_cur, nt, b_cur, 1))
                groups.append((KTm, VXm, NMETA, b_m0 if ti == 0 else b_mf, 2))
            else:
                groups = [(KT[0], VX[0], 128, b_prev, 0), (kt_cur, vx_cur, nt, b_cur, 1), (KTm, VXm, NMETA, b_mf, 2)]
            for g in range(2):
                pr = slice(g * 64, (g + 1) * 64)
                for (kt_, vx_, nk, btab, a) in groups:
                    op('pe', lambda e, kt_=kt_, nk=nk, pr=pr: e.matmul(
                        pS[:nk, 0:4 * nt].rearrange("p (h q) -> p h q", h=4), lhsT=kt_[pr, :nk], rhs=QT[pr, :, :nt],
                        start=True, stop=True), reads=[kt_, QT], writes=[pS])
                    if btab is b_mf:
                        bias_ap = btab[:nk, g * 4:(g + 1) * 4].unsqueeze(2).to_broadcast([nk, 4, nt])
                    elif btab is b_m0:
                        bias_ap = b_m0[:nk, :].rearrange("p (h q) -> p h q", h=8)[:, g * 4:(g + 1) * 4, :nt]
                    else:
                        bias_ap = btab[:nk, g * 4:(g + 1) * 4, :nt]
                    op('dve', lambda e, nk=nk, bias_ap=bias_ap: e.scalar_tensor_tensor(
                        out=sS[:nk, 0:4 * nt].rearrange("p (h q) -> p h q", h=4),
                        in0=pS[:nk, 0:4 * nt].rearrange("p (h q) -> p h q", h=4), scalar=0.125,
                        in1=bias_ap, op0=ALU.mult, op1=ALU.add),
                        reads=[pS, btab], writes=[sS])
                    op('act', lambda e, nk=nk, a=a, g=g: e.activation(out=PT[a][g][:nk, 0:4 * nt], in_=sS[:nk, 0:4 * nt],
                                                                       func=AF.Exp), reads=[sS], writes=[PT[a][g]])
                yield
                for hh in range(4):
                    for gi_, (kt_, vx_, nk, btab, a) in enumerate(groups):
                        op('pe', lambda e, hh=hh, vx_=vx_, nk=nk, a=a, g=g, gi_=gi_: e.matmul(
                            pO[g][:nt, hh * 65:(hh + 1) * 65], lhsT=PT[a][g][:nk, hh * nt:(hh + 1) * nt],
                            rhs=vx_[:nk, g, :], start=(gi_ == 0), stop=(gi_ == len(groups) - 1)),
                            reads=[PT[a][g], vx_], writes=[pO[g]])
                pov = pO[g][:nt, 0:260].rearrange("p (h d) -> p h d", h=4)
                op('dve', lambda e, pov=pov, g=g: e.tensor_tensor(out=den[:nt, g * 4:(g + 1) * 4].unsqueeze(2),
                                                                    in0=pov[:, :, 64:65],
                                                                    in1=esink[:nt, g * 4:(g + 1) * 4].unsqueeze(2), op=ALU.add),
                   reads=[pO[g], esink], writes=[den])
                op('dve', lambda e, g=g: e.reciprocal(out=den[:nt, g * 4:(g + 1) * 4], in_=den[:nt, g * 4:(g + 1) * 4]),
                   reads=[den], writes=[den])
                op('dve', lambda e, pov=pov, g=g: e.tensor_tensor(
                    out=mixed[:nt, g * 256:(g + 1) * 256].rearrange("p (h d) -> p h d", h=4), in0=pov[:, :, 0:64],
                    in1=den[:nt, g * 4:(g + 1) * 4].unsqueeze(2).to_broadcast([nt, 4, 64]), op=ALU.mult),
                   reads=[pO[g], den], writes=[mixed])
                yield

        transpose_to(rqkb, T8b, nt, 8, 'act')
        RT = T8b
        if kind != 'meta':
            for h in range(4):
                op('pe', lambda e, h=h: e.matmul(pS[:nt, h * 128:h * 128 + nt], lhsT=RT[:, 4 + h, :nt], rhs=RT[:, h, :nt],
                                                 start=True, stop=True), reads=[RT], writes=[pS])
            op('dve', lambda e: e.tensor_tensor(out=SmT[:nt, :, :nt],
                                                in0=pS[:nt, :].rearrange("p (h q) -> p h q", h=4)[:, :, :nt],
                                                in1=rmask[:nt, :nt].unsqueeze(1).to_broadcast([nt, 4, nt]), op=ALU.mult),
               reads=[pS, rmask], writes=[SmT])
            for h in range(4):
                op('pe', lambda e, h=h: e.matmul(pO[0][:nt, h * 128:(h + 1) * 128], lhsT=SmT[:nt, h, :nt],
                                                 rhs=rvb[:nt, h * 128:(h + 1) * 128], start=True, stop=False),
                   reads=[SmT, rvb], writes=[pO[0]])
                op('pe', lambda e, h=h: e.matmul(pO[0][:nt, h * 128:(h + 1) * 128], lhsT=RT[:, h, :nt],
                                                 rhs=stbf[:, h * 128:(h + 1) * 128], start=False, stop=True),
                   reads=[RT, stbf], writes=[pO[0]])
        yield
        for h in range(4):
            op('pe', lambda e, h=h: e.matmul(pO[1][:, h * 128:(h + 1) * 128], lhsT=rqkb[:nt, 512 + h * 128:512 + (h + 1) * 128],
                                             rhs=rvb[:nt, h * 128:(h + 1) * 128], start=True, stop=True),
               reads=[rqkb, rvb], writes=[pO[1]])
        op('dve', lambda e: e.tensor_tensor(out=tmpS[:], in0=pO[1][:], in1=state[:], op=ALU.add),
           reads=[pO[1], state], writes=[tmpS])
        op('dve', lambda e: e.tensor_tensor(out=state[:].rearrange("p (h e) -> p h e", h=4),
                                            in0=tmpS[:].rearrange("p (h e) -> p h e", h=4),
                                            in1=gl[:, gli * 4:(gli + 1) * 4].unsqueeze(2).to_broadcast([128, 4, 128]),
                                            op=ALU.mult), reads=[tmpS, gl], writes=[state])
        op('act', lambda e: e.activation(out=stbf[:], in_=state[:], func=AF.Copy), reads=[state], writes=[stbf])
        yield
        if kind == 'meta':
            return

        for h in range(4):
            op('dve', lambda e, h=h: e.bn_stats(out=gst[:nt, h, :], in_=pO[0][:nt, h * 128:(h + 1) * 128]),
               reads=[pO[0]], writes=[gst])
        for h in range(4):
            op('dve', lambda e, h=h: e.bn_aggr(out=gmv[:nt, h, :], in_=gst[:nt, h, :]), reads=[gst], writes=[gmv])
        op('act', lambda e: e.activation(out=grs[:nt, :].unsqueeze(2), in_=gmv[:nt, :, 1:2], func=AF.Sqrt, bias=EPS, scale=1.0),
           reads=[gmv], writes=[grs])
        op('dve', lambda e: e.reciprocal(out=grs[:nt, :], in_=grs[:nt, :]), reads=[grs], writes=[grs])
        yv = yA[:nt, :].rearrange("p (h d) -> p h d", h=4)
        op('dve', lambda e: e.tensor_tensor(out=yv, in0=pO[0][:nt, :].rearrange("p (h d) -> p h d", h=4),
                                            in1=gmv[:nt, :, 0:1].to_broadcast([nt, 4, 128]), op=ALU.subtract),
           reads=[pO[0], gmv], writes=[yA])
        op('dve', lambda e: e.tensor_tensor(out=yv, in0=yv, in1=grs[:nt, :].unsqueeze(2).to_broadcast([nt, 4, 128]),
                                            op=ALU.mult), reads=[yA, grs], writes=[yA])
        op('dve', lambda e: e.tensor_tensor(out=yA[:nt, :], in0=yA[:nt, :], in1=gng[:nt, :], op=ALU.mult),
           reads=[yA, gng], writes=[yA])
        op('dve', lambda e: e.tensor_tensor(out=mixed[:nt, 512:1024], in0=yA[:nt, :], in1=gate[:nt, :], op=ALU.mult),
           reads=[yA, gate], writes=[mixed])
        yield

        transpose_to(mixed, T8a, nt, 8, 'dve')
        mm_full(pA, T8a, nt, w_out, 0, D)
        yield
        op('dve', lambda e: e.scalar_tensor_tensor(out=h1[:nt, :], in0=h_t[:nt, :], scalar=ALPHA, in1=pA[:nt, :],
                                                   op0=ALU.mult, op1=ALU.add), reads=[h_t, pA], writes=[h1])
        layernorm(h1, h1, 1, nt)
        yield

        op('act', lambda e: e.activation(out=B2a[:nt, :], in_=h1[:nt, :], func=AF.Copy), reads=[h1], writes=[B2a])
        transpose_to(B2a, T8a, nt, 8, 'dve')
        mm_full(pA, T8a, nt, w_q, 0, D)
        yield
        op('act', lambda e: e.activation(out=B2b[:nt, :], in_=pA[:nt, :], func=AF.Copy), reads=[pA], writes=[B2b])
        transpose_to(B2b, T8b, nt, 8, 'act')
        s_sb = F4c
        for hf in range(2):
            for pl in range(4):
                p_ = hf * 4 + pl
                op('pe', lambda e, p_=p_, pl=pl: e.matmul(pA[:nt, pl * 256:(pl + 1) * 256], lhsT=T8b[:, p_, :nt],
                                                          rhs=skb[:, p_, :], start=True, stop=True),
                   reads=[T8b, skb], writes=[pA])
            if hf == 0:
                op('act', lambda e: e.activation(out=s_sb[:nt, :], in_=pA[:nt, :], func=AF.Copy), reads=[pA], writes=[s_sb])
            else:
                op('dve', lambda e: e.tensor_copy(out=s_sb[:nt, :], in_=pA[:nt, :]), reads=[pA], writes=[s_sb])
            for gl_ in range(8):
                gi_ = hf * 8 + gl_
                sg = slice(gl_ * 128, (gl_ + 1) * 128)
                w_ = sw[gi_ % 2]
                op('dve', lambda e, gi_=gi_, sg=sg: e.max(out=tv[:nt, gi_, 0:8], in_=s_sb[:nt, sg]), reads=[s_sb], writes=[tv])
                op('dve', lambda e, gi_=gi_, sg=sg: e.max_index(out=tiu[:nt, gi_, 0:8], in_max=tv[:nt, gi_, 0:8],
                                                                in_values=s_sb[:nt, sg]), reads=[s_sb, tv], writes=[tiu])
                op('dve', lambda e, gi_=gi_, sg=sg, w_=w_: e.match_replace(out=w_[:nt, 0:128], in_to_replace=tv[:nt, gi_, 0:8],
                                                                           in_values=s_sb[:nt, sg], imm_value=-1e30),
                   reads=[s_sb, tv], writes=[w_])
                op('dve', lambda e, gi_=gi_, w_=w_: e.max(out=tv[:nt, gi_, 8:16], in_=w_[:nt, 0:128]), reads=[w_], writes=[tv])
                op('dve', lambda e, gi_=gi_, w_=w_: e.max_index(out=tiu[:nt, gi_, 8:16], in_max=tv[:nt, gi_, 8:16],
                                                                in_values=w_[:nt, 0:128]), reads=[w_, tv], writes=[tiu])
                if gl_ % 2 == 1:
                    yield
        op('dve', lambda e: e.tensor_copy(out=tif[:nt], in_=tiu[:nt]), reads=[tiu], writes=[tif])
        tv4 = tv[:nt].rearrange("p (a c) k -> p a c k", c=2)
        tif4 = tif[:nt].rearrange("p (a c) k -> p a c k", c=2)
        cv = cand[:nt, :].rearrange("p (a k m) -> p a k m", a=4, k=16)
        for hf in range(2):
            hs = slice(hf * 4, (hf + 1) * 4)
            op('dve', lambda e, hs=hs: e.tensor_tensor(out=cv, in0=tv4[:, hs, 0, :].unsqueeze(3).to_broadcast([nt, 4, 16, 16]),
                                                       in1=tv4[:, hs, 1, :].unsqueeze(2).to_broadcast([nt, 4, 16, 16]), op=ALU.add),
               reads=[tv], writes=[cand])
            for pl in range(4):
                p_ = hf * 4 + pl
                sg = slice(pl * 256, (pl + 1) * 256)
                w_ = sw[p_ % 2]
                op('dve', lambda e, p_=p_, sg=sg: e.max(out=t2v[:nt, p_, 0:8], in_=cand[:nt, sg]), reads=[cand], writes=[t2v])
                op('dve', lambda e, p_=p_, sg=sg: e.max_index(out=t2i[:nt, p_, 0:8], in_max=t2v[:nt, p_, 0:8],
                                                              in_values=cand[:nt, sg]), reads=[cand, t2v], writes=[t2i])
                op('dve', lambda e, p_=p_, sg=sg, w_=w_: e.match_replace(out=w_[:nt, :], in_to_replace=t2v[:nt, p_, 0:8],
                                                                         in_values=cand[:nt, sg], imm_value=-1e30),
                   reads=[cand, t2v], writes=[w_])
                op('dve', lambda e, p_=p_, w_=w_: e.max(out=t2v[:nt, p_, 8:16], in_=w_[:nt, :]), reads=[w_], writes=[t2v])
                op('dve', lambda e, p_=p_, w_=w_: e.max_index(out=t2i[:nt, p_, 8:16], in_max=t2v[:nt, p_, 8:16],
                                                              in_values=w_[:nt, :]), reads=[w_, t2v], writes=[t2i])
                if pl % 2 == 1:
                    yield
        t2if = t2i[:nt].rearrange("p a k -> p (a k)")
        op('dve', lambda e: e.tensor_single_scalar(out=k1u[:nt, :], in_=t2if, scalar=4, op=ALU.logical_shift_right),
           reads=[t2i], writes=[k1u])
        op('dve', lambda e: e.tensor_single_scalar(out=k2u[:nt, :], in_=t2if, scalar=15, op=ALU.bitwise_and),
           reads=[t2i], writes=[k2u])
        op('dve', lambda e: e.tensor_copy(out=k1f[:nt, :], in_=k1u[:nt, :]), reads=[k1u], writes=[k1f])
        op('dve', lambda e: e.tensor_copy(out=k2f[:nt, :], in_=k2u[:nt, :]), reads=[k2u], writes=[k2f])
        ohv = s_sb[:nt, :].rearrange("p (a k m) -> p a k m", a=4, k=16)
        iob = iota16[:nt, :].unsqueeze(1).unsqueeze(1).to_broadcast([nt, 4, 16, 16])
        for (kf, c_, dst_) in ((k1f, 0, i1s), (k2f, 1, i2s)):
            for hf in range(2):
                hs = slice(hf * 4, (hf + 1) * 4)
                js = slice(hf * 64, (hf + 1) * 64)
                op('dve', lambda e, kf=kf, js=js: e.tensor_tensor(
                    out=ohv, in0=iob,
                    in1=kf[:nt, js].rearrange("p (a k) -> p a k", a=4).unsqueeze(3).to_broadcast([nt, 4, 16, 16]),
                    op=ALU.is_equal), reads=[iota16, kf], writes=[s_sb])
                op('dve', lambda e, c_=c_, hs=hs: e.tensor_tensor(
                    out=ohv, in0=ohv, in1=tif4[:, hs, c_, :].unsqueeze(2).to_broadcast([nt, 4, 16, 16]), op=ALU.mult),
                   reads=[s_sb, tif], writes=[s_sb])
                op('dve', lambda e, dst_=dst_, js=js: e.tensor_reduce(out=dst_[:nt, js].rearrange("p (a k) -> p a k", a=4),
                                                                     in_=ohv, axis=AX.X, op=ALU.add),
                   reads=[s_sb], writes=[dst_])
                yield
        op('dve', lambda e: e.scalar_tensor_tensor(out=eif[:nt, :], in0=i1s[:nt, :], scalar=128.0, in1=i2s[:nt, :],
                                                   op0=ALU.mult, op1=ALU.add), reads=[i1s, i2s], writes=[eif])
        op('dve', lambda e: e.tensor_copy(out=eidx[:nt, :], in_=eif[:nt, :]), reads=[eif], writes=[eidx])
        gv = gsm[:nt, :].rearrange("p (a k) -> p a k", a=8)
        op('dve', lambda e: e.tensor_tensor(out=gv, in0=t2v[:nt], in1=t2v[:nt, :, 0:1].to_broadcast([nt, 8, 16]),
                                            op=ALU.subtract), reads=[t2v], writes=[gsm])
        op('act', lambda e: e.activation(out=gsm[:nt, :], in_=gsm[:nt, :], func=AF.Exp), reads=[gsm], writes=[gsm])
        op('dve', lambda e: e.tensor_reduce(out=gsum[:nt, :], in_=gv, axis=AX.X, op=ALU.add), reads=[gsm], writes=[gsum])
        op('dve', lambda e: e.reciprocal(out=gsum[:nt, :], in_=gsum[:nt, :]), reads=[gsum], writes=[gsum])
        op('dve', lambda e: e.tensor_tensor(out=gv, in0=gv, in1=gsum[:nt, :].unsqueeze(2).to_broadcast([nt, 8, 16]),
                                            op=ALU.mult), reads=[gsm, gsum], writes=[gsm])
        yield

    def uvphase(kind, ti, k):
        nt = 128 if kind == 'prompt' else DS
        h1 = H1[k % 2]
        eidx = EI[k % 2]
        gsm = GS[k % 2]
        NG = NJ // GRP
        NS = len(DT)

        def part1(g):
            s_ = g % NS
            dt_, ga_, gb_, gc_ = DT[s_], GA[s_], GB[s_], GC[s_]
            gs = slice(g * GRP, (g + 1) * GRP)
            for jj in range(GRP):
                j = g * GRP + jj
                b_ = uvb[j % NB]
                dma('pool', lambda e, j=j, b_=b_: e.indirect_dma_start(
                    out=b_[:, :], out_offset=None, in_=uvbf_d,
                    in_offset=bass.IndirectOffsetOnAxis(ap=eidx[:, j:j + 1], axis=0)), reads=[eidx, UVBF], writes=[b_])
                op('dve', lambda e, jj=jj, b_=b_, dt_=dt_: e.scalar_tensor_tensor(
                    out=b_[:nt, 0:D], in0=b_[:nt, 0:D], scalar=1.0, in1=h1[:nt, :], op0=ALU.mult, op1=ALU.mult,
                    accum_out=dt_[:nt, jj:jj + 1]), reads=[b_, h1], writes=[b_, dt_])
            op('dve', lambda e: e.scalar_tensor_tensor(out=ga_[:nt, :], in0=dt_[:nt, :], scalar=0.044715,
                                                       in1=dt_[:nt, :], op0=ALU.mult, op1=ALU.mult),
               reads=[dt_], writes=[ga_])
            op('dve', lambda e: e.scalar_tensor_tensor(out=ga_[:nt, :], in0=ga_[:nt, :], scalar=1.0,
                                                       in1=dt_[:nt, :], op0=ALU.add, op1=ALU.mult),
               reads=[ga_, dt_], writes=[ga_])
            op('dve', lambda e: e.scalar_tensor_tensor(out=gc_[:nt, :], in0=dt_[:nt, :], scalar=0.5, in1=gsm[:nt, gs],
                                                       op0=ALU.mult, op1=ALU.mult), reads=[dt_, gsm], writes=[gc_])

        def part1b(g):
            s_ = g % NS
            ga_, gb_ = GA[s_], GB[s_]
            op('act', lambda e: e.activation(out=gb_[:nt, :], in_=ga_[:nt, :], func=AF.Tanh,
                                             scale=0.7978845608028654), reads=[ga_], writes=[gb_])

        def part2(g):
            s_ = g % NS
            gb_, gc_, wt_ = GB[s_], GC[s_], WT[s_]
            op('dve', lambda e: e.scalar_tensor_tensor(out=wt_[:nt, :], in0=gb_[:nt, :], scalar=1.0, in1=gc_[:nt, :],
                                                       op0=ALU.add, op1=ALU.mult), reads=[gb_, gc_], writes=[wt_])
            for jj in range(GRP):
                j = g * GRP + jj
                d_ = dg[j % NDG]
                op('act', lambda e, jj=jj, d_=d_: e.activation(out=d_[:nt, :nt], in_=ident[:nt, :nt], func=AF.Copy,
                                                               scale=wt_[:nt, jj:jj + 1]), reads=[ident, wt_], writes=[d_])

        def part3(g):
            for jj in range(GRP):
                j = g * GRP + jj
                b_ = uvb[j % NB]
                d_ = dg[j % NDG]
                for hb in range(2):
                    op('pe', lambda e, j=j, b_=b_, d_=d_, hb=hb: e.matmul(
                        pV[:nt, hb * 512:(hb + 1) * 512], lhsT=d_[:nt, :nt], rhs=b_[:nt, D + hb * 512:D + (hb + 1) * 512],
                        start=(j == 0), stop=(j == NJ - 1)), reads=[b_, d_], writes=[pV])

        D1 = int(os.environ.get("K_D1", "0"))
        D2 = int(os.environ.get("K_D2", "1"))
        D3 = int(os.environ.get("K_D3", "2"))
        for g in range(NG + D3):
            if g < NG:
                part1(g)
            if 0 <= g - D1 < NG:
                part1b(g - D1)
            if 0 <= g - D2 < NG:
                part2(g - D2)
            if 0 <= g - D3 < NG:
                part3(g - D3)
            yield
        op('dve', lambda e: e.scalar_tensor_tensor(out=h1[:nt, :], in0=h1[:nt, :], scalar=ALPHA, in1=pV[:nt, :],
                                                   op0=ALU.mult, op1=ALU.add), reads=[h1, pV], writes=[h1])
        layernorm(h1, h1, 2, nt)
        if kind == 'prompt':
            dma('sp', lambda e: e.dma_start(out=yp_d[ti * 128:(ti + 1) * 128, :], in_=h1[:nt, :]), reads=[h1], store_src=h1)
        else:
            dma('sp', lambda e: e.dma_start(out=ys_d, in_=h1[:nt, :]), reads=[h1], store_src=h1)
            dma('sp', lambda e: e.dma_start(out=rss_d.rearrange("h d e -> d h e"), in_=state[:].rearrange("p (h e) -> p h e", h=4)),
                reads=[state], store_src=state)
        yield

    def run(g):
        n = 0
        for _ in g:
            n += 1
        return n

    def interleave(gens, ests):
        n = len(gens)
        prog = [0] * n
        alive = [True] * n
        while any(alive):
            best = None
            for i in range(n):
                if alive[i] and (best is None or prog[i] / ests[i] < prog[best] / ests[best]):
                    best = i
            try:
                next(gens[best])
                prog[best] += 1
            except StopIteration:
                alive[best] = False

    n_prompt = int(os.environ.get("K_NPROMPT", NTILES))
    tiles = [('prompt', i) for i in range(n_prompt)] + [('sample', 0)]
    K = len(tiles) - 1
    def first_fronts():
        yield from front('meta', 0, 0)
        n0 = [0]
        for _ in front(tiles[0][0], tiles[0][1], 0):
            n0[0] += 1
            yield
        fest_box.append(n0[0])
        if K >= 1:
            yield from front(tiles[1][0], tiles[1][1], 1)

    fest_box = []
    interleave([conv_gen(), first_fronts()], [len(chunks) + DEPTH, 170])
    fest = fest_box[0]
    nopipe = int(os.environ.get("K_NOPIPE", "0"))
    for k in range(K + 1):
        gens = [uvphase(tiles[k][0], tiles[k][1], k)]
        ests = [NJ // GRP + 3]
        if k + 1 <= K and k >= 1:
            gens.append(front(tiles[k + 1][0], tiles[k + 1][1], k + 1))
            ests.append(fest)
        if nopipe:
            for g_ in gens:
                run(g_)
        else:
            interleave(gens, ests)
    P.finish()
    P.emit()
    return nc


_NC_CACHE = {}


def _in_maps(inp):
    c = _host_consts()
    f = lambda a: np.ascontiguousarray(np.asarray(a, dtype=np.float32))
    rel_bias = f(inp['rel_bias'])
    shared = {
        "xm": f(inp['meta_tokens']),
        "ln0_g": f(inp['ln_in_g']), "ln0_b": f(inp['ln_in_b']),
        "ln1_g": f(inp['ln1_g'][0]), "ln1_b": f(inp['ln1_b'][0]),
        "ln2_g": f(inp['ln2_g'][0]), "ln2_b": f(inp['ln2_b'][0]),
        "w_in": f(inp['w_in'][0]), "w_out": f(inp['w_out'][0]), "wq": f(inp['peer_wq'][0]),
        "sinks": f(inp['attn_sinks'][0]), "gng": f(inp['ret_gn_g'][0]),
        "peer_u": f(inp['peer_u'][0]), "peer_v": f(inp['peer_v'][0]),
        "b_prev": f(rel_bias[c['bk_prev']].transpose(0, 2, 1)),
        "b_cur": f(rel_bias[c['bk_cur']].transpose(0, 2, 1)),
        "b_meta0": f(rel_bias[c['bk_meta0']].transpose(0, 2, 1)),
        "b_metaf": f(np.broadcast_to(rel_bias[15][None, :], (NMETA, 8))),
        "mask_prev": c['mask_prev'], "mask_cur": c['mask_cur'],
        "rot": c['rot'], "dqk": c['dqk'], "gl": c['gl'], "rmask": c['rmask'], "iota16": c['iota16'],
    }
    sk = f(inp['peer_subkeys'][0])
    skb = np.zeros((8, 128, 256), np.float32)
    for cc in range(2):
        skb[:, cc * 64:(cc + 1) * 64, cc * 128:(cc + 1) * 128] = sk[:, cc].transpose(0, 2, 1)
    shared["skb"] = skb
    maps = []
    for b in range(NCORES):
        m = dict(shared)
        m["xp"] = f(inp['x_prompt'][b])
        m["xs"] = f(inp['x_sample'][b])
        m["cmk"] = f(inp['cache_meta_k'][0, b]).reshape(NMETA, 128)
        m["cmv"] = f(inp['cache_meta_v'][0, b]).reshape(NMETA, 128)
        m["csk"] = f(inp['cache_swa_k'][0, b]).reshape(128, 128)
        m["csv"] = f(inp['cache_swa_v'][0, b]).reshape(128, 128)
        m["st0"] = f(inp['state_ret'][0, b])
        maps.append(m)
    return maps


def kernel(**inp):
    dbg = bool(int(os.environ.get("K_DEBUG", "0")))
    key = (dbg,) + tuple(os.environ.get(k_, "") for k_ in ("K_NPROMPT", "K_STAGE", "K_D1", "K_D2", "K_D3", "K_NDG", "K_NB", "K_GRP", "K_NOPIPE"))
    if key not in _NC_CACHE:
        _NC_CACHE[key] = build_program(dbg)
    nc = _NC_CACHE[key]
    maps = _in_maps(inp)
    res = run_bass_kernel_spmd(nc, maps, core_ids=list(range(NCORES)))
    r = res.results
    st = lambda k: np.stack([np.asarray(r[b][k]) for b in range(NCORES)])
    outs = (
        st("y_prompt"),
        st("y_sample"),
        st("meta_k").reshape(1, NCORES, NMETA, 2, 64),
        st("meta_v").reshape(1, NCORES, NMETA, 2, 64),
        st("swa_k").reshape(1, NCORES, 128, 2, 64),
        st("swa_v").reshape(1, NCORES, 128, 2, 64),
        st("ret_state").reshape(1, NCORES, 4, 128, 128),
        st("swa_k_s").reshape(1, NCORES, DS, 2, 64),
        st("swa_v_s").reshape(1, NCORES, DS, 2, 64),
        st("ret_state_s").reshape(1, NCORES, 4, 128, 128),
    )
    if dbg:
        kernel.debug = [{k: np.asarray(v) for k, v in r[b].items() if k.startswith("d_")} for b in range(NCORES)]
    return tuple(np.ascontiguousarray(o.astype(np.float32)) for o in outs)
```

```python
import contextlib
import math
import os
import numpy as np
import concourse.bass as bass
import concourse.mybir as mybir
from concourse.bass_utils import run_bass_kernel_spmd

F32 = mybir.dt.float32
BF16 = mybir.dt.bfloat16
I32 = mybir.dt.int32
U32 = mybir.dt.uint32
ALU = mybir.AluOpType
AF = mybir.ActivationFunctionType
AX = mybir.AxisListType

D = 1024
S = 2048
DS = 64
NMETA = 16
NTILES = S // 128
PW = 2816
NE = 16384
ALPHA = 2.0 ** 0.25
EPS = 1e-5
NCORES = 8
NJ = 128


class T:
    def __init__(self, t, name):
        self.t = t
        self.name = name
        self.w = None
        self.r = {}
        self.dsem = None
        self.dcnt = 0
        self.psem = None
        self.pcnt = 0

    def __getitem__(self, k):
        return self.t[k]


class Prog:
    ENG = ('pe', 'act', 'dve', 'pool', 'sp')

    def __init__(self, nc):
        self.nc = nc
        self.es = contextlib.ExitStack()
        self.q = {e: [] for e in self.ENG}
        self.cnt = {e: 0 for e in self.ENG}
        self.seen = {e: {} for e in self.ENG}
        self.esem = {}
        for e in self.ENG:
            self.esem[e] = self.es.enter_context(nc.semaphore('es_' + e))
        self.nsem = 0
        self.stores = []

    def sb(self, name, shape, dt=F32):
        return T(self.es.enter_context(self.nc.sbuf_tensor('sb_' + name, list(shape), dt)), name)

    def ps(self, name, shape, dt=F32):
        return T(self.es.enter_context(self.nc.psum_tensor('ps_' + name, list(shape), dt)), name)

    def newsem(self, name):
        self.nsem += 1
        return self.es.enter_context(self.nc.semaphore('d_%s_%d' % (name, self.nsem)))

    def _waits(self, eng, reads, writes):
        deps = {}

        def add(d):
            if d is None:
                return
            sem, val, src = d
            if eng == 'pe' and src == 'pe':
                return
            k = id(sem)
            if k not in deps or deps[k][1] < val:
                deps[k] = (sem, val)

        for t in reads:
            add(t.w)
        for t in writes:
            add(t.w)
            for d in t.r.values():
                add(d)
        out = []
        for k, (sem, val) in deps.items():
            if self.seen[eng].get(k, 0) >= val:
                continue
            self.seen[eng][k] = val
            out.append((sem, val))
        return out

    def op(self, eng, fn, reads=(), writes=()):
        waits = self._waits(eng, reads, writes)
        self.cnt[eng] += 1
        me = (self.esem[eng], self.cnt[eng], eng)
        self.q[eng].append((waits, fn, (self.esem[eng], 1)))
        for t in reads:
            t.r[id(me[0])] = me
        for t in writes:
            t.w = me
            t.r = {}

    def dma(self, eng, fn, reads=(), writes=(), store_src=None):
        waits = self._waits(eng, reads, writes)
        owner = writes[0] if writes else store_src
        if eng == 'pool':
            if owner.psem is None:
                owner.psem = self.newsem('p' + owner.name)
            owner.pcnt += 16
            me = (owner.psem, owner.pcnt, 'dma')
        else:
            if owner.dsem is None:
                owner.dsem = self.newsem(owner.name)
            owner.dcnt += 16
            me = (owner.dsem, owner.dcnt, 'dma')
        self.q[eng].append((waits, fn, (me[0], 16)))
        for t in reads:
            t.r[id(me[0])] = me
        for t in writes:
            t.w = me
            t.r = {}
        if store_src is not None:
            self.stores.append(me)

    def finish(self):
        deps = {}
        for sem, val, _ in self.stores:
            k = id(sem)
            if k not in deps or deps[k][1] < val:
                deps[k] = (sem, val)
        for e in self.ENG:
            if e != 'sp' and self.cnt[e] > 0:
                deps[id(self.esem[e])] = (self.esem[e], self.cnt[e])
        self.q['sp'].append((list(deps.values()), None, None))

    def emit(self):
        q = self.q
        import bisect
        eng_of = {id(self.esem[e]): e for e in self.ENG}
        targets = {e: set() for e in self.ENG}
        for name in self.ENG:
            for waits, fn, inc in q[name]:
                for sem, val in waits:
                    if id(sem) in eng_of:
                        targets[eng_of[id(sem)]].add(val)
        tsorted = {e: sorted(targets[e]) for e in self.ENG}

        def real(sem, val):
            e_ = eng_of.get(id(sem))
            if e_ is None:
                return val
            return bisect.bisect_right(tsorted[e_], val)

        def replay(name, e):
            vidx = 0
            for waits, fn, inc in q[name]:
                for sem, val in waits:
                    e.wait_ge(sem, real(sem, val))
                if fn is None:
                    continue
                ins = fn(e)
                if inc is not None:
                    if id(inc[0]) in eng_of:
                        vidx += 1
                        if vidx in targets[name]:
                            ins.then_inc(inc[0], 1)
                    else:
                        ins.then_inc(inc[0], inc[1])

        with self.nc.Block() as block:
            @block.sync
            def _(e):
                replay('sp', e)

            @block.scalar
            def _(e):
                replay('act', e)

            @block.vector
            def _(e):
                replay('dve', e)

            @block.gpsimd
            def _(e):
                replay('pool', e)

            @block.tensor
            def _(e):
                replay('pe', e)
        self.es.close()


def _t5_bucket(rel):
    nb = 16
    max_exact = 8
    n = np.abs(rel)
    v = (np.log(np.maximum(n, max_exact).astype(np.float32) / np.float32(max_exact))
         / np.float32(math.log(128 / 8)) * np.float32(nb - max_exact))
    large = np.minimum(max_exact + v.astype(np.int32), nb - 1)
    return np.where(rel > 0, nb, 0) + np.where(n < max_exact, n, large)


def _host_consts():
    c = {}
    k = np.arange(128)[:, None]
    q = np.arange(128)[None, :]
    c['bk_prev'] = _t5_bucket(k - 128 - q)
    c['bk_cur'] = _t5_bucket(k - q)
    m = np.arange(NMETA)[:, None]
    c['bk_meta0'] = _t5_bucket(m - NMETA - q)
    ok_prev = (q < 64) | (k >= 64)
    ok_cur = (q >= 64) | (k < 64)
    neg = np.float32(-30000.0)
    c['mask_prev'] = np.where(ok_prev, 0.0, neg).astype(np.float32)
    c['mask_cur'] = np.where(ok_cur, 0.0, neg).astype(np.float32)
    half = 64
    inv = (np.float32(10000.0) ** (-np.arange(half, dtype=np.float32) / np.float32(half))).astype(np.float32)
    rot = np.zeros((18, 128, 128), np.float32)
    for ti in range(18):
        if ti < 16:
            pos = NMETA + ti * 128 + np.arange(128)
        elif ti == 16:
            pos = NMETA + S + np.arange(128)
        else:
            pos = np.arange(128)
        ang = pos.astype(np.float32)[:, None] * inv[None, :]
        rot[ti, :, :64] = np.cos(ang)
        rot[ti, :, 64:] = np.sin(ang)
    c['rot'] = rot
    gam = 1.0 - 2.0 ** (-5.0 - np.arange(4, dtype=np.float64))
    t = np.arange(128, dtype=np.float64)[:, None]
    dq = gam[None, :] ** (t + 1.0)
    dk = gam[None, :] ** (-(t + 1.0)) * (128.0 ** -0.5)
    c['dqk'] = np.concatenate([dq, dk], axis=1).astype(np.float32)
    gl = np.zeros((128, 3, 4), np.float32)
    for li, L in enumerate((128, 64, 16)):
        gl[:, li, :] = (gam ** L)[None, :]
    c['gl'] = gl.reshape(128, 12)
    c['rmask'] = (k <= q).astype(np.float32)
    c['iota16'] = np.tile(np.arange(16, dtype=np.float32)[None, :], (128, 1))
    return c


def build_program(dbg=False):
    nc = bass.Bass("TRN2", target_bir_lowering=False)

    def din(name, shape, dt=F32):
        return nc.dram_tensor(name, list(shape), dt, kind="ExternalInput").ap()

    def dout(name, shape, dt=F32):
        return nc.dram_tensor(name, list(shape), dt, kind="ExternalOutput").ap()

    xp_d = din("xp", [S, D])
    xs_d = din("xs", [DS, D])
    xm_d = din("xm", [NMETA, D])
    cmk_d = din("cmk", [NMETA, 128])
    cmv_d = din("cmv", [NMETA, 128])
    csk_d = din("csk", [128, 128])
    csv_d = din("csv", [128, 128])
    st0_d = din("st0", [4, 128, 128])
    lng_d = [din("ln%d_g" % i, [D]) for i in range(3)]
    lnb_d = [din("ln%d_b" % i, [D]) for i in range(3)]
    win_d = din("w_in", [D, PW])
    wout_d = din("w_out", [D, D])
    wq_d = din("wq", [D, D])
    skb_d = din("skb", [8, 128, 256])
    sinks_d = din("sinks", [8])
    gng_d = din("gng", [512])
    u_d = din("peer_u", [NE, D])
    v_d = din("peer_v", [NE, D])
    bprev_d = din("b_prev", [128, 8, 128])
    bcur_d = din("b_cur", [128, 8, 128])
    bm0_d = din("b_meta0", [NMETA, 8, 128])
    bmf_d = din("b_metaf", [NMETA, 8])
    mprev_d = din("mask_prev", [128, 128])
    mcur_d = din("mask_cur", [128, 128])
    rot_d = din("rot", [18, 128, 128])
    dqk_d = din("dqk", [128, 8])
    gl_d = din("gl", [128, 12])
    rmask_d = din("rmask", [128, 128])
    iota16_d = din("iota16", [128, 16])

    yp_d = dout("y_prompt", [S, D])
    ys_d = dout("y_sample", [DS, D])
    mk_d = dout("meta_k", [NMETA, 128])
    mv_d = dout("meta_v", [NMETA, 128])
    sk_d = dout("swa_k", [128, 128])
    sv_d = dout("swa_v", [128, 128])
    rs_d = dout("ret_state", [4, 128, 128])
    sks_d = dout("swa_k_s", [DS, 128])
    svs_d = dout("swa_v_s", [DS, 128])
    rss_d = dout("ret_state_s", [4, 128, 128])
    uvbf_d = nc.dram_tensor("uvbf_scratch", [NE, 2 * D], BF16, kind="Internal").ap()
    dbg_d = {}
    if dbg:
        for nm, w in (("d_h", 1024), ("d_p", 2816), ("d_mixed", 1024), ("d_h1", 1024), ("d_eidx", 128),
                      ("d_g", 128), ("d_dots", 128), ("d_peer", 1024)):
            dbg_d[nm] = dout(nm, [128, w])

    P = Prog(nc)
    op, dma = P.op, P.dma

    w_in = P.sb("w_in", [128, 8, PW], BF16)
    w_out = P.sb("w_out", [128, 8, D], BF16)
    w_q = P.sb("w_q", [128, 8, D], BF16)
    skb = P.sb("skb", [128, 8, 256], BF16)
    lng = [P.sb("lng%d" % i, [128, D], BF16) for i in range(3)]
    lnb = [P.sb("lnb%d" % i, [128, D], BF16) for i in range(3)]
    gng = P.sb("gng", [128, 512])
    esink = P.sb("esink", [128, 8])
    b_prev = P.sb("b_prev", [128, 8, 128], BF16)
    b_cur = P.sb("b_cur", [128, 8, 128], BF16)
    b_mf = P.sb("b_mf", [NMETA, 8])
    rot = P.sb("rot", [128, 128])
    dqk = P.sb("dqk", [128, 8])
    gl = P.sb("gl", [128, 12])
    rmask = P.sb("rmask", [128, 128])
    iota16 = P.sb("iota16", [128, 16])
    ident = P.sb("ident", [128, 128], BF16)

    F4a = P.sb("F4a", [128, D])
    H1 = [P.sb("H1_%d" % i, [128, D]) for i in range(2)]
    F4c = P.sb("F4c", [128, D])
    B2a = P.sb("B2a", [128, D], BF16)
    B2b = P.sb("B2b", [128, D], BF16)
    T8a = P.sb("T8a", [128, 8, 128], BF16)
    T8b = P.sb("T8b", [128, 8, 128], BF16)
    lst = P.sb("lst", [128, 2, 6])
    lmv = P.sb("lmv", [128, 2])
    lrs = P.sb("lrs", [128, 1])
    kvf = P.sb("kvf", [128, 256])
    qbf = P.sb("qbf", [128, 512], BF16)
    kbf = P.sb("kbf", [128, 128], BF16)
    KT = [P.sb("KT%d" % i, [128, 128], BF16) for i in range(2)]
    VX = [P.sb("VX%d" % i, [128, 2, 65], BF16) for i in range(2)]
    KTm = P.sb("KTm", [128, NMETA], BF16)
    VXm = P.sb("VXm", [NMETA, 2, 65], BF16)
    QT = P.sb("QT", [128, 4, 128], BF16)
    sS = P.sb("sS", [128, 512])
    PT1 = [P.sb("PT%d" % a, [128, 512], BF16) for a in range(3)]
    PT = [[PT1[a], PT1[a]] for a in range(3)]
    den = P.sb("den", [128, 8])
    tA = P.sb("tA", [128, 512])
    tB = P.sb("tB", [128, 512])
    rqkb = P.sb("rqkb", [128, D], BF16)
    rvb = P.sb("rvb", [128, 512], BF16)
    gate = P.sb("gate", [128, 512], BF16)
    SmT = P.sb("SmT", [128, 4, 128], BF16)
    state = P.sb("state", [128, 512])
    stbf = P.sb("stbf", [128, 512], BF16)
    tmpS = tA
    gst = P.sb("gst", [128, 4, 6])
    gmv = P.sb("gmv", [128, 4, 2])
    grs = P.sb("grs", [128, 4])
    yA = tB
    cand = F4a
    b_m0 = H1[1]
    sw = [P.sb("sw%d" % i, [128, 256]) for i in range(2)]
    tv = P.sb("tv", [128, 16, 16])
    tiu = P.sb("tiu", [128, 16, 16], U32)
    tif = P.sb("tif", [128, 16, 16])
    t2v = P.sb("t2v", [128, 8, 16])
    t2i = P.sb("t2i", [128, 8, 16], U32)
    k1u = P.sb("k1u", [128, 128], U32)
    k2u = P.sb("k2u", [128, 128], U32)
    k1f = P.sb("k1f", [128, 128])
    k2f = P.sb("k2f", [128, 128])
    i1s = P.sb("i1s", [128, 128])
    i2s = P.sb("i2s", [128, 128])
    eif = P.sb("eif", [128, 128])
    EI = [P.sb("eidx%d" % i, [128, 128], I32) for i in range(2)]
    GS = [P.sb("gsm%d" % i, [128, 128]) for i in range(2)]
    gsum = P.sb("gsum", [128, 8])
    NB = int(os.environ.get("K_NB", "10"))
    GRP = int(os.environ.get("K_GRP", "2"))
    uvb = [P.sb("uvb%d" % i, [128, 2 * D], BF16) for i in range(NB)]
    NDG = int(os.environ.get("K_NDG", "8"))
    dg = [P.sb("dg%d" % i, [128, 128], BF16) for i in range(NDG)]
    DT = [P.sb("dt%d" % i, [128, GRP]) for i in range(4)]
    GA = [P.sb("gga%d" % i, [128, GRP]) for i in range(4)]
    GB = [P.sb("ggb%d" % i, [128, GRP]) for i in range(4)]
    GC = [P.sb("ggc%d" % i, [128, GRP]) for i in range(4)]
    WT = [P.sb("wt%d" % i, [128, GRP]) for i in range(4)]

    ptr = P.ps("ptr", [128, 8, 128], BF16)
    pA = P.ps("pA", [128, D])
    pV = P.ps("pV", [128, D])
    pS = P.ps("pS", [128, 512])
    pO = [P.ps("pO%d" % i, [128, 512]) for i in range(2)]

    def bc_part(ap, n=128):
        return ap.partition_broadcast(n)

    for i in range(3):
        dma('sp', lambda e, i=i: e.dma_start(out=F4c[:], in_=bc_part(lng_d[i])), writes=[F4c])
        op('dve', lambda e, i=i: e.tensor_scalar(out=lng[i][:], in0=F4c[:], scalar1=-1.0, scalar2=None, op0=ALU.add),
           reads=[F4c], writes=[lng[i]])
        dma('sp', lambda e, i=i: e.dma_start(out=F4a[:], in_=bc_part(lnb_d[i])), writes=[F4a])
        op('dve', lambda e, i=i: e.tensor_copy(out=lnb[i][:], in_=F4a[:]), reads=[F4a], writes=[lnb[i]])
    dma('sp', lambda e: e.dma_start(out=gng[:], in_=bc_part(gng_d)), writes=[gng])
    dma('sp', lambda e: e.dma_start(out=esink[:], in_=bc_part(sinks_d)), writes=[esink])
    op('act', lambda e: e.activation(out=esink[:], in_=esink[:], func=AF.Exp), reads=[esink], writes=[esink])
    dma('sp', lambda e: e.dma_start(out=dqk[:], in_=dqk_d), writes=[dqk])
    dma('sp', lambda e: e.dma_start(out=gl[:], in_=gl_d), writes=[gl])
    dma('sp', lambda e: e.dma_start(out=rmask[:], in_=rmask_d), writes=[rmask])
    dma('sp', lambda e: e.dma_start(out=iota16[:], in_=iota16_d), writes=[iota16])
    dma('sp', lambda e: e.dma_start(out=F4a[:].rearrange("p (h q) -> p h q", h=8), in_=bprev_d), writes=[F4a])
    dma('sp', lambda e: e.dma_start(out=F4c[:].rearrange("p (h q) -> p h q", h=8), in_=bcur_d), writes=[F4c])
    dma('sp', lambda e: e.dma_start(out=b_mf[:], in_=bmf_d), writes=[b_mf])
    dma('sp', lambda e: e.dma_start(out=tA[:, 0:128], in_=mprev_d), writes=[tA])
    dma('sp', lambda e: e.dma_start(out=tB[:, 0:128], in_=mcur_d), writes=[tB])
    op('dve', lambda e: e.tensor_tensor(out=b_prev[:], in0=F4a[:].rearrange("p (h q) -> p h q", h=8),
                                        in1=tA[:, 0:128].unsqueeze(1).to_broadcast([128, 8, 128]), op=ALU.add),
       reads=[F4a, tA], writes=[b_prev])
    op('dve', lambda e: e.tensor_tensor(out=b_cur[:], in0=F4c[:].rearrange("p (h q) -> p h q", h=8),
                                        in1=tB[:, 0:128].unsqueeze(1).to_broadcast([128, 8, 128]), op=ALU.add),
       reads=[F4c, tB], writes=[b_cur])
    iota_i = EI[0]
    identf = sS
    op('pool', lambda e: e.iota(iota_i[:], pattern=[[1, 128]], base=0, channel_multiplier=-1), writes=[iota_i])
    op('dve', lambda e: e.tensor_copy(out=identf[:, 0:128], in_=iota_i[:]), reads=[iota_i], writes=[identf])
    op('dve', lambda e: e.tensor_scalar(out=ident[:], in0=identf[:, 0:128], scalar1=0.0, scalar2=None, op0=ALU.is_equal),
       reads=[identf], writes=[ident])
    for t_ in VX + [VXm]:
        op('dve', lambda e, t_=t_: e.memset(t_[:], 1.0), writes=[t_])
    for t_ in EI:
        op('dve', lambda e, t_=t_: e.memset(t_[:], 0), writes=[t_])
    op('dve', lambda e: e.memset(state[:], 0.0), writes=[state])
    op('dve', lambda e: e.memset(stbf[:], 0.0), writes=[stbf])

    stg = [F4a, H1[0], H1[1], F4c]
    nst = [0]

    def load_cast(dst_fn, src_ap, ncols, dst_tile, permq=False):
        s_ = stg[nst[0] % len(stg)]
        eng = 'act' if (nst[0] % 2 == 0) else 'dve'
        nst[0] += 1
        dma('sp', lambda e: e.dma_start(out=s_[:, 0:ncols], in_=src_ap), writes=[s_])
        if permq:
            def f(e):
                return e.tensor_copy(out=dst_fn(0, 512).rearrange("p (h g d) -> p h g d", h=4, g=2),
                                     in_=s_[:, 0:512].rearrange("p (g h d) -> p h g d", g=2, h=4))
            op('dve', f, reads=[s_], writes=[dst_tile])
            if eng == 'act':
                op('act', lambda e: e.activation(out=dst_fn(512, ncols), in_=s_[:, 512:ncols], func=AF.Copy),
                   reads=[s_], writes=[dst_tile])
            else:
                op('dve', lambda e: e.tensor_copy(out=dst_fn(512, ncols), in_=s_[:, 512:ncols]),
                   reads=[s_], writes=[dst_tile])
        else:
            if eng == 'act':
                op('act', lambda e: e.activation(out=dst_fn(0, ncols), in_=s_[:, 0:ncols], func=AF.Copy),
                   reads=[s_], writes=[dst_tile])
            else:
                op('dve', lambda e: e.tensor_copy(out=dst_fn(0, ncols), in_=s_[:, 0:ncols]),
                   reads=[s_], writes=[dst_tile])

    for kc in range(8):
        for (c0, c1) in ((0, 1024), (1024, 2048), (2048, PW)):
            load_cast(lambda a, b, kc=kc, c0=c0: w_in[:, kc, c0 + a:c0 + b],
                      win_d[kc * 128:(kc + 1) * 128, c0:c1], c1 - c0, w_in, permq=(c0 == 0))
    for kc in range(8):
        load_cast(lambda a, b, kc=kc: w_out[:, kc, a:b], wout_d[kc * 128:(kc + 1) * 128, :], D, w_out)
    for kc in range(8):
        load_cast(lambda a, b, kc=kc: w_q[:, kc, a:b], wq_d[kc * 128:(kc + 1) * 128, :], D, w_q)
    for hp in range(2):
        s_ = stg[hp]
        dma('sp', lambda e, hp=hp, s_=s_: e.dma_start(out=s_[:, 0:1024].rearrange("p (k n) -> p k n", k=4),
                                                      in_=skb_d[hp * 4:(hp + 1) * 4].rearrange("k p n -> p k n")),
            writes=[s_])
        op('dve', lambda e, hp=hp, s_=s_: e.tensor_copy(out=skb[:, hp * 4:(hp + 1) * 4, :],
                                                        in_=s_[:, 0:1024].rearrange("p (k n) -> p k n", k=4)),
           reads=[s_], writes=[skb])

    UVBF = T(None, "uvbf")
    chunks = []
    for (src_d, off) in ((u_d, 0), (v_d, D)):
        src_v = src_d.rearrange("(p r) d -> p r d", p=128)
        dst_v = uvbf_d.rearrange("(p r) d -> p r d", p=128)[:, :, off:off + D]
        for c in range(128):
            chunks.append((src_v, dst_v, c))
    NSTG = NB // 2
    DEPTH = NSTG - 1

    def conv_load(i):
        src_v, dst_v, c = chunks[i]
        s_ = uvb[i % NSTG]
        dma('sp', lambda e: e.dma_start(out=s_.t.bitcast(F32)[:, :], in_=src_v[:, c, :]), writes=[s_])

    def conv_cast_store(i):
        src_v, dst_v, c = chunks[i]
        s_ = uvb[i % NSTG]
        b_ = uvb[NSTG + i % NSTG]
        eng = ('act', 'dve', 'pool')[i % 3]
        if eng == 'act':
            op('act', lambda e: e.activation(out=b_[:, 0:D], in_=s_.t.bitcast(F32)[:, :], func=AF.Copy), reads=[s_], writes=[b_])
        else:
            op(eng, lambda e: e.tensor_copy(out=b_[:, 0:D], in_=s_.t.bitcast(F32)[:, :]), reads=[s_], writes=[b_])
        dma('sp', lambda e: e.dma_start(out=dst_v[:, c, :], in_=b_[:, 0:D]), reads=[b_], writes=[UVBF])

    def conv_gen():
        for i in range(len(chunks) + DEPTH):
            if i < len(chunks):
                conv_load(i)
            if i >= DEPTH:
                conv_cast_store(i - DEPTH)
            yield

    dma('sp', lambda e: e.dma_start(out=b_m0[:NMETA, :].rearrange("p (h q) -> p h q", h=8), in_=bm0_d), writes=[b_m0])

    def layernorm(src, dst, gi, nt):
        for c in range(2):
            op('dve', lambda e, c=c: e.bn_stats(out=lst[:nt, c, :], in_=src[:nt, c * 512:(c + 1) * 512]),
               reads=[src], writes=[lst])
        op('dve', lambda e: e.bn_aggr(out=lmv[:nt, :], in_=lst[:nt].rearrange("p a b -> p (a b)")),
           reads=[lst], writes=[lmv])
        op('act', lambda e: e.activation(out=lrs[:nt, :], in_=lmv[:nt, 1:2], func=AF.Sqrt, bias=EPS, scale=1.0),
           reads=[lmv], writes=[lrs])
        op('dve', lambda e: e.reciprocal(out=lrs[:nt, :], in_=lrs[:nt, :]), reads=[lrs], writes=[lrs])
        op('dve', lambda e: e.tensor_scalar(out=dst[:nt, :], in0=src[:nt, :], scalar1=lmv[:nt, 0:1],
                                            scalar2=lrs[:nt, 0:1], op0=ALU.subtract, op1=ALU.mult),
           reads=[src, lmv, lrs], writes=[dst])
        op('dve', lambda e: e.scalar_tensor_tensor(out=dst[:nt, :], in0=lng[gi][:nt, :], scalar=1.0, in1=dst[:nt, :],
                                                   op0=ALU.add, op1=ALU.mult), reads=[dst, lng[gi]], writes=[dst])
        op('dve', lambda e: e.tensor_tensor(out=dst[:nt, :], in0=dst[:nt, :], in1=lnb[gi][:nt, :], op=ALU.add),
           reads=[dst, lnb[gi]], writes=[dst])

    def transpose_to(src_bf, dstT, nt, nblk=8, evac='dve'):
        for j in range(nblk):
            op('pe', lambda e, j=j: e.transpose(out=ptr[:, j, :nt], in_=src_bf[:nt, j * 128:(j + 1) * 128],
                                                identity=ident[:nt, :nt]),
               reads=[src_bf, ident], writes=[ptr])
        if evac == 'dve':
            op('dve', lambda e: e.tensor_copy(out=dstT[:, 0:nblk, :nt], in_=ptr[:, 0:nblk, :nt]),
               reads=[ptr], writes=[dstT])
        else:
            op('act', lambda e: e.activation(out=dstT[:, 0:nblk, :nt], in_=ptr[:, 0:nblk, :nt], func=AF.Copy),
               reads=[ptr], writes=[dstT])

    def mm_full(out_ps, lhsT8, nt, rhs_w, c0, c1):
        c = c0
        while c < c1:
            ce = min(c + 512, c1)
            for kc in range(8):
                op('pe', lambda e, kc=kc, c=c, ce=ce: e.matmul(out_ps[:nt, c - c0:ce - c0], lhsT=lhsT8[:, kc, :nt],
                                                              rhs=rhs_w[:, kc, c:ce], start=(kc == 0), stop=(kc == 7)),
                   reads=[lhsT8, rhs_w], writes=[out_ps])
            c = ce

    def dbg_dump(name, tile_, ncols, nt):
        if dbg and name in dbg_d:
            dma('sp', lambda e: e.dma_start(out=dbg_d[name][:nt, 0:ncols], in_=tile_[:nt, 0:ncols]),
                reads=[tile_], store_src=tile_)

    def tile_params(kind, ti):
        if kind == 'meta':
            return NMETA, 17, 2, xm_d
        if kind == 'prompt':
            return 128, ti, 0, xp_d[ti * 128:(ti + 1) * 128, :]
        return DS, 16, 1, xs_d

    def sample_setup():
        dma('sp', lambda e: e.dma_start(out=rs_d.rearrange("h d e -> d h e"), in_=state[:].rearrange("p (h e) -> p h e", h=4)),
            reads=[state], store_src=state)
        dma('sp', lambda e: e.dma_start(out=kvf[:, 0:128], in_=csk_d), writes=[kvf])
        dma('sp', lambda e: e.dma_start(out=kvf[:, 128:256], in_=csv_d), writes=[kvf])
        op('act', lambda e: e.activation(out=kbf[:, :], in_=kvf[:, 0:128], func=AF.Copy), reads=[kvf], writes=[kbf])
        op('dve', lambda e: e.tensor_copy(out=VX[0][:, :, 0:64], in_=kvf[:, 128:256].rearrange("p (g d) -> p g d", g=2)),
           reads=[kvf], writes=[VX[0]])
        op('pe', lambda e: e.transpose(out=ptr[:, 0, :], in_=kbf[:, :], identity=ident[:, :]), reads=[kbf, ident], writes=[ptr])
        op('dve', lambda e: e.tensor_copy(out=KT[0][:, :], in_=ptr[:, 0, :]), reads=[ptr], writes=[KT[0]])
        dma('sp', lambda e: e.dma_start(out=kvf[:NMETA, 0:128], in_=cmk_d), writes=[kvf])
        dma('sp', lambda e: e.dma_start(out=kvf[:NMETA, 128:256], in_=cmv_d), writes=[kvf])
        op('act', lambda e: e.activation(out=kbf[:NMETA, :], in_=kvf[:NMETA, 0:128], func=AF.Copy), reads=[kvf], writes=[kbf])
        op('dve', lambda e: e.tensor_copy(out=VXm[:NMETA, :, 0:64], in_=kvf[:NMETA, 128:256].rearrange("p (g d) -> p g d", g=2)),
           reads=[kvf], writes=[VXm])
        op('pe', lambda e: e.transpose(out=ptr[:, 0, :NMETA], in_=kbf[:NMETA, :], identity=ident[:NMETA, :NMETA]),
           reads=[kbf, ident], writes=[ptr])
        op('dve', lambda e: e.tensor_copy(out=KTm[:, :NMETA], in_=ptr[:, 0, :NMETA]), reads=[ptr], writes=[KTm])
        dma('sp', lambda e: e.dma_start(out=state[:].rearrange("p (h e) -> p h e", h=4), in_=st0_d.rearrange("h d e -> d h e")),
            writes=[state])
        op('act', lambda e: e.activation(out=stbf[:], in_=state[:], func=AF.Copy), reads=[state], writes=[stbf])

    def front(kind, ti, k):
        nt, rti, gli, x_src = tile_params(kind, ti)
        gsm = GS[k % 2]
        if kind == 'sample':
            sample_setup()
            yield
        h_t = F4a
        h1 = H1[k % 2]
        eidx = EI[k % 2]
        dma('sp', lambda e: e.dma_start(out=h_t[:nt, :], in_=x_src), writes=[h_t])
        dma('sp', lambda e: e.dma_start(out=rot[:, :], in_=rot_d[rti]), writes=[rot])
        layernorm(h_t, h_t, 0, nt)
        yield
        op('act', lambda e: e.activation(out=B2a[:nt, :], in_=h_t[:nt, :], func=AF.Copy), reads=[h_t], writes=[B2a])
        transpose_to(B2a, T8a, nt, 8, 'dve')
        hT = T8a
        yield
        mm_full(pA, hT, nt, w_in, 0, 768)
        yield
        op('act', lambda e: e.activation(out=qbf[:nt, :], in_=pA[:nt, 0:512], func=AF.Copy), reads=[pA], writes=[qbf])
        op('dve', lambda e: e.tensor_copy(out=kvf[:nt, :], in_=pA[:nt, 512:768]), reads=[pA], writes=[kvf])
        if kind == 'meta':
            dma('sp', lambda e: e.dma_start(out=mk_d, in_=kvf[:nt, 0:128]), reads=[kvf], store_src=kvf)
            dma('sp', lambda e: e.dma_start(out=mv_d, in_=kvf[:nt, 128:256]), reads=[kvf], store_src=kvf)
        elif kind == 'prompt' and ti == NTILES - 1:
            dma('sp', lambda e: e.dma_start(out=sk_d, in_=kvf[:nt, 0:128]), reads=[kvf], store_src=kvf)
            dma('sp', lambda e: e.dma_start(out=sv_d, in_=kvf[:nt, 128:256]), reads=[kvf], store_src=kvf)
        elif kind == 'sample':
            dma('sp', lambda e: e.dma_start(out=sks_d, in_=kvf[:nt, 0:128]), reads=[kvf], store_src=kvf)
            dma('sp', lambda e: e.dma_start(out=svs_d, in_=kvf[:nt, 128:256]), reads=[kvf], store_src=kvf)
        if kind == 'meta':
            kt_cur, vx_cur = KTm, VXm
        elif kind == 'prompt':
            kt_cur, vx_cur = KT[ti % 2], VX[ti % 2]
        else:
            kt_cur, vx_cur = KT[1], VX[1]
        op('act', lambda e: e.activation(out=kbf[:nt, :], in_=kvf[:nt, 0:128], func=AF.Copy), reads=[kvf], writes=[kbf])
        op('dve', lambda e: e.tensor_copy(out=vx_cur[:nt, :, 0:64],
                                          in_=kvf[:nt, 128:256].rearrange("p (g d) -> p g d", g=2)),
           reads=[kvf], writes=[vx_cur])
        op('pe', lambda e: e.transpose(out=ptr[:, 0, :nt], in_=kbf[:nt, :], identity=ident[:nt, :nt]),
           reads=[kbf, ident], writes=[ptr])
        op('dve', lambda e: e.tensor_copy(out=kt_cur[:, :nt], in_=ptr[:, 0, :nt]), reads=[ptr], writes=[kt_cur])
        yield
        mm_full(pA, hT, nt, w_in, 768, 1792)
        yield
        rqk = F4c
        cos_b = rot[:nt, 0:64].unsqueeze(1).to_broadcast([nt, 8, 64])
        sin_b = rot[:nt, 64:128].unsqueeze(1).to_broadcast([nt, 8, 64])
        pv = pA[:nt, :].rearrange("p (a t d) -> p a t d", a=8, t=2)
        x1 = pv[:, :, 0, :]
        x2 = pv[:, :, 1, :]
        tAv = tA[:nt, :].rearrange("p (a d) -> p a d", a=8)
        tBv = tB[:nt, :].rearrange("p (a d) -> p a d", a=8)
        rv4 = rqk[:nt, :].rearrange("p (a t d) -> p a t d", a=8, t=2)
        op('dve', lambda e: e.tensor_tensor(out=tAv, in0=x1, in1=cos_b, op=ALU.mult), reads=[pA, rot], writes=[tA])
        op('dve', lambda e: e.tensor_tensor(out=tBv, in0=x2, in1=sin_b, op=ALU.mult), reads=[pA, rot], writes=[tB])
        op('dve', lambda e: e.tensor_tensor(out=rv4[:, :, 0, :], in0=tAv, in1=tBv, op=ALU.subtract),
           reads=[tA, tB], writes=[rqk])
        op('dve', lambda e: e.tensor_tensor(out=tAv, in0=x1, in1=sin_b, op=ALU.mult), reads=[pA, rot], writes=[tA])
        op('dve', lambda e: e.tensor_tensor(out=tBv, in0=x2, in1=cos_b, op=ALU.mult), reads=[pA, rot], writes=[tB])
        op('dve', lambda e: e.tensor_tensor(out=rv4[:, :, 1, :], in0=tAv, in1=tBv, op=ALU.add),
           reads=[tA, tB], writes=[rqk])
        op('dve', lambda e: e.tensor_tensor(out=rqkb[:nt, :].rearrange("p (a d) -> p a d", a=8),
                                            in0=rqk[:nt, :].rearrange("p (a d) -> p a d", a=8),
                                            in1=dqk[:nt, :].unsqueeze(2).to_broadcast([nt, 8, 128]), op=ALU.mult),
           reads=[rqk, dqk], writes=[rqkb])
        yield
        mm_full(pA, hT, nt, w_in, 1792, PW)
        op('act', lambda e: e.activation(out=rvb[:nt, :], in_=pA[:nt, 0:512], func=AF.Copy), reads=[pA], writes=[rvb])
        if kind != 'meta':
            op('act', lambda e: e.activation(out=gate[:nt, :], in_=pA[:nt, 512:1024], func=AF.Silu),
               reads=[pA], writes=[gate])
        yield
        mixed = B2b
        if kind != 'meta':
            transpose_to(qbf, QT, nt, 4, 'act')
            if kind == 'prompt':
                groups = []
                if ti > 0:
                    groups.append((KT[(ti - 1) % 2], VX[(ti - 1) % 2], 128, b_prev, 0))
                groups.append((kt_cur, vx_cur, nt, b_cur, 1))
                groups.append((KTm, VXm, NMETA, b_m0 if ti == 0 else b_mf, 2))
            else:
                groups = [(KT[0], VX[0], 128, b_prev, 0), (kt_cur, vx_cur, nt, b_cur, 1), (KTm, VXm, NMETA, b_mf, 2)]
            for g in range(2):
                pr = slice(g * 64, (g + 1) * 64)
                for (kt_, vx_, nk, btab, a) in groups:
                    op('pe', lambda e, kt_=kt_, nk=nk, pr=pr: e.matmul(
                        pS[:nk, 0:4 * nt].rearrange("p (h q) -> p h q", h=4), lhsT=kt_[pr, :nk], rhs=QT[pr, :, :nt],
                        start=True, stop=True), reads=[kt_, QT], writes=[pS])
                    if btab is b_mf:
                        bias_ap = btab[:nk, g * 4:(g + 1) * 4].unsqueeze(2).to_broadcast([nk, 4, nt])
                    elif btab is b_m0:
                        bias_ap = b_m0[:nk, :].rearrange("p (h q) -> p h q", h=8)[:, g * 4:(g + 1) * 4, :nt]
                    else:
                        bias_ap = btab[:nk, g * 4:(g + 1) * 4, :nt]
                    op('dve', lambda e, nk=nk, bias_ap=bias_ap: e.scalar_tensor_tensor(
                        out=sS[:nk, 0:4 * nt].rearrange("p (h q) -> p h q", h=4),
                        in0=pS[:nk, 0:4 * nt].rearrange("p (h q) -> p h q", h=4), scalar=0.125,
                        in1=bias_ap, op0=ALU.mult, op1=ALU.add),
                        reads=[pS, btab], writes=[sS])
                    op('act', lambda e, nk=nk, a=a, g=g: e.activation(out=PT[a][g][:nk, 0:4 * nt], in_=sS[:nk, 0:4 * nt],
                                                                       func=AF.Exp), reads=[sS], writes=[PT[a][g]])
                yield
                for hh in range(4):
                    for gi_, (kt_, vx_, nk, btab, a) in enumerate(groups):
                        op('pe', lambda e, hh=hh, vx_=vx_, nk=nk, a=a, g=g, gi_=gi_: e.matmul(
                            pO[g][:nt, hh * 65:(hh + 1) * 65], lhsT=PT[a][g][:nk, hh * nt:(hh + 1) * nt],
                            rhs=vx_[:nk, g, :], start=(gi_ == 0), stop=(gi_ == len(groups) - 1)),
                            reads=[PT[a][g], vx_], writes=[pO[g]])
                pov = pO[g][:nt, 0:260].rearrange("p (h d) -> p h d", h=4)
                op('dve', lambda e, pov=pov, g=g: e.tensor_tensor(out=den[:nt, g * 4:(g + 1) * 4].unsqueeze(2),
                                                                    in0=pov[:, :, 64:65],
                                                                    in1=esink[:nt, g * 4:(g + 1) * 4].unsqueeze(2), op=ALU.add),
                   reads=[pO[g], esink], writes=[den])
                op('dve', lambda e, g=g: e.reciprocal(out=den[:nt, g * 4:(g + 1) * 4], in_=den[:nt, g * 4:(g + 1) * 4]),
                   reads=[den], writes=[den])
                op('dve', lambda e, pov=pov, g=g: e.tensor_tensor(
                    out=mixed[:nt, g * 256:(g + 1) * 256].rearrange("p (h d) -> p h d", h=4), in0=pov[:, :, 0:64],
                    in1=den[:nt, g * 4:(g + 1) * 4].unsqueeze(2).to_broadcast([nt, 4, 64]), op=ALU.mult),
                   reads=[pO[g], den], writes=[mixed])
                yield

        transpose_to(rqkb, T8b, nt, 8, 'act')
        RT = T8b
        if kind != 'meta':
            for h in range(4):
                op('pe', lambda e, h=h: e.matmul(pS[:nt, h * 128:h * 128 + nt], lhsT=RT[:, 4 + h, :nt], rhs=RT[:, h, :nt],
                                                 start=True, stop=True), reads=[RT], writes=[pS])
            op('dve', lambda e: e.tensor_tensor(out=SmT[:nt, :, :nt],
                                                in0=pS[:nt, :].rearrange("p (h q) -> p h q", h=4)[:, :, :nt],
                                                in1=rmask[:nt, :nt].unsqueeze(1).to_broadcast([nt, 4, nt]), op=ALU.mult),
               reads=[pS, rmask], writes=[SmT])
            for h in range(4):
                op('pe', lambda e, h=h: e.matmul(pO[0][:nt, h * 128:(h + 1) * 128], lhsT=SmT[:nt, h, :nt],
                                                 rhs=rvb[:nt, h * 128:(h + 1) * 128], start=True, stop=False),
                   reads=[SmT, rvb], writes=[pO[0]])
                op('pe', lambda e, h=h: e.matmul(pO[0][:nt, h * 128:(h + 1) * 128], lhsT=RT[:, h, :nt],
                                                 rhs=stbf[:, h * 128:(h + 1) * 128], start=False, stop=True),
                   reads=[RT, stbf], writes=[pO[0]])
        yield
        for h in range(4):
            op('pe', lambda e, h=h: e.matmul(pO[1][:, h * 128:(h + 1) * 128], lhsT=rqkb[:nt, 512 + h * 128:512 + (h + 1) * 128],
                                             rhs=rvb[:nt, h * 128:(h + 1) * 128], start=True, stop=True),
               reads=[rqkb, rvb], writes=[pO[1]])
        op('dve', lambda e: e.tensor_tensor(out=tmpS[:], in0=pO[1][:], in1=state[:], op=ALU.add),
           reads=[pO[1], state], writes=[tmpS])
        op('dve', lambda e: e.tensor_tensor(out=state[:].rearrange("p (h e) -> p h e", h=4),
                                            in0=tmpS[:].rearrange("p (h e) -> p h e", h=4),
                                            in1=gl[:, gli * 4:(gli + 1) * 4].unsqueeze(2).to_broadcast([128, 4, 128]),
                                            op=ALU.mult), reads=[tmpS, gl], writes=[state])
        op('act', lambda e: e.activation(out=stbf[:], in_=state[:], func=AF.Copy), reads=[state], writes=[stbf])
        yield
        if kind == 'meta':
            return

        for h in range(4):
            op('dve', lambda e, h=h: e.bn_stats(out=gst[:nt, h, :], in_=pO[0][:nt, h * 128:(h + 1) * 128]),
               reads=[pO[0]], writes=[gst])
        for h in range(4):
            op('dve', lambda e, h=h: e.bn_aggr(out=gmv[:nt, h, :], in_=gst[:nt, h, :]), reads=[gst], writes=[gmv])
        op('act', lambda e: e.activation(out=grs[:nt, :].unsqueeze(2), in_=gmv[:nt, :, 1:2], func=AF.Sqrt, bias=EPS, scale=1.0),
           reads=[gmv], writes=[grs])
        op('dve', lambda e: e.reciprocal(out=grs[:nt, :], in_=grs[:nt, :]), reads=[grs], writes=[grs])
        yv = yA[:nt, :].rearrange("p (h d) -> p h d", h=4)
        op('dve', lambda e: e.tensor_tensor(out=yv, in0=pO[0][:nt, :].rearrange("p (h d) -> p h d", h=4),
                                            in1=gmv[:nt, :, 0:1].to_broadcast([nt, 4, 128]), op=ALU.subtract),
           reads=[pO[0], gmv], writes=[yA])
        op('dve', lambda e: e.tensor_tensor(out=yv, in0=yv, in1=grs[:nt, :].unsqueeze(2).to_broadcast([nt, 4, 128]),
                                            op=ALU.mult), reads=[yA, grs], writes=[yA])
        op('dve', lambda e: e.tensor_tensor(out=yA[:nt, :], in0=yA[:nt, :], in1=gng[:nt, :], op=ALU.mult),
           reads=[yA, gng], writes=[yA])
        op('dve', lambda e: e.tensor_tensor(out=mixed[:nt, 512:1024], in0=yA[:nt, :], in1=gate[:nt, :], op=ALU.mult),
           reads=[yA, gate], writes=[mixed])
        yield

        transpose_to(mixed, T8a, nt, 8, 'dve')
        mm_full(pA, T8a, nt, w_out, 0, D)
        yield
        op('dve', lambda e: e.scalar_tensor_tensor(out=h1[:nt, :], in0=h_t[:nt, :], scalar=ALPHA, in1=pA[:nt, :],
                                                   op0=ALU.mult, op1=ALU.add), reads=[h_t, pA], writes=[h1])
        layernorm(h1, h1, 1, nt)
        yield

        op('act', lambda e: e.activation(out=B2a[:nt, :], in_=h1[:nt, :], func=AF.Copy), reads=[h1], writes=[B2a])
        transpose_to(B2a, T8a, nt, 8, 'dve')
        mm_full(pA, T8a, nt, w_q, 0, D)
        yield
        op('act', lambda e: e.activation(out=B2b[:nt, :], in_=pA[:nt, :], func=AF.Copy), reads=[pA], writes=[B2b])
        transpose_to(B2b, T8b, nt, 8, 'act')
        s_sb = F4c
        for hf in range(2):
            for pl in range(4):
                p_ = hf * 4 + pl
                op('pe', lambda e, p_=p_, pl=pl: e.matmul(pA[:nt, pl * 256:(pl + 1) * 256], lhsT=T8b[:, p_, :nt],
                                                          rhs=skb[:, p_, :], start=True, stop=True),
                   reads=[T8b, skb], writes=[pA])
            if hf == 0:
                op('act', lambda e: e.activation(out=s_sb[:nt, :], in_=pA[:nt, :], func=AF.Copy), reads=[pA], writes=[s_sb])
            else:
                op('dve', lambda e: e.tensor_copy(out=s_sb[:nt, :], in_=pA[:nt, :]), reads=[pA], writes=[s_sb])
            for gl_ in range(8):
                gi_ = hf * 8 + gl_
                sg = slice(gl_ * 128, (gl_ + 1) * 128)
                w_ = sw[gi_ % 2]
                op('dve', lambda e, gi_=gi_, sg=sg: e.max(out=tv[:nt, gi_, 0:8], in_=s_sb[:nt, sg]), reads=[s_sb], writes=[tv])
                op('dve', lambda e, gi_=gi_, sg=sg: e.max_index(out=tiu[:nt, gi_, 0:8], in_max=tv[:nt, gi_, 0:8],
                                                                in_values=s_sb[:nt, sg]), reads=[s_sb, tv], writes=[tiu])
                op('dve', lambda e, gi_=gi_, sg=sg, w_=w_: e.match_replace(out=w_[:nt, 0:128], in_to_replace=tv[:nt, gi_, 0:8],
                                                                           in_values=s_sb[:nt, sg], imm_value=-1e30),
                   reads=[s_sb, tv], writes=[w_])
                op('dve', lambda e, gi_=gi_, w_=w_: e.max(out=tv[:nt, gi_, 8:16], in_=w_[:nt, 0:128]), reads=[w_], writes=[tv])
                op('dve', lambda e, gi_=gi_, w_=w_: e.max_index(out=tiu[:nt, gi_, 8:16], in_max=tv[:nt, gi_, 8:16],
                                                                in_values=w_[:nt, 0:128]), reads=[w_, tv], writes=[tiu])
                if gl_ % 2 == 1:
                    yield
        op('dve', lambda e: e.tensor_copy(out=tif[:nt], in_=tiu[:nt]), reads=[tiu], writes=[tif])
        tv4 = tv[:nt].rearrange("p (a c) k -> p a c k", c=2)
        tif4 = tif[:nt].rearrange("p (a c) k -> p a c k", c=2)
        cv = cand[:nt, :].rearrange("p (a k m) -> p a k m", a=4, k=16)
        for hf in range(2):
            hs = slice(hf * 4, (hf + 1) * 4)
            op('dve', lambda e, hs=hs: e.tensor_tensor(out=cv, in0=tv4[:, hs, 0, :].unsqueeze(3).to_broadcast([nt, 4, 16, 16]),
                                                       in1=tv4[:, hs, 1, :].unsqueeze(2).to_broadcast([nt, 4, 16, 16]), op=ALU.add),
               reads=[tv], writes=[cand])
            for pl in range(4):
                p_ = hf * 4 + pl
                sg = slice(pl * 256, (pl + 1) * 256)
                w_ = sw[p_ % 2]
                op('dve', lambda e, p_=p_, sg=sg: e.max(out=t2v[:nt, p_, 0:8], in_=cand[:nt, sg]), reads=[cand], writes=[t2v])
                op('dve', lambda e, p_=p_, sg=sg: e.max_index(out=t2i[:nt, p_, 0:8], in_max=t2v[:nt, p_, 0:8],
                                                              in_values=cand[:nt, sg]), reads=[cand, t2v], writes=[t2i])
                op('dve', lambda e, p_=p_, sg=sg, w_=w_: e.match_replace(out=w_[:nt, :], in_to_replace=t2v[:nt, p_, 0:8],
                                                                         in_values=cand[:nt, sg], imm_value=-1e30),
                   reads=[cand, t2v], writes=[w_])
                op('dve', lambda e, p_=p_, w_=w_: e.max(out=t2v[:nt, p_, 8:16], in_=w_[:nt, :]), reads=[w_], writes=[t2v])
                op('dve', lambda e, p_=p_, w_=w_: e.max_index(out=t2i[:nt, p_, 8:16], in_max=t2v[:nt, p_, 8:16],
                                                              in_values=w_[:nt, :]), reads=[w_, t2v], writes=[t2i])
                if pl % 2 == 1:
                    yield
        t2if = t2i[:nt].rearrange("p a k -> p (a k)")
        op('dve', lambda e: e.tensor_single_scalar(out=k1u[:nt, :], in_=t2if, scalar=4, op=ALU.logical_shift_right),
           reads=[t2i], writes=[k1u])
        op('dve', lambda e: e.tensor_single_scalar(out=k2u[:nt, :], in_=t2if, scalar=15, op=ALU.bitwise_and),
           reads=[t2i], writes=[k2u])
        op('dve', lambda e: e.tensor_copy(out=k1f[:nt, :], in_=k1u[:nt, :]), reads=[k1u], writes=[k1f])
        op('dve', lambda e: e.tensor_copy(out=k2f[:nt, :], in_=k2u[:nt, :]), reads=[k2u], writes=[k2f])
        ohv = s_sb[:nt, :].rearrange("p (a k m) -> p a k m", a=4, k=16)
        iob = iota16[:nt, :].unsqueeze(1).unsqueeze(1).to_broadcast([nt, 4, 16, 16])
        for (kf, c_, dst_) in ((k1f, 0, i1s), (k2f, 1, i2s)):
            for hf in range(2):
                hs = slice(hf * 4, (hf + 1) * 4)
                js = slice(hf * 64, (hf + 1) * 64)
                op('dve', lambda e, kf=kf, js=js: e.tensor_tensor(
                    out=ohv, in0=iob,
                    in1=kf[:nt, js].rearrange("p (a k) -> p a k", a=4).unsqueeze(3).to_broadcast([nt, 4, 16, 16]),
                    op=ALU.is_equal), reads=[iota16, kf], writes=[s_sb])
                op('dve', lambda e, c_=c_, hs=hs: e.tensor_tensor(
                    out=ohv, in0=ohv, in1=tif4[:, hs, c_, :].unsqueeze(2).to_broadcast([nt, 4, 16, 16]), op=ALU.mult),
                   reads=[s_sb, tif], writes=[s_sb])
                op('dve', lambda e, dst_=dst_, js=js: e.tensor_reduce(out=dst_[:nt, js].rearrange("p (a k) -> p a k", a=4),
                                                                     in_=ohv, axis=AX.X, op=ALU.add),
                   reads=[s_sb], writes=[dst_])
                yield
        op('dve', lambda e: e.scalar_tensor_tensor(out=eif[:nt, :], in0=i1s[:nt, :], scalar=128.0, in1=i2s[:nt, :],
                                                   op0=ALU.mult, op1=ALU.add), reads=[i1s, i2s], writes=[eif])
        op('dve', lambda e: e.tensor_copy(out=eidx[:nt, :], in_=eif[:nt, :]), reads=[eif], writes=[eidx])
        gv = gsm[:nt, :].rearrange("p (a k) -> p a k", a=8)
        op('dve', lambda e: e.tensor_tensor(out=gv, in0=t2v[:nt], in1=t2v[:nt, :, 0:1].to_broadcast([nt, 8, 16]),
                                            op=ALU.subtract), reads=[t2v], writes=[gsm])
        op('act', lambda e: e.activation(out=gsm[:nt, :], in_=gsm[:nt, :], func=AF.Exp), reads=[gsm], writes=[gsm])
        op('dve', lambda e: e.tensor_reduce(out=gsum[:nt, :], in_=gv, axis=AX.X, op=ALU.add), reads=[gsm], writes=[gsum])
        op('dve', lambda e: e.reciprocal(out=gsum[:nt, :], in_=gsum[:nt, :]), reads=[gsum], writes=[gsum])
        op('dve', lambda e: e.tensor_tensor(out=gv, in0=gv, in1=gsum[:nt, :].unsqueeze(2).to_broadcast([nt, 8, 16]),
                                            op=ALU.mult), reads=[gsm, gsum], writes=[gsm])
        yield

    def uvphase(kind, ti, k):
        nt = 128 if kind == 'prompt' else DS
        h1 = H1[k % 2]
        eidx = EI[k % 2]
        gsm = GS[k % 2]
        NG = NJ // GRP
        NS = len(DT)

        def part1(g):
            s_ = g % NS
            dt_, ga_, gb_, gc_ = DT[s_], GA[s_], GB[s_], GC[s_]
            gs = slice(g * GRP, (g + 1) * GRP)
            for jj in range(GRP):
                j = g * GRP + jj
                b_ = uvb[j % NB]
                dma('pool', lambda e, j=j, b_=b_: e.indirect_dma_start(
                    out=b_[:, :], out_offset=None, in_=uvbf_d,
                    in_offset=bass.IndirectOffsetOnAxis(ap=eidx[:, j:j + 1], axis=0)), reads=[eidx, UVBF], writes=[b_])
                op('dve', lambda e, jj=jj, b_=b_, dt_=dt_: e.scalar_tensor_tensor(
                    out=b_[:nt, 0:D], in0=b_[:nt, 0:D], scalar=1.0, in1=h1[:nt, :], op0=ALU.mult, op1=ALU.mult,
                    accum_out=dt_[:nt, jj:jj + 1]), reads=[b_, h1], writes=[b_, dt_])
            op('dve', lambda e: e.scalar_tensor_tensor(out=ga_[:nt, :], in0=dt_[:nt, :], scalar=0.044715,
                                                       in1=dt_[:nt, :], op0=ALU.mult, op1=ALU.mult),
               reads=[dt_], writes=[ga_])
            op('dve', lambda e: e.scalar_tensor_tensor(out=ga_[:nt, :], in0=ga_[:nt, :], scalar=1.0,
                                                       in1=dt_[:nt, :], op0=ALU.add, op1=ALU.mult),
               reads=[ga_, dt_], writes=[ga_])
            op('dve', lambda e: e.scalar_tensor_tensor(out=gc_[:nt, :], in0=dt_[:nt, :], scalar=0.5, in1=gsm[:nt, gs],
                                                       op0=ALU.mult, op1=ALU.mult), reads=[dt_, gsm], writes=[gc_])

        def part1b(g):
            s_ = g % NS
            ga_, gb_ = GA[s_], GB[s_]
            op('act', lambda e: e.activation(out=gb_[:nt, :], in_=ga_[:nt, :], func=AF.Tanh,
                                             scale=0.7978845608028654), reads=[ga_], writes=[gb_])

        def part2(g):
            s_ = g % NS
            gb_, gc_, wt_ = GB[s_], GC[s_], WT[s_]
            op('dve', lambda e: e.scalar_tensor_tensor(out=wt_[:nt, :], in0=gb_[:nt, :], scalar=1.0, in1=gc_[:nt, :],
                                                       op0=ALU.add, op1=ALU.mult), reads=[gb_, gc_], writes=[wt_])
            for jj in range(GRP):
                j = g * GRP + jj
                d_ = dg[j % NDG]
                op('act', lambda e, jj=jj, d_=d_: e.activation(out=d_[:nt, :nt], in_=ident[:nt, :nt], func=AF.Copy,
                                                               scale=wt_[:nt, jj:jj + 1]), reads=[ident, wt_], writes=[d_])

        def part3(g):
            for jj in range(GRP):
                j = g * GRP + jj
                b_ = uvb[j % NB]
                d_ = dg[j % NDG]
                for hb in range(2):
                    op('pe', lambda e, j=j, b_=b_, d_=d_, hb=hb: e.matmul(
                        pV[:nt, hb * 512:(hb + 1) * 512], lhsT=d_[:nt, :nt], rhs=b_[:nt, D + hb * 512:D + (hb + 1) * 512],
                        start=(j == 0), stop=(j == NJ - 1)), reads=[b_, d_], writes=[pV])

        D1 = int(os.environ.get("K_D1", "0"))
        D2 = int(os.environ.get("K_D2", "1"))
        D3 = int(os.environ.get("K_D3", "3"))
        for g in range(NG + D3):
            if g < NG:
                part1(g)
            if 0 <= g - D1 < NG:
                part1b(g - D1)
            if 0 <= g - D2 < NG:
                part2(g - D2)
            if 0 <= g - D3 < NG:
                part3(g - D3)
            yield
        op('dve', lambda e: e.scalar_tensor_tensor(out=h1[:nt, :], in0=h1[:nt, :], scalar=ALPHA, in1=pV[:nt, :],
                                                   op0=ALU.mult, op1=ALU.add), reads=[h1, pV], writes=[h1])
        layernorm(h1, h1, 2, nt)
        if kind == 'prompt':
            dma('sp', lambda e: e.dma_start(out=yp_d[ti * 128:(ti + 1) * 128, :], in_=h1[:nt, :]), reads=[h1], store_src=h1)
        else:
            dma('sp', lambda e: e.dma_start(out=ys_d, in_=h1[:nt, :]), reads=[h1], store_src=h1)
            dma('sp', lambda e: e.dma_start(out=rss_d.rearrange("h d e -> d h e"), in_=state[:].rearrange("p (h e) -> p h e", h=4)),
                reads=[state], store_src=state)
        yield

    def run(g):
        n = 0
        for _ in g:
            n += 1
        return n

    def interleave(gens, ests):
        n = len(gens)
        prog = [0] * n
        alive = [True] * n
        while any(alive):
            best = None
            for i in range(n):
                if alive[i] and (best is None or prog[i] / ests[i] < prog[best] / ests[best]):
                    best = i
            try:
                next(gens[best])
                prog[best] += 1
            except StopIteration:
                alive[best] = False

    n_prompt = int(os.environ.get("K_NPROMPT", NTILES))
    tiles = [('prompt', i) for i in range(n_prompt)] + [('sample', 0)]
    K = len(tiles) - 1
    def first_fronts():
        yield from front('meta', 0, 0)
        n0 = [0]
        for _ in front(tiles[0][0], tiles[0][1], 0):
            n0[0] += 1
            yield
        fest_box.append(n0[0])
        if K >= 1:
            yield from front(tiles[1][0], tiles[1][1], 1)

    fest_box = []
    interleave([conv_gen(), first_fronts()], [len(chunks) + DEPTH, 170])
    fest = fest_box[0]
    nopipe = int(os.environ.get("K_NOPIPE", "0"))
    for k in range(K + 1):
        gens = [uvphase(tiles[k][0], tiles[k][1], k)]
        ests = [NJ // GRP + 3]
        if k + 1 <= K and k >= 1:
            gens.append(front(tiles[k + 1][0], tiles[k + 1][1], k + 1))
            ests.append(fest)
        if nopipe:
            for g_ in gens:
                run(g_)
        else:
            interleave(gens, ests)
    P.finish()
    P.emit()
    return nc


_NC_CACHE = {}


def _in_maps(inp):
    c = _host_consts()
    f = lambda a: np.ascontiguousarray(np.asarray(a, dtype=np.float32))
    rel_bias = f(inp['rel_bias'])
    shared = {
        "xm": f(inp['meta_tokens']),
        "ln0_g": f(inp['ln_in_g']), "ln0_b": f(inp['ln_in_b']),
        "ln1_g": f(inp['ln1_g'][0]), "ln1_b": f(inp['ln1_b'][0]),
        "ln2_g": f(inp['ln2_g'][0]), "ln2_b": f(inp['ln2_b'][0]),
        "w_in": f(inp['w_in'][0]), "w_out": f(inp['w_out'][0]), "wq": f(inp['peer_wq'][0]),
        "sinks": f(inp['attn_sinks'][0]), "gng": f(inp['ret_gn_g'][0]),
        "peer_u": f(inp['peer_u'][0]), "peer_v": f(inp['peer_v'][0]),
        "b_prev": f(rel_bias[c['bk_prev']].transpose(0, 2, 1)),
        "b_cur": f(rel_bias[c['bk_cur']].transpose(0, 2, 1)),
        "b_meta0": f(rel_bias[c['bk_meta0']].transpose(0, 2, 1)),
        "b_metaf": f(np.broadcast_to(rel_bias[15][None, :], (NMETA, 8))),
        "mask_prev": c['mask_prev'], "mask_cur": c['mask_cur'],
        "rot": c['rot'], "dqk": c['dqk'], "gl": c['gl'], "rmask": c['rmask'], "iota16": c['iota16'],
    }
    sk = f(inp['peer_subkeys'][0])
    skb = np.zeros((8, 128, 256), np.float32)
    for cc in range(2):
        skb[:, cc * 64:(cc + 1) * 64, cc * 128:(cc + 1) * 128] = sk[:, cc].transpose(0, 2, 1)
    shared["skb"] = skb
    maps = []
    for b in range(NCORES):
        m = dict(shared)
        m["xp"] = f(inp['x_prompt'][b])
        m["xs"] = f(inp['x_sample'][b])
        m["cmk"] = f(inp['cache_meta_k'][0, b]).reshape(NMETA, 128)
        m["cmv"] = f(inp['cache_meta_v'][0, b]).reshape(NMETA, 128)
        m["csk"] = f(inp['cache_swa_k'][0, b]).reshape(128, 128)
        m["csv"] = f(inp['cache_swa_v'][0, b]).reshape(128, 128)
        m["st0"] = f(inp['state_ret'][0, b])
        maps.append(m)
    return maps


def kernel(**inp):
    dbg = bool(int(os.environ.get("K_DEBUG", "0")))
    key = (dbg,) + tuple(os.environ.get(k_, "") for k_ in ("K_NPROMPT", "K_STAGE", "K_D1", "K_D2", "K_D3", "K_NDG", "K_NB", "K_GRP", "K_NOPIPE"))
    if key not in _NC_CACHE:
        _NC_CACHE[key] = build_program(dbg)
    nc = _NC_CACHE[key]
    maps = _in_maps(inp)
    res = run_bass_kernel_spmd(nc, maps, core_ids=list(range(NCORES)))
    r = res.results
    st = lambda k: np.stack([np.asarray(r[b][k]) for b in range(NCORES)])
    outs = (
        st("y_prompt"),
        st("y_sample"),
        st("meta_k").reshape(1, NCORES, NMETA, 2, 64),
        st("meta_v").reshape(1, NCORES, NMETA, 2, 64),
        st("swa_k").reshape(1, NCORES, 128, 2, 64),
        st("swa_v").reshape(1, NCORES, 128, 2, 64),
        st("ret_state").reshape(1, NCORES, 4, 128, 128),
        st("swa_k_s").reshape(1, NCORES, DS, 2, 64),
        st("swa_v_s").reshape(1, NCORES, DS, 2, 64),
        st("ret_state_s").reshape(1, NCORES, 4, 128, 128),
    )
    if dbg:
        kernel.debug = [{k: np.asarray(v) for k, v in r[b].items() if k.startswith("d_")} for b in range(NCORES)]
    return tuple(np.ascontiguousarray(o.astype(np.float32)) for o in outs)
```
